# Optimizing a Trainium2 kernel written in Bass

```python
import jax, jax.numpy as jnp
from jax import lax
import numpy as np

D_MODEL = 1024
BATCH = 16
SEQ = 2048
DEPTH = 2

GRID_W = 64
CTX_LEN = 256
HEAD_DIM = 64
A_Q_HEADS = 4
A_KV_HEADS = 2
B_Q_HEADS = 4
B_KV_HEADS = 2
WINDOW = 128
Q_BLOCK = 128
LRU_WIDTH = 512
LRU_BLOCKS = 8
CONV_W = 4
CONV_LEFT = CONV_W // 2
LRU_C = 8.0
D_FF = 4 * D_MODEL
ROPE_BASE = 10000.0
EPS = 1e-6
NEG_INF = -1e30

A_Q = A_Q_HEADS * HEAD_DIM
A_KV = A_KV_HEADS * HEAD_DIM
B_Q = B_Q_HEADS * HEAD_DIM
B_KV = B_KV_HEADS * HEAD_DIM
IN_WIDTHS = (A_Q, A_KV, A_KV, B_Q, B_KV, B_KV, LRU_WIDTH, LRU_WIDTH)
IN_WIDTH = sum(IN_WIDTHS)
MIX_WIDTH = A_Q + B_Q + LRU_WIDTH

kernel_name = 'hybrid_headgroup_diffusion_block'


def rms_norm(x, g):
    xf = x.astype(jnp.float32)
    y = xf * lax.rsqrt(jnp.mean(xf * xf, axis=-1, keepdims=True) + EPS)
    return (y * g.astype(jnp.float32)).astype(x.dtype)


def modulate(h, shift, scale):
    return h * (1 + scale) + shift


def axial_rope_tables(n):
    rows = n // GRID_W
    row = jnp.repeat(jnp.arange(rows, dtype=jnp.float32), GRID_W)
    col = jnp.tile(jnp.arange(GRID_W, dtype=jnp.float32), rows)
    half = HEAD_DIM // 2
    inv_freq = ROPE_BASE ** (-jnp.arange(0, half, 2, dtype=jnp.float32) / half)
    ang_r = row[:, None] * inv_freq
    ang_c = col[:, None] * inv_freq
    return jnp.cos(ang_r), jnp.sin(ang_r), jnp.cos(ang_c), jnp.sin(ang_c)


def rope_1d(x, cos, sin):
    h = x.shape[-1] // 2
    x1, x2 = x[..., :h], x[..., h:]
    cos, sin = cos[:, None, :], sin[:, None, :]
    return jnp.concatenate([x1 * cos - x2 * sin, x2 * cos + x1 * sin], axis=-1)


def apply_axial_rope(x, tabs):
    cos_r, sin_r, cos_c, sin_c = tabs
    xf = x.astype(jnp.float32)
    half = HEAD_DIM // 2
    out = jnp.concatenate([rope_1d(xf[..., :half], cos_r, sin_r),
                           rope_1d(xf[..., half:], cos_c, sin_c)], axis=-1)
    return out.astype(x.dtype)


def group_heads(q, n_kv):
    b, n, h, d = q.shape
    return q.reshape(b, n, n_kv, h // n_kv, d)


def sink_softmax(s, sink):
    m = jnp.maximum(jnp.max(s, axis=-1, keepdims=True), sink)
    p = jnp.exp(s - m)
    return p / (jnp.sum(p, axis=-1, keepdims=True) + jnp.exp(sink - m))


def attn_probs(s, sink):
    if sink is None:
        return jax.nn.softmax(s, axis=-1)
    return sink_softmax(s, sink)


def dense_attention(q, k, v, sink=None):
    s = jnp.einsum('bqkgd,bskd->bkgqs', q, k).astype(jnp.float32) * HEAD_DIM ** -0.5
    p = attn_probs(s, sink).astype(v.dtype)
    return jnp.einsum('bkgqs,bskd->bqkgd', p, v)


def global_block_attention(q, k, v):
    b, n, h, d = q.shape
    n_kv = k.shape[2]
    nb = n // Q_BLOCK
    qb = group_heads(q, n_kv).reshape(b, nb, Q_BLOCK, n_kv, h // n_kv, d)
    out = lax.map(lambda blk: dense_attention(blk, k, v), jnp.moveaxis(qb, 1, 0))
    return jnp.moveaxis(out, 0, 1).reshape(b, n, h * d)


def window_sink_attention(q, k, v, k_ctx, v_ctx, sink):
    b, n, h, d = q.shape
    n_kv = k.shape[2]
    nb = n // Q_BLOCK
    band = 3 * Q_BLOCK
    qb = group_heads(q, n_kv).reshape(b, nb, Q_BLOCK, n_kv, h // n_kv, d)

    def banded(t):
        tp = jnp.pad(t, ((0, 0), (WINDOW, WINDOW), (0, 0), (0, 0))).reshape(b, nb + 2, Q_BLOCK, n_kv, d)
        return jnp.concatenate([tp[:, :-2], tp[:, 1:-1], tp[:, 2:]], axis=2)

    kb, vb = banded(k), banded(v)
    start = jnp.arange(nb)[:, None, None] * Q_BLOCK
    qpos = start + jnp.arange(Q_BLOCK)[None, :, None]
    kpos = start - WINDOW + jnp.arange(band)[None, None, :]
    valid = (jnp.abs(qpos - kpos) <= WINDOW) & (kpos >= 0) & (kpos < n)
    scale = HEAD_DIM ** -0.5
    s_band = jnp.einsum('bnqkgd,bnskd->bnkgqs', qb, kb).astype(jnp.float32) * scale
    s_band = jnp.where(valid[None, :, None, None], s_band, NEG_INF)
    s_ctx = jnp.einsum('bnqkgd,bskd->bnkgqs', qb, k_ctx).astype(jnp.float32) * scale
    p = sink_softmax(jnp.concatenate([s_band, s_ctx], axis=-1), sink).astype(v.dtype)
    out = (jnp.einsum('bnkgqs,bnskd->bnqkgd', p[..., :band], vb)
           + jnp.einsum('bnkgqs,bskd->bnqkgd', p[..., band:], v_ctx))
    return out.reshape(b, n, h * d)


def centred_dwconv(x, w, bias):
    n = x.shape[1]
    xp = jnp.pad(x, ((0, 0), (CONV_LEFT, CONV_W - 1 - CONV_LEFT), (0, 0)))
    out = bias
    for j in range(CONV_W):
        out = out + xp[:, j:j + n] * w[j]
    return out


def rglru_coeffs(x, w_a, b_a, w_i, b_i, lam):
    b, n, width = x.shape
    xb = x.reshape(b, n, LRU_BLOCKS, width // LRU_BLOCKS)
    r = jax.nn.sigmoid(jnp.einsum('bnhi,hij->bnhj', xb, w_a).reshape(b, n, width) + b_a)
    i = jax.nn.sigmoid(jnp.einsum('bnhi,hij->bnhj', xb, w_i).reshape(b, n, width) + b_i)
    log_a = -LRU_C * r.astype(jnp.float32) * jax.nn.softplus(-lam.astype(jnp.float32))
    u = jnp.sqrt(-jnp.expm1(2.0 * log_a)) * (i * x).astype(jnp.float32)
    return jnp.exp(log_a), u


def _lin_combine(left, right):
    a1, b1 = left
    a2, b2 = right
    return a1 * a2, a2 * b1 + b2


def linear_scan(a, u, h0):
    a_cum, h_zero = lax.associative_scan(_lin_combine, (a, u), axis=1)
    return a_cum * h0[:, None, :] + h_zero


def directional_rglru(x_lat, x_ctx, w_a, b_a, w_i, b_i, lam):
    a_c, u_c = rglru_coeffs(x_ctx, w_a, b_a, w_i, b_i, lam)
    h_ctx = linear_scan(a_c, u_c, jnp.zeros((x_ctx.shape[0], x_ctx.shape[2]), jnp.float32))
    a_l, u_l = rglru_coeffs(x_lat, w_a, b_a, w_i, b_i, lam)
    h_lat = linear_scan(a_l, u_l, h_ctx[:, -1])
    return h_lat, h_ctx


def split_columns(t):
    parts, start = [], 0
    for w in IN_WIDTHS:
        parts.append(t[..., start:start + w])
        start += w
    return parts


def token_mixers(z, z_ctx, g_q_a, g_k_a, sink_b, conv_w, conv_b, lru_w_a, lru_b_a, lru_w_i, lru_b_i,
                 lru_lambda, rope, with_ctx_out):
    b, n, _ = z.shape
    m = z_ctx.shape[1]
    qa, ka, va, qb, kb, vb, xr, gr = split_columns(z)
    qa_c, ka_c, va_c, qb_c, kb_c, vb_c, xr_c, gr_c = split_columns(z_ctx)

    def heads(t):
        return t.reshape(t.shape[0], t.shape[1], -1, HEAD_DIM)

    qa = apply_axial_rope(rms_norm(heads(qa), g_q_a), rope)
    ka = apply_axial_rope(rms_norm(heads(ka), g_k_a), rope)
    qa_c = rms_norm(heads(qa_c), g_q_a)
    ka_c = rms_norm(heads(ka_c), g_k_a)
    va, va_c = heads(va), heads(va_c)
    out_a = global_block_attention(qa, jnp.concatenate([ka_c, ka], axis=1),
                                   jnp.concatenate([va_c, va], axis=1))

    qb = apply_axial_rope(heads(qb), rope)
    kb = apply_axial_rope(heads(kb), rope)
    vb, kb_c, vb_c = heads(vb), heads(kb_c), heads(vb_c)
    sink = sink_b.astype(jnp.float32).reshape(B_KV_HEADS, B_Q_HEADS // B_KV_HEADS, 1, 1)
    out_b = window_sink_attention(qb, kb, vb, kb_c, vb_c, sink)

    xr = centred_dwconv(xr, conv_w, conv_b)
    xr_c = centred_dwconv(xr_c, conv_w, conv_b)
    h_f, hc_f = directional_rglru(xr, xr_c, lru_w_a[0], lru_b_a[0], lru_w_i[0], lru_b_i[0], lru_lambda[0])
    h_b, hc_b = directional_rglru(jnp.flip(xr, axis=1), jnp.flip(xr_c, axis=1), lru_w_a[1], lru_b_a[1],
                                  lru_w_i[1], lru_b_i[1], lru_lambda[1])
    out_c = (h_f + jnp.flip(h_b, axis=1)).astype(z.dtype) * jax.nn.gelu(gr)

    feat = jnp.concatenate([out_a, out_b, out_c], axis=-1)
    if not with_ctx_out:
        return feat, None
    out_ac = dense_attention(group_heads(qa_c, A_KV_HEADS), ka_c, va_c).reshape(b, m, A_Q)
    out_bc = dense_attention(group_heads(heads(qb_c), B_KV_HEADS), kb_c, vb_c, sink).reshape(b, m, B_Q)
    out_cc = (hc_f + jnp.flip(hc_b, axis=1)).astype(z.dtype) * jax.nn.gelu(gr_c)
    return feat, jnp.concatenate([out_ac, out_bc, out_cc], axis=-1)


def sq_relu_mlp(h, w1, w2):
    return jnp.square(jax.nn.relu(h @ w1)) @ w2


def hybrid_layer(x, ctx, c_act, c_ctx_act, w_mod, b_mod, g_pre_mix, g_post_mix, g_pre_mlp, g_post_mlp,
                 w_in, g_q_a, g_k_a, sink_b, conv_w, conv_b, lru_w_a, lru_b_a, lru_w_i, lru_b_i, lru_lambda,
                 w_out, w_mlp_in, w_mlp_out, rope, update_ctx):
    sh1, sc1, gt1, sh2, sc2, gt2 = jnp.split((c_act @ w_mod + b_mod)[:, None, :], 6, axis=-1)
    csh1, csc1, cgt1, csh2, csc2, cgt2 = jnp.split(c_ctx_act @ w_mod + b_mod, 6, axis=-1)
    h = modulate(rms_norm(x, g_pre_mix), sh1, sc1)
    h_ctx = modulate(rms_norm(ctx, g_pre_mix), csh1, csc1)
    feat, feat_ctx = token_mixers(h @ w_in, h_ctx @ w_in, g_q_a, g_k_a, sink_b, conv_w, conv_b,
                                  lru_w_a, lru_b_a, lru_w_i, lru_b_i, lru_lambda, rope, update_ctx)
    x = x + gt1 * rms_norm(feat @ w_out, g_post_mix)
    h2 = modulate(rms_norm(x, g_pre_mlp), sh2, sc2)
    x = x + gt2 * rms_norm(sq_relu_mlp(h2, w_mlp_in, w_mlp_out), g_post_mlp)
    if update_ctx:
        ctx = ctx + cgt1 * rms_norm(feat_ctx @ w_out, g_post_mix)
        h2c = modulate(rms_norm(ctx, g_pre_mlp), csh2, csc2)
        ctx = ctx + cgt2 * rms_norm(sq_relu_mlp(h2c, w_mlp_in, w_mlp_out), g_post_mlp)
    return x, ctx


def setup_inputs(seed: int = 0) -> dict:
    key = jax.random.key(seed)
    ks = jax.random.split(key, 24)

    def nrm(k, shape, scale):
        return jax.random.normal(k, shape, jnp.float32) * scale

    def gain(k, shape):
        return 1.0 + 0.05 * jax.random.normal(k, shape, jnp.float32)

    hb = LRU_WIDTH // LRU_BLOCKS
    s = jax.random.uniform(ks[19], (DEPTH, 2, LRU_WIDTH), jnp.float32, 0.9, 0.999) ** (1.0 / LRU_C)
    return {
        'x': nrm(ks[0], (BATCH, SEQ, D_MODEL), 1.0),
        'c': nrm(ks[1], (BATCH, D_MODEL), 1.0),
        'ctx': nrm(ks[2], (BATCH, CTX_LEN, D_MODEL), 1.0),
        'c_ctx': nrm(ks[3], (D_MODEL,), 1.0),
        'w_mod': nrm(ks[4], (DEPTH, D_MODEL, 6 * D_MODEL), 0.5 * D_MODEL ** -0.5),
        'b_mod': nrm(ks[5], (DEPTH, 6 * D_MODEL), 0.01),
        'g_pre_mix': gain(ks[6], (DEPTH, D_MODEL)),
        'g_post_mix': gain(ks[7], (DEPTH, D_MODEL)),
        'g_pre_mlp': gain(ks[8], (DEPTH, D_MODEL)),
        'g_post_mlp': gain(ks[9], (DEPTH, D_MODEL)),
        'w_in': nrm(ks[10], (DEPTH, D_MODEL, IN_WIDTH), D_MODEL ** -0.5),
        'g_q_a': gain(ks[11], (DEPTH, HEAD_DIM)),
        'g_k_a': gain(ks[12], (DEPTH, HEAD_DIM)),
        'sink_b': nrm(ks[13], (DEPTH, B_Q_HEADS), 0.5),
        'conv_w': nrm(ks[14], (DEPTH, CONV_W, LRU_WIDTH), CONV_W ** -0.5),
        'conv_b': nrm(ks[15], (DEPTH, LRU_WIDTH), 0.01),
        'lru_w_a': nrm(ks[16], (DEPTH, 2, LRU_BLOCKS, hb, hb), hb ** -0.5),
        'lru_b_a': nrm(ks[17], (DEPTH, 2, LRU_WIDTH), 0.01),
        'lru_w_i': nrm(ks[18], (DEPTH, 2, LRU_BLOCKS, hb, hb), hb ** -0.5),
        'lru_b_i': nrm(ks[20], (DEPTH, 2, LRU_WIDTH), 0.01),
        'lru_lambda': jnp.log(s) - jnp.log1p(-s),
        'w_out': nrm(ks[21], (DEPTH, MIX_WIDTH, D_MODEL), MIX_WIDTH ** -0.5),
        'w_mlp_in': nrm(ks[22], (DEPTH, D_MODEL, D_FF), D_MODEL ** -0.5),
        'w_mlp_out': nrm(ks[23], (DEPTH, D_FF, D_MODEL), D_FF ** -0.5),
    }


def reference(x, c, ctx, c_ctx, w_mod, b_mod, g_pre_mix, g_post_mix, g_pre_mlp, g_post_mlp, w_in,
              g_q_a, g_k_a, sink_b, conv_w, conv_b, lru_w_a, lru_b_a, lru_w_i, lru_b_i, lru_lambda,
              w_out, w_mlp_in, w_mlp_out):
    rope = axial_rope_tables(x.shape[1])
    c_act = jax.nn.silu(c)
    c_ctx_act = jax.nn.silu(c_ctx)
    for l in range(DEPTH):
        x, ctx = hybrid_layer(x, ctx, c_act, c_ctx_act, w_mod[l], b_mod[l], g_pre_mix[l], g_post_mix[l],
                              g_pre_mlp[l], g_post_mlp[l], w_in[l], g_q_a[l], g_k_a[l], sink_b[l],
                              conv_w[l], conv_b[l], lru_w_a[l], lru_b_a[l], lru_w_i[l], lru_b_i[l],
                              lru_lambda[l], w_out[l], w_mlp_in[l], w_mlp_out[l], rope,
                              l < DEPTH - 1)
    return x
```

```python
import numpy as np
from contextlib import ExitStack
import concourse.bass as bass
import concourse.mybir as mybir
from concourse.bass_utils import run_bass_kernel_spmd

F32 = mybir.dt.float32
BF16 = mybir.dt.bfloat16
AF = mybir.ActivationFunctionType
ALU = mybir.AluOpType
AX = mybir.AxisListType

D = 1024
NB = 2
SEQ = 2048
CTX = 256
NT = (SEQ + CTX) // 128
NTOK = SEQ + CTX
DEPTH = 2
EPS = 1e-6
DFF = 4096
XO_C = 2
XO_L = 262
XW = 2312


class Buf:
    __slots__ = ("name", "w", "r")

    def __init__(self, name):
        self.name = name
        self.w = None
        self.r = []


class Sched:
    KD = 8

    def __init__(self, nc, es):
        self.nc = nc
        self.eng = {"pe": nc.tensor, "act": nc.scalar, "dve": nc.vector, "pool": nc.gpsimd, "sp": nc.sync}
        self.semh = {}
        self.cnt = {}
        self.pending = {}
        self.waited = {}
        for e in self.eng:
            self.semh[e] = es.enter_context(nc.semaphore("s_" + e))
            self.cnt[e] = 0
            self.pending[e] = []
            self.waited[e] = {}
        self.dq = {}
        for q in ("sp", "pool"):
            sems = []
            for i in range(self.KD):
                k = "d_%s%d" % (q, i)
                self.semh[k] = es.enter_context(nc.semaphore(k))
                sems.append(k)
            self.dq[q] = {"sems": sems, "cnt": [0] * self.KD, "next": 0}
        self.n_instr = 0

    def _wait(self, e, tok):
        key, val = tok
        if self.waited[e].get(key, 0) >= val:
            return
        self.eng[e].wait_ge(self.semh[key], val)
        self.waited[e][key] = val

    def _deps(self, e, r, w, is_dma):
        deps = set()
        for b in r:
            if b.w is not None:
                deps.add(b.w)
        for b in w:
            if b.w is not None:
                deps.add(b.w)
            for t in b.r:
                deps.add(t)
        out = []
        for t in deps:
            if t == "PENDING":
                raise RuntimeError("dependency on unsignaled op")
            out.append(t)
        return out

    def op(self, e, fn, r=(), w=(), signal=True):
        r = list(r)
        w = list(w)
        deps = set()
        for b in r:
            if b.w is not None:
                deps.add(b.w)
        for b in w:
            if b.w is not None and b.w[0] != e:
                deps.add(b.w)
            for t in b.r:
                if t[0] != e:
                    deps.add(t)
        for t in deps:
            if t[1] is None:
                raise RuntimeError("dependency on unsignaled op: %s" % (t,))
            self._wait(e, t)
        ins = fn(self.eng[e])
        self.n_instr += 1
        self.pending[e].append((r, w))
        if signal:
            self.cnt[e] += 1
            ins.then_inc(self.semh[e], 1)
            tok = (e, self.cnt[e])
            for (rr, ww) in self.pending[e]:
                for b in rr:
                    b.r.append(tok)
                for b in ww:
                    b.w = tok
                    b.r = []
            self.pending[e] = []
            return tok
        else:
            for b in w:
                b.w = (e, None)
                b.r = []
            return None

    def dma(self, q, out, in_, r=(), w=(), **kw):
        r = list(r)
        w = list(w)
        deps = set()
        for b in r:
            if b.w is not None:
                deps.add(b.w)
        for b in w:
            if b.w is not None:
                deps.add(b.w)
            for t in b.r:
                deps.add(t)
        for t in deps:
            if t[1] is None:
                raise RuntimeError("dma dependency on unsignaled op")
            self._wait(q, t)
        st = self.dq[q]
        i = st["next"]
        st["next"] = (i + 1) % self.KD
        key = st["sems"][i]
        if st["cnt"][i] > 0:
            self._wait(q, (key, st["cnt"][i]))
        self.eng[q].dma_start(out=out, in_=in_, **kw).then_inc(self.semh[key], 16)
        self.n_instr += 1
        st["cnt"][i] += 16
        tok = (key, st["cnt"][i])
        for b in r:
            b.r.append(tok)
        for b in w:
            b.w = tok
            b.r = []
        return tok

    def barrier_bufs(self, bufs_old, bufs_new):
        toks = set()
        for b in bufs_old:
            if b.w is not None:
                toks.add(b.w)
            for t in b.r:
                toks.add(t)
        for b in bufs_new:
            b.r = list(toks)


def _rope_tables():
    n = SEQ
    rows = n // 64
    row = np.repeat(np.arange(rows, dtype=np.float32), 64)
    col = np.tile(np.arange(64, dtype=np.float32), rows)
    half = 32
    inv_freq = (np.float32(10000.0) ** (-np.arange(0, half, 2, dtype=np.float32) / np.float32(half))).astype(np.float32)
    ang_r = (row[:, None] * inv_freq).astype(np.float32)
    ang_c = (col[:, None] * inv_freq).astype(np.float32)
    cr, sr, cc, sc = np.cos(ang_r), np.sin(ang_r), np.cos(ang_c), np.sin(ang_c)
    cos12 = np.concatenate([cr, cr, cc, cc], axis=1).astype(np.float32)
    sin12 = np.concatenate([-sr, sr, -sc, sc], axis=1).astype(np.float32)
    return np.ascontiguousarray(cos12), np.ascontiguousarray(sin12)


def _const_inputs():
    cos12, sin12 = _rope_tables()
    kk = np.arange(128)[:, None]
    qq = np.arange(128)[None, :]
    return {
        "k_ident": np.eye(128, dtype=np.float32),
        "k_mlo": (kk >= qq).astype(np.float32),
        "k_mhi": (kk <= qq).astype(np.float32),
        "k_cos": cos12,
        "k_sin": sin12,
    }


W_SHAPES = {
    "w_mod": [DEPTH, D, 6 * D], "b_mod": [DEPTH, 6 * D],
    "g_pre_mix": [DEPTH, D], "g_post_mix": [DEPTH, D], "g_pre_mlp": [DEPTH, D], "g_post_mlp": [DEPTH, D],
    "w_in": [DEPTH, D, 2048], "g_q_a": [DEPTH, 64], "g_k_a": [DEPTH, 64], "sink_b": [DEPTH, 4],
    "conv_w": [DEPTH, 4, 512], "conv_b": [DEPTH, 512],
    "lru_w_a": [DEPTH, 2, 8, 64, 64], "lru_b_a": [DEPTH, 2, 512],
    "lru_w_i": [DEPTH, 2, 8, 64, 64], "lru_b_i": [DEPTH, 2, 512], "lru_lambda": [DEPTH, 2, 512],
    "w_out": [DEPTH, D, D], "w_mlp_in": [DEPTH, D, DFF], "w_mlp_out": [DEPTH, DFF, D],
}


class _Stop(Exception):
    pass


def build_program(layers=(0, 1), final=True, dbg=None, stop=None):
    nc = bass.Bass("TRN2", target_bir_lowering=False)
    es = ExitStack()
    dram = {}

    def din(name, shape):
        dram[name] = nc.dram_tensor(name, list(shape), F32, kind="ExternalInput").ap()
        return dram[name]

    x_in = din("x", [NB, SEQ, D])
    ctx_in = din("ctx", [NB, CTX, D])
    c_in = din("c", [NB, D])
    cctx_in = din("c_ctx", [D])
    W = {k: din(k, s) for k, s in W_SHAPES.items()}
    k_ident = din("k_ident", [128, 128])
    k_mlo = din("k_mlo", [128, 128])
    k_mhi = din("k_mhi", [128, 128])
    k_cos = din("k_cos", [SEQ, 64])
    k_sin = din("k_sin", [SEQ, 64])
    out_d = nc.dram_tensor("out", [NB, SEQ, D], F32, kind="ExternalOutput").ap()
    ikind = "ExternalOutput" if dbg else "Internal"
    xmid = nc.dram_tensor("xmid", [NB, NTOK, D], F32, kind=ikind).ap()
    xs = nc.dram_tensor("xs", [NB, NTOK, D], F32, kind=ikind).ap()
    h2s = nc.dram_tensor("h2s", [NB, NT, 128, D], BF16, kind="Internal").ap()
    modrows = nc.dram_tensor("modrows", [DEPTH, 3, 6, D], F32, kind=ikind).ap()
    dbg_out = {}
    if dbg:
        for name, shape in dbg.items():
            if name.startswith("_"):
                continue
            dbg_out[name] = nc.dram_tensor("dbg_" + name, list(shape), F32, kind="ExternalOutput").ap()

    S = Sched(nc, es)

    ckn = [0]

    def ck(tag):
        if stop is not None and stop.startswith("ck:"):
            ckn[0] += 1
            if ckn[0] == int(stop[3:]):
                print("STOP at checkpoint", ckn[0], tag)
                raise _Stop()

    uid = [0]

    SB_TOT = 207 * 1024
    Mbig = es.enter_context(nc.sbuf_tensor("Mbig", [128, SB_TOT], mybir.dt.uint8))
    DTSZ = {F32: 4, BF16: 2}

    class Arena:
        def __init__(self, ranges_kb):
            self.ranges = [(int(a * 1024), int(b_ * 1024)) for a, b_ in ranges_kb]
            self.i = 0
            self.p = self.ranges[0][0]

        def alloc(self, nbytes):
            nbytes = (nbytes + 31) // 32 * 32
            while True:
                lo, hi = self.ranges[self.i]
                if self.p + nbytes <= hi:
                    off = self.p
                    self.p += nbytes
                    return off
                self.i += 1
                if self.i >= len(self.ranges):
                    raise RuntimeError("arena full")
                self.p = self.ranges[self.i][0]

    arena_cur = [None]

    def sb(name, shape, dt, side=None):
        shape = list(shape)
        n = 1
        for d_ in shape[1:]:
            n *= d_
        nbytes = n * DTSZ[dt]
        off = arena_cur[0].alloc(nbytes)
        v = Mbig[0:shape[0], off:off + nbytes].bitcast(dt)
        if len(shape) > 2:
            names = ["d%d" % i for i in range(len(shape) - 1)]
            pat = "p (%s) -> p %s" % (" ".join(names), " ".join(names))
            v = v.rearrange(pat, **{nm: sz for nm, sz in zip(names[:-1], shape[1:-1])})
        return v

    A_CONST = [(0, 1)]
    A_MOD = [(1, 65)]
    A_T0B = [(1, 56)]
    A_T0A = [(65, 111)]
    A_WIN = [(138, 170)]
    A_AW = [(56, 65), (111, 138), (170, 207)]
    A_FEAT = [(170, 207)]
    A_LW = [(56, 65), (111, 170)]
    A_WOUT = [(111, 127)]
    A_TW = [(127, 138)]
    A_W1 = [(1, 65)]
    A_W2 = [(65, 129)]
    A_CAW = {0: [(1, 65)], 1: [(97, 111), (127, 170)]}
    A_MW = [(129, 207)]
    A_DBG = [(1, 56)]

    class Wt:
        pass
    WTS = Wt()
    WTS.w_in = None
    WTS.w_out = None
    WTS.w1 = None
    WTS.w2 = None

    def load_w_in(l):
        arena_cur[0] = Arena(A_WIN)
        T_ = Tracker()
        w = sb("w_in", [128, 8, 2048], BF16)
        B_ = T_.new("w_in")
        wl = W["w_in"][l]
        for k in range(8):
            for hh in range(2):
                S.dma("pool", w[:, k, hh * 1024:(hh + 1) * 1024], wl[k * 128:(k + 1) * 128, hh * 1024:(hh + 1) * 1024], w=[B_])
        WTS.w_in = (w, B_)

    def load_w_out(l):
        arena_cur[0] = Arena(A_WOUT)
        T_ = Tracker()
        w = sb("w_out", [128, 8, D], BF16)
        B_ = T_.new("w_out")
        for k in range(8):
            S.dma("pool", w[:, k, :], W["w_out"][l, k * 128:(k + 1) * 128, :], w=[B_])
        WTS.w_out = (w, B_)

    def load_w1(l):
        arena_cur[0] = Arena(A_W1)
        T_ = Tracker()
        w = sb("w1", [128, 8, DFF], BF16)
        Bs = [T_.new("w1_%d" % k) for k in range(8)]
        for k in range(8):
            for cq in range(4):
                S.dma("pool", w[:, k, cq * 1024:(cq + 1) * 1024],
                      W["w_mlp_in"][l, k * 128:(k + 1) * 128, cq * 1024:(cq + 1) * 1024], w=[Bs[k]])
        WTS.w1 = (w, Bs)

    def load_w2(l, half):
        if half == 0:
            arena_cur[0] = Arena(A_W2)
            w = sb("w2", [128, 32, D], BF16)
            WTS.w2 = (w, [None] * 8)
        w, Bs = WTS.w2
        T_ = Tracker()
        for k in range(4 * half, 4 * half + 4):
            Bs[k] = T_.new("w2_%d" % k)
            for f4 in range(4):
                f = 4 * k + f4
                S.dma("pool", w[:, f, :], W["w_mlp_out"][l, f * 128:(f + 1) * 128, :], w=[Bs[k]])

    def ps(name, shape, dt=F32):
        uid[0] += 1
        return es_cur[0].enter_context(nc.psum_tensor("%s_p%d" % (name, uid[0]), list(shape), dt))

    es_cur = [es]

    B_xmid = [[Buf("xmid%d_%d" % (b, j)) for j in range(NT)] for b in range(NB)]
    B_xs = [[Buf("xs%d_%d" % (b, j)) for j in range(NT)] for b in range(NB)]
    B_h2s = [[Buf("h2s%d_%d" % (bb, j)) for j in range(NT)] for bb in range(NB)]
    B_mod = [Buf("mod%d" % l) for l in range(DEPTH)]
    out_toks = []

    arena_cur[0] = Arena(A_CONST)
    ident = sb("ident", [128, 128], BF16)
    mlo = sb("mlo", [128, 128], BF16)
    mhi = sb("mhi", [128, 128], BF16)
    nhalf = sb("nhalf", [128, 8], F32)
    B_const = Buf("const")
    S.dma("pool", ident[:], k_ident[:, :], w=[B_const])
    S.dma("pool", mlo[:], k_mlo[:, :], w=[B_const])
    S.dma("pool", mhi[:], k_mhi[:, :], w=[B_const])
    S.op("dve", lambda e: e.memset(nhalf[:], -0.5), w=[B_const])
    epsc = sb("epsc", [128, 8], F32)
    S.op("dve", lambda e: e.memset(epsc[:], float(EPS)), w=[B_const])

    def rstd_from_ss(ss_ap, ss_buf, n, inv_n, out_ap, out_buf, tmp_ap, tmp_buf):
        S.op("dve", lambda e: e.tensor_scalar(out=tmp_ap, in0=ss_ap, scalar1=float(inv_n), scalar2=float(EPS),
                                               op0=ALU.mult, op1=ALU.add), r=[ss_buf], w=[tmp_buf])
        S.op("pool", lambda e: e.tensor_tensor(out=out_ap, in0=tmp_ap, in1=nhalf[:, 0:n], op=ALU.pow),
             r=[tmp_buf, B_const], w=[out_buf])

    def modulation(l, preload=None):
        with ExitStack() as les:
            es_cur[0] = les
            TMod = Tracker()
            if preload is not None:
                preload()
            arena_cur[0] = Arena(A_MOD)
            cT = sb("cT", [128, 8, 4], F32)
            cTb = sb("cTb", [128, 8, 4], BF16)
            bmod = sb("bmod", [3, 6 * D], F32)
            g4 = sb("g4", [3, 4, D], F32)
            wm = [sb("wm%d" % i, [128, 8, 512], BF16) for i in range(2)]
            rows = [sb("mrow%d" % i, [3, 512], F32) for i in range(2)]
            pm = [ps("pm%d" % i, [128, 512]) for i in range(2)]
            B_cT, B_cTb, B_bmod, B_g4 = (TMod.new(n_) for n_ in ("cT", "cTb", "bmod", "g4"))
            B_wm = [TMod.new("wm0"), TMod.new("wm1")]
            B_rows = [TMod.new("r0"), TMod.new("r1")]
            B_pm = [TMod.new("pm0"), TMod.new("pm1")]
            S.op("dve", lambda e: e.memset(cT[:], 0.0), w=[B_cT])
            for b in range(NB):
                S.dma("sp", cT[:, :, b], c_in[b, :].rearrange("(k p) -> p k", p=128), w=[B_cT],
                      allow_slow_non_contiguous=True)
            S.dma("sp", cT[:, :, 2], cctx_in.rearrange("(k p) -> p k", p=128), w=[B_cT],
                  allow_slow_non_contiguous=True)
            S.op("act", lambda e: e.activation(out=cTb[:], in_=cT[:], func=AF.Silu), r=[B_cT], w=[B_cTb])
            S.dma("sp", bmod[:], W["b_mod"][l, :].partition_broadcast(3), w=[B_bmod])
            for i, gname in enumerate(("g_pre_mix", "g_post_mix", "g_pre_mlp", "g_post_mlp")):
                S.dma("sp", g4[:, i, :], W[gname][l, :].partition_broadcast(3), w=[B_g4])
            for j in range(12):
                i = j % 2
                S.dma("pool", wm[i][:], W["w_mod"][l, :, j * 512:(j + 1) * 512].rearrange("(k p) n -> p k n", p=128),
                      w=[B_wm[i]])
                for k in range(8):
                    S.op("pe", lambda e, k=k: e.matmul(pm[i][0:3, :], lhsT=cTb[:, k, 0:3], rhs=wm[i][:, k, :],
                                                       start=(k == 0), stop=(k == 7)),
                         r=[B_cTb, B_wm[i]], w=[B_pm[i]], signal=(k == 7))
                seg = j // 2
                cs = slice((j % 2) * 512, (j % 2) * 512 + 512)
                S.op("dve", lambda e: e.tensor_tensor(out=rows[i][:], in0=pm[i][0:3, :], in1=bmod[:, j * 512:(j + 1) * 512],
                                                      op=ALU.add), r=[B_pm[i], B_bmod], w=[B_rows[i]])
                if seg in (1, 4):
                    gi = 0 if seg == 1 else 2
                    S.op("dve", lambda e: e.scalar_tensor_tensor(out=rows[i][:], in0=rows[i][:], scalar=1.0,
                                                                 in1=g4[:, gi, cs], op0=ALU.add, op1=ALU.mult),
                         r=[B_rows[i], B_g4], w=[B_rows[i]])
                elif seg in (2, 5):
                    gi = 1 if seg == 2 else 3
                    S.op("dve", lambda e: e.tensor_tensor(out=rows[i][:], in0=rows[i][:], in1=g4[:, gi, cs], op=ALU.mult),
                         r=[B_rows[i], B_g4], w=[B_rows[i]])
                S.dma("sp", modrows[l, :, seg, cs], rows[i][:], r=[B_rows[i]], w=[B_mod[l]])
            es_cur[0] = es
        if stop == "mod":
            raise _Stop()
        return [B_cT, B_cTb, B_bmod, B_g4] + B_wm + B_rows + B_pm

    def src_tile(l, b, j):
        if l == 0:
            if j < 2:
                return ctx_in[b, j * 128:(j + 1) * 128, :], None
            return x_in[b, (j - 2) * 128:(j - 1) * 128, :], None
        return xs[b, j * 128:(j + 1) * 128, :], B_xs[b][j]

    prev_bufs = [[]]

    def phase_scope():
        return ExitStack()

    class Tracker:
        def __init__(self):
            self.bufs = []
            fr = [(e, S.cnt[e]) for e in S.cnt if S.cnt[e] > 0]
            for q, st in S.dq.items():
                for i, key in enumerate(st["sems"]):
                    if st["cnt"][i] > 0:
                        fr.append((key, st["cnt"][i]))
            self.frontier = fr

        def new(self, name):
            b = Buf(name)
            b.r = list(self.frontier)
            self.bufs.append(b)
            return b

    def layer_batch(l, b, last_layer):
        do_ctx = not last_layer
        with ExitStack() as pes:
            es_cur[0] = pes
            T0 = Tracker()
            arena_cur[0] = Arena(A_T0A)
            QA = sb("QA", [128, NT, 3, 128], BF16)
            QB = sb("QB", [128, NT, 3, 128], BF16)
            VA = sb("VA", [128, NT, 2, 128], BF16)
            VB = sb("VB", [128, NT, 2, 128], BF16)
            arena_cur[0] = Arena(A_T0B)
            xr = sb("xr", [128, 4, XW], F32)
            gg = sb("gg", [128, 4, NTOK], BF16)
            B_QA = [T0.new("QA%d" % j) for j in range(NT)]
            B_QB = [T0.new("QB%d" % j) for j in range(NT)]
            B_VA = [T0.new("VA%d" % j) for j in range(NT)]
            B_VB = [T0.new("VB%d" % j) for j in range(NT)]
            B_xr = [T0.new("xr%d" % m) for m in range(4)]
            B_gg = [T0.new("gg%d" % m) for m in range(4)]
            for j in range(NT):
                S.op("pool", lambda e, j=j: e.memset(VA[:, j, :, 64:128], 1.0), w=[B_VA[j]])
                S.op("pool", lambda e, j=j: e.memset(VB[:, j, :, 64:128], 1.0), w=[B_VB[j]])
            for m in range(4):
                S.op("pool", lambda e, m=m: e.memset(xr[:, m, :], 0.0), w=[B_xr[m]])

            with ExitStack() as aes:
                es_cur[0] = aes
                TA = Tracker()
                w_in, B_win = WTS.w_in
                arena_cur[0] = Arena(A_AW)
                G1 = sb("G1", [128, D], F32)
                S1 = sb("S1", [128, D], F32)
                B_G1 = TA.new("G1")
                gqk = sb("gqk", [128, 6, 64], F32)
                B_gqk = TA.new("gqk")
                S.dma("sp", gqk[:, 0, :], W["g_q_a"][l, :].partition_broadcast(128), w=[B_gqk])
                S.dma("sp", gqk[:, 4, :], W["g_k_a"][l, :].partition_broadcast(128), w=[B_gqk])
                for hh in (1, 2, 3):
                    S.op("dve", lambda e, hh=hh: e.tensor_copy(out=gqk[:, hh, :], in_=gqk[:, 0, :]), r=[B_gqk], w=[B_gqk])
                S.op("dve", lambda e: e.tensor_copy(out=gqk[:, 5, :], in_=gqk[:, 4, :]), r=[B_gqk], w=[B_gqk])
                xt = [sb("xt%d" % i, [128, D], F32) for i in range(2)]
                B_xt = [TA.new("xt%d" % i) for i in range(2)]
                junk = sb("junk", [128, D], BF16)
                B_junk = TA.new("junk")
                st = [sb("st%d" % i, [128, 8], F32) for i in range(2)]
                B_st = [TA.new("st%d" % i) for i in range(2)]
                tmp = sb("tmpA", [128, D], F32)
                B_tmp = TA.new("tmpA")
                hb = [sb("hb%d" % i, [128, D], BF16) for i in range(2)]
                B_hb = [TA.new("hb%d" % i) for i in range(2)]
                hT = [sb("hT%d" % i, [128, 8, 512], BF16) for i in range(2)]
                B_hT = [TA.new("hT%d" % i) for i in range(2)]
                sq = sb("sq", [128, 384], F32)
                B_sq = TA.new("sq")
                st6 = sb("st6", [128, 24], F32)
                B_st6 = TA.new("st6")
                qk12_2 = [sb("qk12_%d" % i, [128, 768], F32) for i in range(2)]
                B_qk12_2 = [TA.new("qk12_%d" % i) for i in range(2)]
                rt1 = sb("rt1", [128, 768], F32)
                rt2 = sb("rt2", [128, 768], F32)
                B_rt1, B_rt2 = TA.new("rt1"), TA.new("rt2")
                stg = [sb("stg%d" % i, [128, 768], BF16) for i in range(2)]
                B_stg = [TA.new("stg%d" % i) for i in range(2)]
                cs_t = [sb("cs%d" % i, [128, 2, 64], F32) for i in range(4)]
                B_cs = [TA.new("cs%d" % i) for i in range(4)]
                tp = ps("tpA", [128, 8, 128], BF16)
                B_tp = TA.new("tpA")
                zt2 = [ps("zt%d" % i, [128, 1024]) for i in range(2)]
                B_zt2 = [TA.new("zt%d" % i) for i in range(2)]
                zf = [ps("zf%d" % i, [128, 512]) for i in range(2)]
                B_zf = [TA.new("zf%d" % i) for i in range(2)]
                tp2 = ps("tp2", [128, 6, 128], BF16)
                B_tp2 = TA.new("tp2")

                ck("A:setup")
                groups = [(0, 2)] + [(2 + 4 * g, 4) for g in range(4)]
                tiles_a = [(gi, jj) for gi, (j0, nj) in enumerate(groups) for jj in range(nj)]

                def a_s1(gi, jj, part):
                    j0, nj = groups[gi]
                    is_ctx = (gi == 0)
                    j = j0 + jj
                    i2 = j % 2
                    if part == "act":
                        if jj == 0 and gi in (0, 1):
                            who = 2 if is_ctx else b
                            S.dma("sp", G1[:], modrows[l, who, 1, :].partition_broadcast(128), r=[B_mod[l]], w=[B_G1])
                            S.dma("sp", S1[:], modrows[l, who, 0, :].partition_broadcast(128), r=[B_mod[l]], w=[B_G1])
                        src, sbuf_tok = src_tile(l, b, j)
                        S.dma("sp", xt[i2][:], src, r=([sbuf_tok] if sbuf_tok else []), w=[B_xt[i2]])
                        if not is_ctx:
                            S.dma("sp", cs_t[j % 4][:, 0, :], k_cos[(j - 2) * 128:(j - 1) * 128, :], w=[B_cs[j % 4]])
                            S.dma("sp", cs_t[j % 4][:, 1, :], k_sin[(j - 2) * 128:(j - 1) * 128, :], w=[B_cs[j % 4]])
                        S.op("act", lambda e: e.activation(out=junk[:], in_=xt[i2][:], func=AF.Square,
                                                           accum_out=st[i2][:, 0:1]),
                             r=[B_xt[i2]], w=[B_junk, B_st[i2]])
                        S.op("act", lambda e: e.activation(out=st[i2][:, 1:2], in_=st[i2][:, 0:1], func=AF.Ln,
                                                           scale=1.0 / D, bias=epsc[:, 0:1]), r=[B_st[i2], B_const], w=[B_st[i2]])
                        S.op("act", lambda e: e.activation(out=st[i2][:, 2:3], in_=st[i2][:, 1:2], func=AF.Exp, scale=-0.5),
                             r=[B_st[i2]], w=[B_st[i2]])
                    if part == "dve":
                        S.op("dve", lambda e: e.scalar_tensor_tensor(out=tmp[:], in0=xt[i2][:], scalar=st[i2][:, 2:3],
                                                                     in1=G1[:], op0=ALU.mult, op1=ALU.mult),
                             r=[B_xt[i2], B_st[i2], B_G1], w=[B_tmp])
                        S.op("dve", lambda e: e.tensor_tensor(out=hb[i2][:], in0=tmp[:], in1=S1[:], op=ALU.add),
                             r=[B_tmp, B_G1], w=[B_hb[i2]])

                def a_s2(gi, jj, part):
                    j0, nj = groups[gi]
                    hTg = hT[gi % 2]
                    B_hTg = B_hT[gi % 2]
                    j = j0 + jj
                    i2 = j % 2
                    zt = zt2[i2]
                    B_zt = B_zt2[i2]
                    if part == "T":
                        for k in range(8):
                            S.op("pe", lambda e, k=k: e.transpose(out=tp[:, k, :], in_=hb[i2][:, k * 128:(k + 1) * 128],
                                                                  identity=ident[:]),
                                 r=[B_hb[i2], B_const], w=[B_tp], signal=(k == 7))
                    if part == "copy":
                        S.op("act", lambda e: e.activation(out=hTg[:, :, jj * 128:(jj + 1) * 128], in_=tp[:], func=AF.Copy),
                             r=[B_tp], w=[B_hTg])
                    if part == "mm":
                        for k in range(8):
                            for n in range(2):
                                S.op("pe", lambda e, k=k, n=n: e.matmul(zt[:, n * 512:(n + 1) * 512],
                                                                        lhsT=hTg[:, k, jj * 128:(jj + 1) * 128],
                                                                        rhs=w_in[:, k, n * 512:(n + 1) * 512],
                                                                        start=(k == 0), stop=(k == 7)),
                                     r=[B_hTg, B_win], w=[B_zt], signal=(k == 7 and n == 1))
                        if jj == nj - 1:
                            a_zf(gi)

                def a_s3(gi, jj, part):
                    j0, nj = groups[gi]
                    j = j0 + jj
                    i2 = j % 2
                    zt = zt2[i2]
                    B_zt = B_zt2[i2]
                    qk12 = qk12_2[i2]
                    B_qk12 = B_qk12_2[i2]
                    if part == "a":
                        S.op("act", lambda e: e.activation(out=sq[:], in_=zt[:, 0:384], func=AF.Square),
                             r=[B_zt], w=[B_sq])
                        S.op("act", lambda e: e.activation(out=qk12[:, 384:640].rearrange("p (b a d) -> p a b d", b=2, a=2, d=64),
                                                           in_=zt[:, 512:768].rearrange("p (a b d) -> p a b d", a=2, b=2, d=64),
                                                           func=AF.Copy), r=[B_zt], w=[B_qk12])
                        S.op("act", lambda e: e.activation(out=qk12[:, 640:768], in_=zt[:, 768:896], func=AF.Copy),
                             r=[B_zt], w=[B_qk12])
                        S.op("act", lambda e: e.activation(out=VA[:, j, :, 0:64],
                                                           in_=zt[:, 384:512].rearrange("p (a d) -> p a d", d=64), func=AF.Copy),
                             r=[B_zt], w=[B_VA[j]])
                        S.op("act", lambda e: e.activation(out=VB[:, j, :, 0:64],
                                                           in_=zt[:, 896:1024].rearrange("p (a d) -> p a d", d=64), func=AF.Copy),
                             r=[B_zt], w=[B_VB[j]])
                    if part == "red":
                        S.op("dve", lambda e: e.tensor_reduce(out=st6[:, 0:6], in_=sq[:].rearrange("p (h d) -> p h d", d=64),
                                                              axis=AX.X, op=ALU.add), r=[B_sq], w=[B_st6])
                    if part == "b":
                        S.op("act", lambda e: e.activation(out=st6[:, 8:14], in_=st6[:, 0:6], func=AF.Ln,
                                                           scale=1.0 / 64, bias=epsc[:, 0:1]), r=[B_st6, B_const], w=[B_st6])
                        S.op("act", lambda e: e.activation(out=st6[:, 16:22], in_=st6[:, 8:14], func=AF.Exp, scale=-0.5),
                             r=[B_st6], w=[B_st6])
                    if part == "qn":
                        S.op("dve", lambda e: e.tensor_tensor(out=qk12[:, 0:256].rearrange("p (b a d) -> p a b d", b=2, a=2, d=64),
                                                              in0=zt[:, 0:256].rearrange("p (a b d) -> p a b d", a=2, b=2, d=64),
                                                              in1=st6[:, 16:20].rearrange("p (a b) -> p a b", a=2).unsqueeze(3)
                                                              .to_broadcast([128, 2, 2, 64]),
                                                              op=ALU.mult), r=[B_zt, B_st6], w=[B_qk12])
                        S.op("dve", lambda e: e.tensor_tensor(out=qk12[:, 256:384].rearrange("p (h d) -> p h d", d=64),
                                                              in0=zt[:, 256:384].rearrange("p (h d) -> p h d", d=64),
                                                              in1=st6[:, 20:22].unsqueeze(2).to_broadcast([128, 2, 64]),
                                                              op=ALU.mult), r=[B_zt, B_st6], w=[B_qk12])

                def a_s4(gi, jj, part):
                    j0, nj = groups[gi]
                    is_ctx = (gi == 0)
                    j = j0 + jj
                    i2 = j % 2
                    qk12 = qk12_2[i2]
                    B_qk12 = B_qk12_2[i2]
                    sg = stg[i2]
                    B_sg = B_stg[i2]
                    if part == "ew" and is_ctx:
                        S.op("pool", lambda e: e.tensor_tensor(out=sg[:, 0:384].rearrange("p (h d) -> p h d", d=64),
                                                               in0=qk12[:, 0:384].rearrange("p (h d) -> p h d", d=64),
                                                               in1=gqk[:], op=ALU.mult),
                             r=[B_qk12, B_gqk], w=[B_sg])
                        S.op("pool", lambda e: e.tensor_copy(out=sg[:, 384:768], in_=qk12[:, 384:768]),
                             r=[B_qk12], w=[B_sg])
                    if part == "ew" and not is_ctx:
                        c4 = cs_t[j % 4]
                        B_c4 = B_cs[j % 4]
                        S.op("pool", lambda e: e.tensor_tensor(out=qk12[:, 0:384].rearrange("p (h d) -> p h d", d=64),
                                                               in0=qk12[:, 0:384].rearrange("p (h d) -> p h d", d=64),
                                                               in1=gqk[:], op=ALU.mult),
                             r=[B_qk12, B_gqk], w=[B_qk12])
                        cosb = c4[:, 0, :].unsqueeze(1).to_broadcast([128, 12, 64])
                        q3 = qk12[:].rearrange("p (h d) -> p h d", d=64)
                        S.op("dve", lambda e: e.tensor_tensor(out=rt1[:].rearrange("p (h d) -> p h d", d=64), in0=q3,
                                                              in1=cosb, op=ALU.mult),
                             r=[B_qk12, B_c4], w=[B_rt1])
                        q5 = qk12[:].rearrange("p (h s t) -> p h s t", s=4, t=16)
                        r5 = rt2[:].rearrange("p (h s t) -> p h s t", s=4, t=16)
                        s5 = c4[:, 1, :].rearrange("p (s t) -> p s t", t=16)
                        for (so, si) in ((0, 1), (1, 0), (2, 3), (3, 2)):
                            S.op("pool", lambda e, so=so, si=si: e.tensor_tensor(
                                out=r5[:, :, so, :], in0=q5[:, :, si, :],
                                in1=s5[:, so, :].unsqueeze(1).to_broadcast([128, 12, 16]), op=ALU.mult),
                                r=[B_qk12, B_c4], w=[B_rt2])
                        S.op("dve", lambda e: e.tensor_tensor(out=sg[:], in0=rt1[:], in1=rt2[:], op=ALU.add),
                             r=[B_rt1, B_rt2], w=[B_sg])
                    if part == "T":
                        for t6 in range(6):
                            S.op("pe", lambda e, t6=t6: e.transpose(out=tp2[:, t6, :], in_=sg[:, t6 * 128:(t6 + 1) * 128],
                                                                    identity=ident[:]),
                                 r=[B_sg, B_const], w=[B_tp2], signal=(t6 == 5))
                    if part == "copy":
                        S.op("dve", lambda e: e.tensor_copy(out=QA[:, j, :, :], in_=tp2[:, 0:3, :]), r=[B_tp2], w=[B_QA[j]])
                        S.op("dve", lambda e: e.tensor_copy(out=QB[:, j, :, :], in_=tp2[:, 3:6, :]), r=[B_tp2], w=[B_QB[j]])

                def a_zf(gi):
                    j0, nj = groups[gi]
                    is_ctx = (gi == 0)
                    hTg = hT[gi % 2]
                    B_hTg = B_hT[gi % 2]
                    N = nj * 128
                    tok0 = j0 * 128
                    for m in range(8):
                        zz = zf[m % 2]
                        B_zz = B_zf[m % 2]
                        for k in range(8):
                            S.op("pe", lambda e, k=k, m=m: e.matmul(zz[:, 0:N], lhsT=w_in[:, k, 1024 + m * 128:1024 + (m + 1) * 128],
                                                                    rhs=hTg[:, k, 0:N], start=(k == 0), stop=(k == 7)),
                                 r=[B_hTg, B_win], w=[B_zz], signal=(k == 7))
                        if m < 4:
                            off = (XO_C if is_ctx else XO_L - CTX) + tok0
                            S.op("dve", lambda e: e.tensor_copy(out=xr[:, m, off:off + N], in_=zz[:, 0:N]),
                                 r=[B_zz], w=[B_xr[m]])
                        else:
                            S.op("act", lambda e: e.activation(out=gg[:, m - 4, tok0:tok0 + N], in_=zz[:, 0:N],
                                                               func=AF.Gelu_apprx_tanh),
                                 r=[B_zz], w=[B_gg[m - 4]])

                nta = len(tiles_a)
                for ti in range(nta + 3):
                    t1 = tiles_a[ti] if ti < nta else None
                    t2 = tiles_a[ti - 1] if 1 <= ti < nta + 1 else None
                    t3 = tiles_a[ti - 2] if 2 <= ti < nta + 2 else None
                    t4 = tiles_a[ti - 3] if 3 <= ti else None
                    if t2:
                        a_s2(*t2, "T")
                    if t1:
                        a_s1(*t1, "act")
                    if t2:
                        a_s2(*t2, "copy")
                    if t3:
                        a_s3(*t3, "a")
                    if t4:
                        a_s4(*t4, "ew")
                    if t2:
                        a_s2(*t2, "mm")
                    if t1:
                        a_s1(*t1, "dve")
                    if t3:
                        a_s3(*t3, "red")
                        a_s3(*t3, "b")
                        a_s3(*t3, "qn")
                    if t4:
                        a_s4(*t4, "T")
                        a_s4(*t4, "copy")
                es_cur[0] = pes
                prev_bufs[0] = TA.bufs
            if stop == "A":
                raise _Stop()

            if dbg and "QA" in dbg_out and b == 0 and l == 0:
                pass

            fes = ExitStack()
            es_cur[0] = fes
            TF = Tracker()
            arena_cur[0] = Arena(A_FEAT)
            featT = sb("featT", [128, 8, NTOK], BF16)
            B_feat = [[TF.new("feat%d_%d" % (k, j)) for j in range(NT)] for k in range(8)]
            es_cur[0] = pes

            with ExitStack() as les:
                es_cur[0] = les
                TL = Tracker()
                arena_cur[0] = Arena(A_LW)
                cw = sb("cw", [128, 4, 4], F32)
                cb = sb("cb", [128, 4], F32)
                lb = sb("lb", [128, 2, 2, 4], F32)
                lam = sb("lam", [128, 2, 4], F32)
                cneg = sb("cneg", [128, 2, 2, 4], F32)
                lt = sb("lt", [128, 8], F32)
                wbd = sb("wbd", [128, 2, 2, 4, 128], BF16)
                B_lw = TL.new("lruw")
                S.op("dve", lambda e: e.memset(wbd[:], 0.0), w=[B_lw])
                for jt in range(4):
                    S.dma("sp", cw[:, :, jt], W["conv_w"][l, jt].rearrange("(m p) -> p m", p=128), w=[B_lw],
                          allow_slow_non_contiguous=True)
                S.dma("sp", cb[:], W["conv_b"][l].rearrange("(m p) -> p m", p=128), w=[B_lw], allow_slow_non_contiguous=True)
                for gi_, nm in enumerate(("lru_b_a", "lru_b_i")):
                    for d in range(2):
                        S.dma("sp", lb[:, gi_, d, :], W[nm][l, d].rearrange("(m p) -> p m", p=128), w=[B_lw],
                              allow_slow_non_contiguous=True)
                for d in range(2):
                    S.dma("sp", lam[:, d, :], W["lru_lambda"][l, d].rearrange("(m p) -> p m", p=128), w=[B_lw],
                          allow_slow_non_contiguous=True)
                for gi_, nm in enumerate(("lru_w_a", "lru_w_i")):
                    for d in range(2):
                        for half in range(2):
                            S.dma("pool", wbd[half * 64:(half + 1) * 64, gi_, d, :, half * 64:(half + 1) * 64],
                                  W[nm][l, d].rearrange("(m two) i j -> two i m j", two=2)[half], w=[B_lw])
                lam2 = lam[:].rearrange("p d m -> p (d m)")
                S.op("act", lambda e: e.activation(out=lt[:], in_=lam2, func=AF.Exp, scale=-1.0), r=[B_lw], w=[B_lw])
                S.op("act", lambda e: e.activation(out=lt[:], in_=lt[:], func=AF.Ln, bias=1.0, scale=1.0), r=[B_lw], w=[B_lw])
                S.op("dve", lambda e: e.tensor_scalar(out=cneg[:, 0, :, :].rearrange("p d m -> p (d m)"), in0=lt[:], scalar1=-8.0,
                                                      scalar2=None, op0=ALU.mult), r=[B_lw], w=[B_lw])
                S.op("dve", lambda e: e.tensor_scalar(out=cneg[:, 1, :, :].rearrange("p d m -> p (d m)"), in0=lt[:], scalar1=-16.0,
                                                      scalar2=None, op0=ALU.mult), r=[B_lw], w=[B_lw])
                nlb = sb("nlb", [128, 2, 2, 4], F32)
                S.op("dve", lambda e: e.tensor_scalar(out=nlb[:].rearrange("p a d m -> p (a d m)"),
                                                      in0=lb[:].rearrange("p a d m -> p (a d m)"),
                                                      scalar1=-1.0, scalar2=None, op0=ALU.mult), r=[B_lw], w=[B_lw])
                xc = sb("xc", [128, NTOK], F32)
                xcb = sb("xcb", [128, NTOK], BF16)
                hf = sb("hf", [128, NTOK], F32)
                B_xc, B_xcb = TL.new("xc"), TL.new("xcb")
                tgroups = [(0, 256)] + [(256 + 512 * g, 512) for g in range(4)]
                B_hfg = [TL.new("hf%d" % g) for g in range(5)]
                NS = 3
                gnames = ("er", "ei", "aa", "mm", "uu", "hb")
                G = {nm: [sb("%s%d" % (nm, i), [128, 512], F32) for i in range(NS)] for nm in gnames}
                BG = {nm: [TL.new("%s%d" % (nm, i)) for i in range(NS)] for nm in gnames}
                pg = [ps("pg%d" % i, [128, 512]) for i in range(2 * NS)]
                B_pg = [TL.new("pg%d" % i) for i in range(2 * NS)]
                cnt = [0]
                for m in range(4):
                    for (dst0, n, off) in ((0, CTX, XO_C), (CTX, SEQ, XO_L)):
                        S.op("dve", lambda e: e.tensor_scalar(out=xc[:, dst0:dst0 + n], in0=xr[:, m, off - 2:off - 2 + n],
                                                              scalar1=cw[:, m, 0:1], scalar2=cb[:, m:m + 1],
                                                              op0=ALU.mult, op1=ALU.add),
                             r=[B_xr[m], B_lw], w=[B_xc])
                        for jt in (1, 2, 3):
                            S.op("dve", lambda e, jt=jt: e.scalar_tensor_tensor(out=xc[:, dst0:dst0 + n],
                                                                               in0=xr[:, m, off - 2 + jt:off - 2 + jt + n],
                                                                               scalar=cw[:, m, jt:jt + 1], in1=xc[:, dst0:dst0 + n],
                                                                               op0=ALU.mult, op1=ALU.add),
                                 r=[B_xr[m], B_lw, B_xc], w=[B_xc])
                    S.op("dve", lambda e: e.tensor_copy(out=xcb[:], in_=xc[:]), r=[B_xc], w=[B_xcb])
                    for d in range(2):
                        order = list(range(5)) if d == 0 else [0, 4, 3, 2, 1]
                        steps = []
                        prev_i2 = None
                        for g in order:
                            i2 = cnt[0] % NS
                            cnt[0] += 1
                            steps.append((g, i2, prev_i2))
                            prev_i2 = i2

                        def stage1(g, i2, prv):
                            t0, n = tgroups[g]
                            pa, pi = pg[2 * i2], pg[2 * i2 + 1]
                            er, ei = G["er"][i2], G["ei"][i2]
                            S.op("pe", lambda e: e.matmul(pa[:, 0:n], lhsT=wbd[:, 0, d, m, :], rhs=xcb[:, t0:t0 + n],
                                                          start=True, stop=True), r=[B_lw, B_xcb], w=[B_pg[2 * i2]])
                            S.op("pe", lambda e: e.matmul(pi[:, 0:n], lhsT=wbd[:, 1, d, m, :], rhs=xcb[:, t0:t0 + n],
                                                          start=True, stop=True), r=[B_lw, B_xcb], w=[B_pg[2 * i2 + 1]])
                            S.op("act", lambda e: e.activation(out=er[:, 0:n], in_=pa[:, 0:n], func=AF.Exp, scale=-1.0,
                                                               bias=nlb[:, 0, d, m:m + 1]), r=[B_pg[2 * i2], B_lw], w=[BG["er"][i2]])
                            S.op("act", lambda e: e.activation(out=ei[:, 0:n], in_=pi[:, 0:n], func=AF.Exp, scale=-1.0,
                                                               bias=nlb[:, 1, d, m:m + 1]), r=[B_pg[2 * i2 + 1], B_lw], w=[BG["ei"][i2]])
                            S.op("act", lambda e: e.activation(out=er[:, 0:n], in_=er[:, 0:n], func=AF.Ln, scale=1.0, bias=1.0),
                                 r=[BG["er"][i2]], w=[BG["er"][i2]])
                            S.op("act", lambda e: e.activation(out=er[:, 0:n], in_=er[:, 0:n], func=AF.Exp, scale=-1.0),
                                 r=[BG["er"][i2]], w=[BG["er"][i2]])
                            S.op("dve", lambda e: e.tensor_scalar(out=ei[:, 0:n], in0=ei[:, 0:n], scalar1=1.0, scalar2=None,
                                                                  op0=ALU.add), r=[BG["ei"][i2]], w=[BG["ei"][i2]])
                            S.op("dve", lambda e: e.reciprocal(out=ei[:, 0:n], in_=ei[:, 0:n]), r=[BG["ei"][i2]], w=[BG["ei"][i2]])

                        def stage2(g, i2, prv):
                            t0, n = tgroups[g]
                            er, ei, aa, mmt, uu, hb_ = (G[k_][i2] for k_ in gnames)
                            S.op("act", lambda e: e.activation(out=aa[:, 0:n], in_=er[:, 0:n], func=AF.Exp,
                                                               scale=cneg[:, 0, d, m:m + 1]), r=[BG["er"][i2], B_lw], w=[BG["aa"][i2]])
                            S.op("act", lambda e: e.activation(out=mmt[:, 0:n], in_=er[:, 0:n], func=AF.Exp,
                                                               scale=cneg[:, 1, d, m:m + 1]), r=[BG["er"][i2], B_lw], w=[BG["mm"][i2]])
                            S.op("act", lambda e: e.activation(out=mmt[:, 0:n], in_=mmt[:, 0:n], func=AF.Ln, scale=-1.0, bias=1.0),
                                 r=[BG["mm"][i2]], w=[BG["mm"][i2]])
                            S.op("act", lambda e: e.activation(out=mmt[:, 0:n], in_=mmt[:, 0:n], func=AF.Exp, scale=0.5),
                                 r=[BG["mm"][i2]], w=[BG["mm"][i2]])
                            S.op("pool", lambda e: e.tensor_tensor(out=mmt[:, 0:n], in0=mmt[:, 0:n], in1=ei[:, 0:n], op=ALU.mult),
                                 r=[BG["mm"][i2], BG["ei"][i2]], w=[BG["mm"][i2]])
                            S.op("pool", lambda e: e.tensor_tensor(out=uu[:, 0:n], in0=mmt[:, 0:n], in1=xc[:, t0:t0 + n], op=ALU.mult),
                                 r=[BG["mm"][i2], B_xc], w=[BG["uu"][i2]])
                            if d == 0:
                                init = 0.0 if g == 0 else hf[:, t0 - 1:t0]
                                rdeps = [BG["aa"][i2], BG["uu"][i2]] + ([B_hfg[g - 1]] if g > 0 else [])
                                S.op("dve", lambda e: e.tensor_tensor_scan(out=hf[:, t0:t0 + n], data0=aa[:, 0:n], data1=uu[:, 0:n],
                                                                           initial=init, op0=ALU.mult, op1=ALU.add),
                                     r=rdeps, w=[B_hfg[g]])
                            else:
                                init = 0.0 if prv is None else G["hb"][prv][:, 0:1]
                                rdeps = [BG["aa"][i2], BG["uu"][i2]] + ([BG["hb"][prv]] if prv is not None else [])
                                S.op("dve", lambda e: e.tensor_tensor_scan(out=hb_[:, 0:n][:, ::-1], data0=aa[:, 0:n][:, ::-1],
                                                                           data1=uu[:, 0:n][:, ::-1], initial=init,
                                                                           op0=ALU.mult, op1=ALU.add),
                                     r=rdeps, w=[BG["hb"][i2]])
                                S.op("pool", lambda e: e.tensor_tensor(out=uu[:, 0:n], in0=hb_[:, 0:n], in1=hf[:, t0:t0 + n], op=ALU.add),
                                     r=[BG["hb"][i2], B_hfg[g]], w=[BG["uu"][i2]])
                                S.op("pool", lambda e: e.tensor_tensor(out=featT[:, 4 + m, t0:t0 + n], in0=uu[:, 0:n],
                                                                       in1=gg[:, m, t0:t0 + n], op=ALU.mult),
                                     r=[BG["uu"][i2], B_gg[m]], w=[B_feat[4 + m][t0 // 128 + t] for t in range(n // 128)])

                        for si in range(len(steps) + 1):
                            if si < len(steps):
                                stage1(*steps[si])
                            if si >= 1:
                                stage2(*steps[si - 1])
                es_cur[0] = pes
                prev_bufs[0] = TL.bufs
            if stop == "lru":
                raise _Stop()

            with ExitStack() as tes:
                es_cur[0] = tes
                TT = Tracker()
                load_w_out(l)
                if b == NB - 1:
                    load_w1(l)
                else:
                    load_w_in(l)
                arena_cur[0] = Arena(A_TW)
                sink = sb("sink", [128, 4], F32)
                esk = sb("esk", [128, 4], F32)
                B_sink = TT.new("sink")
                S.dma("sp", sink[:], W["sink_b"][l, :].partition_broadcast(128), w=[B_sink])
                S.op("act", lambda e: e.activation(out=esk[:], in_=sink[:], func=AF.Exp), r=[B_sink], w=[B_sink])
                NSA = 4
                pT = [sb("pT%d" % i, [128, 512], BF16) for i in range(NSA)]
                B_pT = [TT.new("pT%d" % i) for i in range(NSA)]
                rd = [sb("rd%d" % i, [64, 512], F32) for i in range(2)]
                B_rd = [TT.new("rd%d" % i) for i in range(2)]
                sps = [ps("sps%d" % i, [128, 512]) for i in range(NSA)]
                B_sps = [TT.new("sps%d" % i) for i in range(NSA)]
                ops_ = [ps("ops%d" % i, [128, 512]) for i in range(2)]
                B_ops = [TT.new("ops%d" % i) for i in range(2)]
                cs = [0]
                co = [0]
                asteps = []

                def attn_block(Q, Kt, V, B_Q, B_V, qtiles, pair_cols, rows_a, ktiles, masks, n_per, out_specs, sink_cols):
                    a = rows_a
                    rws = slice(a * 64, (a + 1) * 64)
                    io = co[0] % 2
                    co[0] += 1
                    O = ops_[io]
                    B_O = B_ops[io]
                    nq = len(qtiles) * len(pair_cols) * 128
                    if len(pair_cols) == 1:
                        rhs = Q[rws, qtiles[0]:qtiles[0] + len(qtiles), pair_cols[0], :]
                    else:
                        rhs = Q[rws, qtiles[0], 0:2, :]
                    nk = len(ktiles)
                    for ki, kt in enumerate(ktiles):
                        isx = cs[0] % NSA
                        cs[0] += 1

                        def front(isx=isx, kt=kt, ki=ki):
                            S.op("pe", lambda e: e.matmul(sps[isx][:, 0:nq], lhsT=Kt[rws, kt, 2, :], rhs=rhs, start=True, stop=True),
                                 r=[B_Q[kt]] + [B_Q[t] for t in qtiles], w=[B_sps[isx]])
                            S.op("act", lambda e: e.activation(out=pT[isx][:, 0:nq], in_=sps[isx][:, 0:nq], func=AF.Exp, scale=0.125),
                                 r=[B_sps[isx]], w=[B_pT[isx]])
                            if masks[ki] is not None:
                                mk = masks[ki]
                                S.op("pool", lambda e: e.tensor_tensor(out=pT[isx][:, 0:nq].rearrange("p (h q) -> p h q", q=128),
                                                                       in0=pT[isx][:, 0:nq].rearrange("p (h q) -> p h q", q=128),
                                                                       in1=mk[:].unsqueeze(1).to_broadcast([128, nq // 128, 128]),
                                                                       op=ALU.mult), r=[B_pT[isx], B_const], w=[B_pT[isx]])

                        def back(isx=isx, kt=kt, ki=ki):
                            S.op("pe", lambda e: e.matmul(O[:, 0:nq], lhsT=V[:, kt, a, :], rhs=pT[isx][:, 0:nq],
                                                          start=(ki == 0), stop=(ki == nk - 1)),
                                 r=[B_V[kt], B_pT[isx]], w=[B_O], signal=(ki == nk - 1))
                            if ki != nk - 1:
                                return
                            rdt = rd[io]
                            B_rdt = B_rd[io]
                            for (c0, ncol, chunk, poff, tt0, sk) in out_specs:
                                if sk is None:
                                    S.op("dve", lambda e: e.reciprocal(out=rdt[:, c0:c0 + ncol], in_=O[64:128, c0:c0 + ncol]),
                                         r=[B_O], w=[B_rdt])
                                else:
                                    S.op("dve", lambda e: e.tensor_scalar(out=rdt[:, c0:c0 + ncol], in0=O[64:128, c0:c0 + ncol],
                                                                          scalar1=esk[64:128, sk:sk + 1], scalar2=None, op0=ALU.add),
                                         r=[B_O, B_sink], w=[B_rdt])
                                    S.op("dve", lambda e: e.reciprocal(out=rdt[:, c0:c0 + ncol], in_=rdt[:, c0:c0 + ncol]),
                                         r=[B_rdt], w=[B_rdt])
                                ntile = ncol // 128
                                S.op("dve", lambda e: e.tensor_tensor(out=featT[poff:poff + 64, chunk, tt0 * 128:tt0 * 128 + ncol],
                                                                      in0=O[0:64, c0:c0 + ncol], in1=rdt[:, c0:c0 + ncol], op=ALU.mult),
                                     r=[B_O, B_rdt], w=[B_feat[chunk][tt0 + t] for t in range(ntile)])

                        asteps.append((front, back))

                for a in range(2):
                    for bq in range(2):
                        for qb in range(4):
                            qt = list(range(2 + 4 * qb, 6 + 4 * qb))
                            attn_block(QA, QA, VA, B_QA, B_VA, qt, [bq], a, list(range(NT)), [None] * NT, 512,
                                       [(0, 512, a, bq * 64, qt[0], None)], None)
                        if do_ctx:
                            attn_block(QA, QA, VA, B_QA, B_VA, [0, 1], [bq], a, [0, 1], [None, None], 256,
                                       [(0, 256, a, bq * 64, 0, None)], None)
                for a in range(2):
                    for i in range(16):
                        j = 2 + i
                        kts, mks = [0, 1], [None, None]
                        if i > 0:
                            kts.append(j - 1)
                            mks.append(mlo)
                        kts.append(j)
                        mks.append(None)
                        if i < 15:
                            kts.append(j + 1)
                            mks.append(mhi)
                        attn_block(QB, QB, VB, B_QB, B_VB, [j], [0, 1], a, kts, mks, 256,
                                   [(0, 128, 2 + a, 0, j, 2 * a), (128, 128, 2 + a, 64, j, 2 * a + 1)], None)
                    if do_ctx:
                        for j in (0, 1):
                            attn_block(QB, QB, VB, B_QB, B_VB, [j], [0, 1], a, [0, 1], [None, None], 256,
                                       [(0, 128, 2 + a, 0, j, 2 * a), (128, 128, 2 + a, 64, j, 2 * a + 1)], None)
                LOOK = NSA - 1
                for si in range(len(asteps) + LOOK):
                    if si < len(asteps):
                        asteps[si][0]()
                    if si >= LOOK:
                        asteps[si - LOOK][1]()
                es_cur[0] = pes
                prev_bufs[0] = TT.bufs
            if stop == "attn":
                raise _Stop()
            es_cur[0] = es
            prev_bufs[0] = prev_bufs[0] + T0.bufs

        if dbg and "feat" in dbg_out and l == dbg.get("_l", 0) and b == 0:
            with ExitStack() as des:
                es_cur[0] = des
                arena_cur[0] = Arena(A_DBG)
                f32t = sb("dbgf", [128, NTOK], F32)
                B_d = Buf("dbgf")
                S.barrier_bufs(prev_bufs[0], [B_d])
                for k in range(8):
                    S.op("dve", lambda e, k=k: e.tensor_copy(out=f32t[:], in_=featT[:, k, :]),
                         r=[bb for bb in B_feat[k]], w=[B_d])
                    S.dma("sp", dbg_out["feat"][k], f32t[:], r=[B_d])
                es_cur[0] = es
                prev_bufs[0] = prev_bufs[0] + [B_d]

        tiles_c = list(range(NT)) if do_ctx else list(range(2, NT))
        with ExitStack() as ces:
            es_cur[0] = ces
            TC = Tracker()
            if b == NB - 1:
                load_w2(l, 0)
            w_out, B_wout = WTS.w_out
            arena_cur[0] = Arena(A_CAW[1 if b == NB - 1 else 0])
            GT1 = sb("GT1", [128, D], F32)
            G2 = sb("G2", [128, D], F32)
            S2 = sb("S2", [128, D], F32)
            B_GC = TC.new("GC")
            xt = [sb("xtc%d" % i, [128, D], F32) for i in range(3)]
            B_xt = [TC.new("xtc%d" % i) for i in range(3)]
            xn = [sb("xn%d" % i, [128, D], F32) for i in range(3)]
            B_xn = [TC.new("xn%d" % i) for i in range(3)]
            junk = sb("junkc", [128, D], BF16)
            B_junk = TC.new("junkc")
            st = [sb("stc%d" % i, [128, 8], F32) for i in range(3)]
            B_st = [TC.new("stc%d" % i) for i in range(3)]
            tmp = sb("tmpC", [128, D], F32)
            B_tmp = TC.new("tmpC")
            hb = [sb("hbc%d" % i, [128, D], BF16) for i in range(2)]
            B_hb = [TC.new("hbc%d" % i) for i in range(2)]
            h2t = [sb("h2t%d" % i, [128, 8, 128], BF16) for i in range(2)]
            B_h2t = [TC.new("h2t%d" % i) for i in range(2)]
            yp = [ps("yp%d" % i, [128, D]) for i in range(3)]
            B_yp = [TC.new("yp%d" % i) for i in range(3)]
            tp = ps("tpC", [128, 8, 128], BF16)
            B_tp = TC.new("tpC")
            cur_who = [None]

            def c_1(j):
                i2 = tiles_c.index(j) % 3
                src, stok = src_tile(l, b, j)
                S.dma("sp", xt[i2][:], src, r=([stok] if stok else []), w=[B_xt[i2]])
                y = yp[i2]
                for k in range(8):
                    for n in range(2):
                        S.op("pe", lambda e, k=k, n=n: e.matmul(y[:, n * 512:(n + 1) * 512],
                                                                lhsT=featT[:, k, j * 128:(j + 1) * 128],
                                                                rhs=w_out[:, k, n * 512:(n + 1) * 512],
                                                                start=(k == 0), stop=(k == 7)),
                             r=[B_feat[k][j], B_wout], w=[B_yp[i2]], signal=(k == 7 and n == 1))

            def c_2(j, part):
                i2 = tiles_c.index(j) % 3
                y = yp[i2]
                if part == "act":
                    S.op("act", lambda e: e.activation(out=junk[:], in_=y[:], func=AF.Square, accum_out=st[i2][:, 0:1]),
                         r=[B_yp[i2]], w=[B_junk, B_st[i2]])
                    S.op("act", lambda e: e.activation(out=st[i2][:, 1:2], in_=st[i2][:, 0:1], func=AF.Ln,
                                                       scale=1.0 / D, bias=epsc[:, 0:1]), r=[B_st[i2], B_const], w=[B_st[i2]])
                    S.op("act", lambda e: e.activation(out=st[i2][:, 2:3], in_=st[i2][:, 1:2], func=AF.Exp, scale=-0.5),
                         r=[B_st[i2]], w=[B_st[i2]])
                if part == "dve":
                    who = 2 if j < 2 else b
                    if cur_who[0] != who:
                        cur_who[0] = who
                        S.dma("sp", GT1[:], modrows[l, who, 2, :].partition_broadcast(128), r=[B_mod[l]], w=[B_GC])
                    S.op("dve", lambda e: e.scalar_tensor_tensor(out=tmp[:], in0=y[:], scalar=st[i2][:, 2:3], in1=GT1[:],
                                                                 op0=ALU.mult, op1=ALU.mult),
                         r=[B_yp[i2], B_st[i2], B_GC], w=[B_tmp])
                    S.op("dve", lambda e: e.tensor_tensor(out=xn[i2][:], in0=tmp[:], in1=xt[i2][:], op=ALU.add),
                         r=[B_tmp, B_xt[i2]], w=[B_xn[i2]])
                    S.dma("pool", xmid[b, j * 128:(j + 1) * 128, :], xn[i2][:], r=[B_xn[i2]], w=[B_xmid[b][j]])

            def c_3(j, part):
                i2 = tiles_c.index(j) % 3
                ih = j % 2
                s3 = st3[i2]
                B_s3 = B_st3[i2]
                if part == "act":
                    S.op("act", lambda e: e.activation(out=junk[:], in_=xn[i2][:], func=AF.Square, accum_out=s3[:, 0:1]),
                         r=[B_xn[i2]], w=[B_junk, B_s3])
                    S.op("act", lambda e: e.activation(out=s3[:, 1:2], in_=s3[:, 0:1], func=AF.Ln,
                                                       scale=1.0 / D, bias=epsc[:, 0:1]), r=[B_s3, B_const], w=[B_s3])
                    S.op("act", lambda e: e.activation(out=s3[:, 2:3], in_=s3[:, 1:2], func=AF.Exp, scale=-0.5),
                         r=[B_s3], w=[B_s3])
                if part == "dve":
                    who = 2 if j < 2 else b
                    if cur_who3[0] != who:
                        cur_who3[0] = who
                        S.dma("sp", G2[:], modrows[l, who, 4, :].partition_broadcast(128), r=[B_mod[l]], w=[B_GC3])
                        S.dma("sp", S2[:], modrows[l, who, 3, :].partition_broadcast(128), r=[B_mod[l]], w=[B_GC3])
                    S.op("dve", lambda e: e.scalar_tensor_tensor(out=tmp[:], in0=xn[i2][:], scalar=s3[:, 2:3], in1=G2[:],
                                                                 op0=ALU.mult, op1=ALU.mult),
                         r=[B_xn[i2], B_s3, B_GC3], w=[B_tmp])
                    S.op("dve", lambda e: e.tensor_tensor(out=hb[ih][:], in0=tmp[:], in1=S2[:], op=ALU.add),
                         r=[B_tmp, B_GC3], w=[B_hb[ih]])

            def c_4(j, part):
                i2 = j % 2
                if part == "T":
                    for k in range(8):
                        S.op("pe", lambda e, k=k: e.transpose(out=tp[:, k, :], in_=hb[i2][:, k * 128:(k + 1) * 128],
                                                              identity=ident[:]),
                             r=[B_hb[i2], B_const], w=[B_tp], signal=(k == 7))
                if part == "copy":
                    S.op("act", lambda e: e.activation(out=h2t[i2][:], in_=tp[:], func=AF.Copy), r=[B_tp], w=[B_h2t[i2]])
                    S.dma("pool", h2s[b, j].rearrange("p (k t) -> p k t", k=8), h2t[i2][:], r=[B_h2t[i2]], w=[B_h2s[b][j]])

            cur_who3 = [None]
            B_GC3 = TC.new("GC3")
            st3 = [sb("st3c%d" % i, [128, 8], F32) for i in range(3)]
            B_st3 = [TC.new("st3c%d" % i) for i in range(3)]
            ntc = len(tiles_c)

            def tl(k_):
                return tiles_c[k_] if 0 <= k_ < ntc else None
            for ti in range(ntc + 5):
                j1, j2a, j2d, j3a, j3d, j4 = tl(ti), tl(ti - 1), tl(ti - 2), tl(ti - 3), tl(ti - 4), tl(ti - 5)
                if j4 is not None:
                    c_4(j4, "T")
                if j1 is not None:
                    c_1(j1)
                if j2a is not None:
                    c_2(j2a, "act")
                if j3a is not None:
                    c_3(j3a, "act")
                if j4 is not None:
                    c_4(j4, "copy")
                if j2d is not None:
                    c_2(j2d, "dve")
                if j3d is not None:
                    c_3(j3d, "dve")
            es_cur[0] = es
            prev_bufs[0] = TC.bufs
        if stop == "Ca":
            raise _Stop()
        fes.close()
        prev_bufs[0] = prev_bufs[0] + TF.bufs

    def mlp_phase(l, last_layer):
        do_ctx = not last_layer
        tiles_c = list(range(NT)) if do_ctx else list(range(2, NT))
        with ExitStack() as mes:
            es_cur[0] = mes
            TM = Tracker()
            load_w2(l, 1)
            w1, B_w1 = WTS.w1
            w2, B_w2 = WTS.w2
            arena_cur[0] = Arena(A_MW)
            ck("M:w")
            GT2 = sb("GT2", [128, D], F32)
            B_GT2 = TM.new("GT2")
            h2g = [sb("h2g0", [128, 4, 8, 128], BF16)]
            B_h2g = [TM.new("h2g0")]
            aT = sb("aT", [128, 32, 512], BF16)
            B_aT = TM.new("aT")
            rl = [sb("rl%d" % i, [128, 512], F32) for i in range(3)]
            B_rl = [TM.new("rl%d" % i) for i in range(3)]
            xm = [sb("xm%d" % i, [128, D], F32) for i in range(2)]
            B_xm = [TM.new("xm%d" % i) for i in range(2)]
            xo = [sb("xo%d" % i, [128, D], F32) for i in range(2)]
            B_xo = [TM.new("xo%d" % i) for i in range(2)]
            junk = sb("junkm", [128, D], BF16)
            B_junk = TM.new("junkm")
            st = [sb("stm%d" % i, [128, 8], F32) for i in range(2)]
            B_st = [TM.new("stm%d" % i) for i in range(2)]
            tmp = sb("tmpM", [128, D], F32)
            B_tmp = TM.new("tmpM")
            pu = [ps("pu%d" % i, [128, 512]) for i in range(4)]
            B_pu = [TM.new("pu%d" % i) for i in range(4)]
            y2 = [ps("y2_%d" % i, [128, D]) for i in range(2)]
            B_y2 = [TM.new("y2_%d" % i) for i in range(2)]
            cur_who = [None]
            mgroups = ([(0, 2)] if do_ctx else []) + [(2 + 4 * g_, 4) for g_ in range(4)]
            for b in range(NB):
                for (j0, ng) in mgroups:
                    who = 2 if j0 < 2 else b
                    if cur_who[0] != who:
                        cur_who[0] = who
                        S.dma("sp", GT2[:], modrows[l, who, 5, :].partition_broadcast(128), r=[B_mod[l]], w=[B_GT2])
                    hg = h2g[0]
                    B_hg = B_h2g[0]
                    nn = ng * 128
                    for t in range(ng):
                        S.dma("sp", hg[:, t, :, :], h2s[b, j0 + t].rearrange("p (k t) -> p k t", k=8), r=[B_h2s[b][j0 + t]], w=[B_hg])
                    for f in range(32):
                        p_ = pu[f % 4]
                        for k in range(8):
                            S.op("pe", lambda e, k=k, f=f: e.matmul(p_[:, 0:nn].rearrange("p (t q) -> p t q", t=ng),
                                                                    lhsT=w1[:, k, f * 128:(f + 1) * 128], rhs=hg[:, 0:ng, k, :],
                                                                    start=(k == 0), stop=(k == 7)),
                                 r=[B_hg, B_w1[k]], w=[B_pu[f % 4]], signal=(k == 7))
                        r_ = rl[f % 3]
                        S.op("act", lambda e: e.activation(out=r_[:, 0:nn], in_=p_[:, 0:nn], func=AF.Relu),
                             r=[B_pu[f % 4]], w=[B_rl[f % 3]])
                        S.op("pool", lambda e, f=f: e.tensor_tensor(out=aT[:, f, 0:nn], in0=r_[:, 0:nn], in1=r_[:, 0:nn], op=ALU.mult),
                             r=[B_rl[f % 3]], w=[B_aT])
                    for t in range(ng):
                        j = j0 + t
                        i2 = j % 2
                        y = y2[i2]
                        S.dma("sp", xm[i2][:], xmid[b, j * 128:(j + 1) * 128, :], r=[B_xmid[b][j]], w=[B_xm[i2]])
                        for f in range(32):
                            for n in range(2):
                                S.op("pe", lambda e, f=f, n=n: e.matmul(y[:, n * 512:(n + 1) * 512], lhsT=aT[:, f, t * 128:(t + 1) * 128],
                                                                        rhs=w2[:, f, n * 512:(n + 1) * 512],
                                                                        start=(f == 0), stop=(f == 31)),
                                     r=[B_aT, B_w2[f // 4]], w=[B_y2[i2]], signal=(f == 31 and n == 1))
                        S.op("act", lambda e: e.activation(out=junk[:], in_=y[:], func=AF.Square, accum_out=st[i2][:, 0:1]),
                             r=[B_y2[i2]], w=[B_junk, B_st[i2]])
                        S.op("act", lambda e: e.activation(out=st[i2][:, 1:2], in_=st[i2][:, 0:1], func=AF.Ln,
                                                           scale=1.0 / D, bias=epsc[:, 0:1]), r=[B_st[i2], B_const], w=[B_st[i2]])
                        S.op("act", lambda e: e.activation(out=st[i2][:, 2:3], in_=st[i2][:, 1:2], func=AF.Exp, scale=-0.5),
                             r=[B_st[i2]], w=[B_st[i2]])
                        S.op("dve", lambda e: e.scalar_tensor_tensor(out=tmp[:], in0=y[:], scalar=st[i2][:, 2:3], in1=GT2[:],
                                                                     op0=ALU.mult, op1=ALU.mult),
                             r=[B_y2[i2], B_st[i2], B_GT2], w=[B_tmp])
                        S.op("dve", lambda e: e.tensor_tensor(out=xo[i2][:], in0=tmp[:], in1=xm[i2][:], op=ALU.add),
                             r=[B_tmp, B_xm[i2]], w=[B_xo[i2]])
                        if last_layer and final:
                            out_toks.append(S.dma("sp", out_d[b, (j - 2) * 128:(j - 1) * 128, :], xo[i2][:], r=[B_xo[i2]]))
                        else:
                            S.dma("sp", xs[b, j * 128:(j + 1) * 128, :], xo[i2][:], r=[B_xo[i2]], w=[B_xs[b][j]])

            es_cur[0] = es
            prev_bufs[0] = TM.bufs


    stopped = [False]
    try:
        for l in layers:
            mb = modulation(l, preload=lambda l=l: load_w_in(l))
            prev_bufs[0] = prev_bufs[0] + mb
            for b in range(NB):
                layer_batch(l, b, last_layer=(l == DEPTH - 1))
            mlp_phase(l, last_layer=(l == DEPTH - 1))
            if stop == "Cb":
                raise _Stop()
    except _Stop:
        es_cur[0] = es
        stopped[0] = True
    if not final:
        pass
    for t in out_toks:
        S._wait("sp", t)
    for e in ("pe", "act", "dve", "pool"):
        if S.cnt[e] > 0:
            S._wait("sp", (e, S.cnt[e]))
    if not stopped[0]:
        es.close()
    return nc, S


_PROG = {}


def kernel(**inputs):
    n_cores = 8
    if "p" not in _PROG:
        _PROG["p"] = build_program()
    nc, _ = _PROG["p"]
    consts = _const_inputs()
    in_maps = []
    f = lambda a: np.ascontiguousarray(np.asarray(a, dtype=np.float32))
    wts = {k: f(inputs[k]) for k in W_SHAPES}
    x = f(inputs["x"])
    ctx = f(inputs["ctx"])
    c = f(inputs["c"])
    c_ctx = f(inputs["c_ctx"])
    for i in range(n_cores):
        m = {"x": x[NB * i:NB * (i + 1)], "ctx": ctx[NB * i:NB * (i + 1)], "c": c[NB * i:NB * (i + 1)], "c_ctx": c_ctx}
        m.update(wts)
        m.update(consts)
        in_maps.append(m)
    res = run_bass_kernel_spmd(nc, in_maps, core_ids=list(range(n_cores)))
    return np.concatenate([r["out"] for r in res.results], axis=0)
```

```python
import numpy as np
from contextlib import ExitStack
import concourse.bass as bass
import concourse.mybir as mybir
from concourse.bass_utils import run_bass_kernel_spmd

F32 = mybir.dt.float32
BF16 = mybir.dt.bfloat16
AF = mybir.ActivationFunctionType
ALU = mybir.AluOpType
AX = mybir.AxisListType

D = 1024
NB = 2
SEQ = 2048
CTX = 256
NT = (SEQ + CTX) // 128
NTOK = SEQ + CTX
DEPTH = 2
EPS = 1e-6
DFF = 4096
XO_C = 2
XO_L = 262
XW = 2312


class Buf:
    __slots__ = ("name", "w", "r")

    def __init__(self, name):
        self.name = name
        self.w = None
        self.r = []


class Sched:
    KD = 8

    def __init__(self, nc, es):
        self.nc = nc
        self.eng = {"pe": nc.tensor, "act": nc.scalar, "dve": nc.vector, "pool": nc.gpsimd, "sp": nc.sync}
        self.semh = {}
        self.cnt = {}
        self.pending = {}
        self.waited = {}
        for e in self.eng:
            self.semh[e] = es.enter_context(nc.semaphore("s_" + e))
            self.cnt[e] = 0
            self.pending[e] = []
            self.waited[e] = {}
        self.dq = {}
        for q in ("sp", "pool"):
            sems = []
            for i in range(self.KD):
                k = "d_%s%d" % (q, i)
                self.semh[k] = es.enter_context(nc.semaphore(k))
                sems.append(k)
            self.dq[q] = {"sems": sems, "cnt": [0] * self.KD, "next": 0}
        self.n_instr = 0

    def _wait(self, e, tok):
        key, val = tok
        if self.waited[e].get(key, 0) >= val:
            return
        self.eng[e].wait_ge(self.semh[key], val)
        self.waited[e][key] = val

    def _deps(self, e, r, w, is_dma):
        deps = set()
        for b in r:
            if b.w is not None:
                deps.add(b.w)
        for b in w:
            if b.w is not None:
                deps.add(b.w)
            for t in b.r:
                deps.add(t)
        out = []
        for t in deps:
            if t == "PENDING":
                raise RuntimeError("dependency on unsignaled op")
            out.append(t)
        return out

    def op(self, e, fn, r=(), w=(), signal=True):
        r = list(r)
        w = list(w)
        deps = set()
        for b in r:
            if b.w is not None:
                deps.add(b.w)
        for b in w:
            if b.w is not None and b.w[0] != e:
                deps.add(b.w)
            for t in b.r:
                if t[0] != e:
                    deps.add(t)
        for t in deps:
            if t[1] is None:
                raise RuntimeError("dependency on unsignaled op: %s" % (t,))
            self._wait(e, t)
        ins = fn(self.eng[e])
        self.n_instr += 1
        self.pending[e].append((r, w))
        if signal:
            self.cnt[e] += 1
            ins.then_inc(self.semh[e], 1)
            tok = (e, self.cnt[e])
            for (rr, ww) in self.pending[e]:
                for b in rr:
                    b.r.append(tok)
                for b in ww:
                    b.w = tok
                    b.r = []
            self.pending[e] = []
            return tok
        else:
            for b in w:
                b.w = (e, None)
                b.r = []
            return None

    def dma(self, q, out, in_, r=(), w=(), **kw):
        r = list(r)
        w = list(w)
        deps = set()
        for b in r:
            if b.w is not None:
                deps.add(b.w)
        for b in w:
            if b.w is not None:
                deps.add(b.w)
            for t in b.r:
                deps.add(t)
        for t in deps:
            if t[1] is None:
                raise RuntimeError("dma dependency on unsignaled op")
            self._wait(q, t)
        st = self.dq[q]
        i = st["next"]
        st["next"] = (i + 1) % self.KD
        key = st["sems"][i]
        if st["cnt"][i] > 0:
            self._wait(q, (key, st["cnt"][i]))
        self.eng[q].dma_start(out=out, in_=in_, **kw).then_inc(self.semh[key], 16)
        self.n_instr += 1
        st["cnt"][i] += 16
        tok = (key, st["cnt"][i])
        for b in r:
            b.r.append(tok)
        for b in w:
            b.w = tok
            b.r = []
        return tok

    def barrier_bufs(self, bufs_old, bufs_new):
        toks = set()
        for b in bufs_old:
            if b.w is not None:
                toks.add(b.w)
            for t in b.r:
                toks.add(t)
        for b in bufs_new:
            b.r = list(toks)


def _rope_tables():
    n = SEQ
    rows = n // 64
    row = np.repeat(np.arange(rows, dtype=np.float32), 64)
    col = np.tile(np.arange(64, dtype=np.float32), rows)
    half = 32
    inv_freq = (np.float32(10000.0) ** (-np.arange(0, half, 2, dtype=np.float32) / np.float32(half))).astype(np.float32)
    ang_r = (row[:, None] * inv_freq).astype(np.float32)
    ang_c = (col[:, None] * inv_freq).astype(np.float32)
    cr, sr, cc, sc = np.cos(ang_r), np.sin(ang_r), np.cos(ang_c), np.sin(ang_c)
    cos12 = np.concatenate([cr, cr, cc, cc], axis=1).astype(np.float32)
    sin12 = np.concatenate([-sr, sr, -sc, sc], axis=1).astype(np.float32)
    return np.ascontiguousarray(cos12), np.ascontiguousarray(sin12)


def _const_inputs():
    cos12, sin12 = _rope_tables()
    kk = np.arange(128)[:, None]
    qq = np.arange(128)[None, :]
    return {
        "k_ident": np.eye(128, dtype=np.float32),
        "k_mlo": (kk >= qq).astype(np.float32),
        "k_mhi": (kk <= qq).astype(np.float32),
        "k_cos": cos12,
        "k_sin": sin12,
    }


W_SHAPES = {
    "w_mod": [DEPTH, D, 6 * D], "b_mod": [DEPTH, 6 * D],
    "g_pre_mix": [DEPTH, D], "g_post_mix": [DEPTH, D], "g_pre_mlp": [DEPTH, D], "g_post_mlp": [DEPTH, D],
    "w_in": [DEPTH, D, 2048], "g_q_a": [DEPTH, 64], "g_k_a": [DEPTH, 64], "sink_b": [DEPTH, 4],
    "conv_w": [DEPTH, 4, 512], "conv_b": [DEPTH, 512],
    "lru_w_a": [DEPTH, 2, 8, 64, 64], "lru_b_a": [DEPTH, 2, 512],
    "lru_w_i": [DEPTH, 2, 8, 64, 64], "lru_b_i": [DEPTH, 2, 512], "lru_lambda": [DEPTH, 2, 512],
    "w_out": [DEPTH, D, D], "w_mlp_in": [DEPTH, D, DFF], "w_mlp_out": [DEPTH, DFF, D],
}


class _Stop(Exception):
    pass


def build_program(layers=(0, 1), final=True, dbg=None, stop=None):
    nc = bass.Bass("TRN2", target_bir_lowering=False)
    es = ExitStack()
    dram = {}

    def din(name, shape):
        dram[name] = nc.dram_tensor(name, list(shape), F32, kind="ExternalInput").ap()
        return dram[name]

    x_in = din("x", [NB, SEQ, D])
    ctx_in = din("ctx", [NB, CTX, D])
    c_in = din("c", [NB, D])
    cctx_in = din("c_ctx", [D])
    W = {k: din(k, s) for k, s in W_SHAPES.items()}
    k_ident = din("k_ident", [128, 128])
    k_mlo = din("k_mlo", [128, 128])
    k_mhi = din("k_mhi", [128, 128])
    k_cos = din("k_cos", [SEQ, 64])
    k_sin = din("k_sin", [SEQ, 64])
    out_d = nc.dram_tensor("out", [NB, SEQ, D], F32, kind="ExternalOutput").ap()
    ikind = "ExternalOutput" if dbg else "Internal"
    xmid = nc.dram_tensor("xmid", [NB, NTOK, D], F32, kind=ikind).ap()
    xs = nc.dram_tensor("xs", [NB, NTOK, D], F32, kind=ikind).ap()
    h2s = nc.dram_tensor("h2s", [NB, NT, 128, D], BF16, kind="Internal").ap()
    modrows = nc.dram_tensor("modrows", [DEPTH, 3, 6, D], F32, kind=ikind).ap()
    dbg_out = {}
    if dbg:
        for name, shape in dbg.items():
            if name.startswith("_"):
                continue
            dbg_out[name] = nc.dram_tensor("dbg_" + name, list(shape), F32, kind="ExternalOutput").ap()

    S = Sched(nc, es)

    ckn = [0]

    def ck(tag):
        if stop is not None and stop.startswith("ck:"):
            ckn[0] += 1
            if ckn[0] == int(stop[3:]):
                print("STOP at checkpoint", ckn[0], tag)
                raise _Stop()

    uid = [0]

    SB_TOT = 207 * 1024
    Mbig = es.enter_context(nc.sbuf_tensor("Mbig", [128, SB_TOT], mybir.dt.uint8))
    DTSZ = {F32: 4, BF16: 2}

    class Arena:
        def __init__(self, ranges_kb):
            self.ranges = [(int(a * 1024), int(b_ * 1024)) for a, b_ in ranges_kb]
            self.i = 0
            self.p = self.ranges[0][0]

        def alloc(self, nbytes):
            nbytes = (nbytes + 31) // 32 * 32
            while True:
                lo, hi = self.ranges[self.i]
                if self.p + nbytes <= hi:
                    off = self.p
                    self.p += nbytes
                    return off
                self.i += 1
                if self.i >= len(self.ranges):
                    raise RuntimeError("arena full")
                self.p = self.ranges[self.i][0]

    arena_cur = [None]

    def sb(name, shape, dt, side=None):
        shape = list(shape)
        n = 1
        for d_ in shape[1:]:
            n *= d_
        nbytes = n * DTSZ[dt]
        off = arena_cur[0].alloc(nbytes)
        v = Mbig[0:shape[0], off:off + nbytes].bitcast(dt)
        if len(shape) > 2:
            names = ["d%d" % i for i in range(len(shape) - 1)]
            pat = "p (%s) -> p %s" % (" ".join(names), " ".join(names))
            v = v.rearrange(pat, **{nm: sz for nm, sz in zip(names[:-1], shape[1:-1])})
        return v

    A_CONST = [(0, 1)]
    A_MOD = [(1, 65)]
    A_T0B = [(1, 56)]
    A_T0A = [(61, 111)]
    A_WIN = [(138, 170)]
    A_AW = [(56, 61), (111, 138), (170, 207)]
    A_FEAT = [(170, 207)]
    A_LW = [(56, 61), (111, 170)]
    A_WOUT = [(111, 127)]
    A_TW = [(127, 138)]
    A_W1 = [(1, 65)]
    A_W2 = [(65, 129)]
    A_CAW = {0: [(1, 65)], 1: [(97, 111), (127, 170)]}
    A_MW = [(129, 207)]
    A_DBG = [(1, 56)]

    class Wt:
        pass
    WTS = Wt()
    WTS.w_in = None
    WTS.w_out = None
    WTS.w1 = None
    WTS.w2 = None

    def load_w_in(l):
        arena_cur[0] = Arena(A_WIN)
        T_ = Tracker()
        w = sb("w_in", [128, 8, 2048], BF16)
        B_ = T_.new("w_in")
        wl = W["w_in"][l]
        for k in range(8):
            for hh in range(2):
                S.dma("pool", w[:, k, hh * 1024:(hh + 1) * 1024], wl[k * 128:(k + 1) * 128, hh * 1024:(hh + 1) * 1024], w=[B_])
        WTS.w_in = (w, B_)

    def load_w_out(l):
        arena_cur[0] = Arena(A_WOUT)
        T_ = Tracker()
        w = sb("w_out", [128, 8, D], BF16)
        B_ = T_.new("w_out")
        for k in range(8):
            S.dma("pool", w[:, k, :], W["w_out"][l, k * 128:(k + 1) * 128, :], w=[B_])
        WTS.w_out = (w, B_)

    def load_w1(l, part):
        if part == 0:
            arena_cur[0] = Arena(A_W1)
            w = sb("w1", [128, 8, DFF], BF16)
            WTS.w1 = (w, [None] * 8)
        w, Bs = WTS.w1
        T_ = Tracker()
        for k in (range(7) if part == 0 else [7]):
            Bs[k] = T_.new("w1_%d" % k)
            for cq in range(4):
                S.dma("pool", w[:, k, cq * 1024:(cq + 1) * 1024],
                      W["w_mlp_in"][l, k * 128:(k + 1) * 128, cq * 1024:(cq + 1) * 1024], w=[Bs[k]])

    def load_w2(l, half):
        if half == 0:
            arena_cur[0] = Arena(A_W2)
            w = sb("w2", [128, 32, D], BF16)
            WTS.w2 = (w, [None] * 8)
        w, Bs = WTS.w2
        T_ = Tracker()
        for k in range(4 * half, 4 * half + 4):
            Bs[k] = T_.new("w2_%d" % k)
            for f4 in range(4):
                f = 4 * k + f4
                S.dma("pool", w[:, f, :], W["w_mlp_out"][l, f * 128:(f + 1) * 128, :], w=[Bs[k]])

    def ps(name, shape, dt=F32):
        uid[0] += 1
        return es_cur[0].enter_context(nc.psum_tensor("%s_p%d" % (name, uid[0]), list(shape), dt))

    es_cur = [es]

    B_xmid = [[Buf("xmid%d_%d" % (b, j)) for j in range(NT)] for b in range(NB)]
    B_xs = [[Buf("xs%d_%d" % (b, j)) for j in range(NT)] for b in range(NB)]
    B_h2s = [[Buf("h2s%d_%d" % (bb, j)) for j in range(NT)] for bb in range(NB)]
    B_mod = [Buf("mod%d" % l) for l in range(DEPTH)]
    out_toks = []

    arena_cur[0] = Arena(A_CONST)
    ident = sb("ident", [128, 128], BF16)
    mlo = sb("mlo", [128, 128], BF16)
    mhi = sb("mhi", [128, 128], BF16)
    nhalf = sb("nhalf", [128, 8], F32)
    B_const = Buf("const")
    S.dma("pool", ident[:], k_ident[:, :], w=[B_const])
    S.dma("pool", mlo[:], k_mlo[:, :], w=[B_const])
    S.dma("pool", mhi[:], k_mhi[:, :], w=[B_const])
    S.op("dve", lambda e: e.memset(nhalf[:], -0.5), w=[B_const])
    epsc = sb("epsc", [128, 8], F32)
    S.op("dve", lambda e: e.memset(epsc[:], float(EPS)), w=[B_const])

    def rstd_from_ss(ss_ap, ss_buf, n, inv_n, out_ap, out_buf, tmp_ap, tmp_buf):
        S.op("dve", lambda e: e.tensor_scalar(out=tmp_ap, in0=ss_ap, scalar1=float(inv_n), scalar2=float(EPS),
                                               op0=ALU.mult, op1=ALU.add), r=[ss_buf], w=[tmp_buf])
        S.op("pool", lambda e: e.tensor_tensor(out=out_ap, in0=tmp_ap, in1=nhalf[:, 0:n], op=ALU.pow),
             r=[tmp_buf, B_const], w=[out_buf])

    def modulation(l, preload=None):
        with ExitStack() as les:
            es_cur[0] = les
            TMod = Tracker()
            if preload is not None:
                preload()
            arena_cur[0] = Arena(A_MOD)
            cT = sb("cT", [128, 8, 4], F32)
            cTb = sb("cTb", [128, 8, 4], BF16)
            bmod = sb("bmod", [3, 6 * D], F32)
            g4 = sb("g4", [3, 4, D], F32)
            wm = [sb("wm%d" % i, [128, 8, 512], BF16) for i in range(2)]
            rows = [sb("mrow%d" % i, [3, 512], F32) for i in range(2)]
            pm = [ps("pm%d" % i, [128, 512]) for i in range(2)]
            B_cT, B_cTb, B_bmod, B_g4 = (TMod.new(n_) for n_ in ("cT", "cTb", "bmod", "g4"))
            B_wm = [TMod.new("wm0"), TMod.new("wm1")]
            B_rows = [TMod.new("r0"), TMod.new("r1")]
            B_pm = [TMod.new("pm0"), TMod.new("pm1")]
            S.op("dve", lambda e: e.memset(cT[:], 0.0), w=[B_cT])
            for b in range(NB):
                S.dma("sp", cT[:, :, b], c_in[b, :].rearrange("(k p) -> p k", p=128), w=[B_cT],
                      allow_slow_non_contiguous=True)
            S.dma("sp", cT[:, :, 2], cctx_in.rearrange("(k p) -> p k", p=128), w=[B_cT],
                  allow_slow_non_contiguous=True)
            S.op("act", lambda e: e.activation(out=cTb[:], in_=cT[:], func=AF.Silu), r=[B_cT], w=[B_cTb])
            S.dma("sp", bmod[:], W["b_mod"][l, :].partition_broadcast(3), w=[B_bmod])
            for i, gname in enumerate(("g_pre_mix", "g_post_mix", "g_pre_mlp", "g_post_mlp")):
                S.dma("sp", g4[:, i, :], W[gname][l, :].partition_broadcast(3), w=[B_g4])
            for j in range(12):
                i = j % 2
                S.dma("pool", wm[i][:], W["w_mod"][l, :, j * 512:(j + 1) * 512].rearrange("(k p) n -> p k n", p=128),
                      w=[B_wm[i]])
                for k in range(8):
                    S.op("pe", lambda e, k=k: e.matmul(pm[i][0:3, :], lhsT=cTb[:, k, 0:3], rhs=wm[i][:, k, :],
                                                       start=(k == 0), stop=(k == 7)),
                         r=[B_cTb, B_wm[i]], w=[B_pm[i]], signal=(k == 7))
                seg = j // 2
                cs = slice((j % 2) * 512, (j % 2) * 512 + 512)
                S.op("dve", lambda e: e.tensor_tensor(out=rows[i][:], in0=pm[i][0:3, :], in1=bmod[:, j * 512:(j + 1) * 512],
                                                      op=ALU.add), r=[B_pm[i], B_bmod], w=[B_rows[i]])
                if seg in (1, 4):
                    gi = 0 if seg == 1 else 2
                    S.op("dve", lambda e: e.scalar_tensor_tensor(out=rows[i][:], in0=rows[i][:], scalar=1.0,
                                                                 in1=g4[:, gi, cs], op0=ALU.add, op1=ALU.mult),
                         r=[B_rows[i], B_g4], w=[B_rows[i]])
                elif seg in (2, 5):
                    gi = 1 if seg == 2 else 3
                    S.op("dve", lambda e: e.tensor_tensor(out=rows[i][:], in0=rows[i][:], in1=g4[:, gi, cs], op=ALU.mult),
                         r=[B_rows[i], B_g4], w=[B_rows[i]])
                S.dma("sp", modrows[l, :, seg, cs], rows[i][:], r=[B_rows[i]], w=[B_mod[l]])
            es_cur[0] = es
        if stop == "mod":
            raise _Stop()
        return [B_cT, B_cTb, B_bmod, B_g4] + B_wm + B_rows + B_pm

    def src_tile(l, b, j):
        if l == 0:
            if j < 2:
                return ctx_in[b, j * 128:(j + 1) * 128, :], None
            return x_in[b, (j - 2) * 128:(j - 1) * 128, :], None
        return xs[b, j * 128:(j + 1) * 128, :], B_xs[b][j]

    prev_bufs = [[]]

    def phase_scope():
        return ExitStack()

    class Tracker:
        def __init__(self):
            self.bufs = []
            fr = [(e, S.cnt[e]) for e in S.cnt if S.cnt[e] > 0]
            for q, st in S.dq.items():
                for i, key in enumerate(st["sems"]):
                    if st["cnt"][i] > 0:
                        fr.append((key, st["cnt"][i]))
            self.frontier = fr

        def new(self, name):
            b = Buf(name)
            b.r = list(self.frontier)
            self.bufs.append(b)
            return b

    def layer_batch(l, b, last_layer):
        do_ctx = not last_layer
        with ExitStack() as pes:
            es_cur[0] = pes
            T0 = Tracker()
            arena_cur[0] = Arena(A_T0A)
            QA = sb("QA", [128, NT, 2, 128], BF16)
            KzA = sb("KzA", [128, 2, NT, 128], BF16)
            QB = sb("QB", [128, NT, 3, 128], BF16)
            VA = sb("VA", [128, NT, 2, 128], BF16)
            VB = sb("VB", [128, NT, 2, 128], BF16)
            arena_cur[0] = Arena(A_T0B)
            xr = sb("xr", [128, 4, XW], F32)
            gg = sb("gg", [128, 4, NTOK], BF16)
            B_QA = [T0.new("QA%d" % j) for j in range(NT)]
            B_QB = [T0.new("QB%d" % j) for j in range(NT)]
            B_VA = [T0.new("VA%d" % j) for j in range(NT)]
            B_VB = [T0.new("VB%d" % j) for j in range(NT)]
            B_xr = [T0.new("xr%d" % m) for m in range(4)]
            B_gg = [T0.new("gg%d" % m) for m in range(4)]
            S.op("pool", lambda e: e.memset(KzA[:], 0.0), w=B_QA)
            for j in range(NT):
                S.op("pool", lambda e, j=j: e.memset(VA[:, j, :, 64:128], 1.0), w=[B_VA[j]])
                S.op("pool", lambda e, j=j: e.memset(VB[:, j, :, 64:128], 1.0), w=[B_VB[j]])
            for m in range(4):
                S.op("pool", lambda e, m=m: e.memset(xr[:, m, :], 0.0), w=[B_xr[m]])

            with ExitStack() as aes:
                es_cur[0] = aes
                TA = Tracker()
                w_in, B_win = WTS.w_in
                arena_cur[0] = Arena(A_AW)
                G1 = sb("G1", [128, D], F32)
                S1 = sb("S1", [128, D], F32)
                B_G1 = TA.new("G1")
                gqk = sb("gqk", [128, 6, 64], F32)
                B_gqk = TA.new("gqk")
                S.dma("sp", gqk[:, 0, :], W["g_q_a"][l, :].partition_broadcast(128), w=[B_gqk])
                S.dma("sp", gqk[:, 4, :], W["g_k_a"][l, :].partition_broadcast(128), w=[B_gqk])
                for hh in (1, 2, 3):
                    S.op("dve", lambda e, hh=hh: e.tensor_copy(out=gqk[:, hh, :], in_=gqk[:, 0, :]), r=[B_gqk], w=[B_gqk])
                S.op("dve", lambda e: e.tensor_copy(out=gqk[:, 5, :], in_=gqk[:, 4, :]), r=[B_gqk], w=[B_gqk])
                xt = [sb("xt%d" % i, [128, D], F32) for i in range(2)]
                B_xt = [TA.new("xt%d" % i) for i in range(2)]
                junk = sb("junk", [128, D], BF16)
                B_junk = TA.new("junk")
                st = [sb("st%d" % i, [128, 8], F32) for i in range(2)]
                B_st = [TA.new("st%d" % i) for i in range(2)]
                tmp = sb("tmpA", [128, D], F32)
                B_tmp = TA.new("tmpA")
                hb = [sb("hb%d" % i, [128, D], BF16) for i in range(2)]
                B_hb = [TA.new("hb%d" % i) for i in range(2)]
                hT = [sb("hT%d" % i, [128, 8, 512], BF16) for i in range(2)]
                B_hT = [TA.new("hT%d" % i) for i in range(2)]
                sq = sb("sq", [128, 384], F32)
                B_sq = TA.new("sq")
                st6 = sb("st6", [128, 24], F32)
                B_st6 = TA.new("st6")
                qk12_2 = [sb("qk12_%d" % i, [128, 768], F32) for i in range(2)]
                B_qk12_2 = [TA.new("qk12_%d" % i) for i in range(2)]
                rt1 = sb("rt1", [128, 768], F32)
                rt2 = sb("rt2", [128, 768], F32)
                B_rt1, B_rt2 = TA.new("rt1"), TA.new("rt2")
                stg = [sb("stg%d" % i, [128, 768], BF16) for i in range(2)]
                B_stg = [TA.new("stg%d" % i) for i in range(2)]
                cs_t = [sb("cs%d" % i, [128, 2, 64], F32) for i in range(4)]
                B_cs = [TA.new("cs%d" % i) for i in range(4)]
                tp = ps("tpA", [128, 8, 128], BF16)
                B_tp = TA.new("tpA")
                zt2 = [ps("zt%d" % i, [128, 1024]) for i in range(2)]
                B_zt2 = [TA.new("zt%d" % i) for i in range(2)]
                zf = [ps("zf%d" % i, [128, 512]) for i in range(2)]
                B_zf = [TA.new("zf%d" % i) for i in range(2)]
                tp2 = ps("tp2", [128, 6, 128], BF16)
                B_tp2 = TA.new("tp2")

                ck("A:setup")
                groups = [(0, 2)] + [(2 + 4 * g, 4) for g in range(4)]
                tiles_a = [(gi, jj) for gi, (j0, nj) in enumerate(groups) for jj in range(nj)]

                def a_s1(gi, jj, part):
                    j0, nj = groups[gi]
                    is_ctx = (gi == 0)
                    j = j0 + jj
                    i2 = j % 2
                    if part == "act":
                        if jj == 0 and gi in (0, 1):
                            who = 2 if is_ctx else b
                            S.dma("sp", G1[:], modrows[l, who, 1, :].partition_broadcast(128), r=[B_mod[l]], w=[B_G1])
                            S.dma("sp", S1[:], modrows[l, who, 0, :].partition_broadcast(128), r=[B_mod[l]], w=[B_G1])
                        src, sbuf_tok = src_tile(l, b, j)
                        S.dma("sp", xt[i2][:], src, r=([sbuf_tok] if sbuf_tok else []), w=[B_xt[i2]])
                        if not is_ctx:
                            S.dma("sp", cs_t[j % 4][:, 0, :], k_cos[(j - 2) * 128:(j - 1) * 128, :], w=[B_cs[j % 4]])
                            S.dma("sp", cs_t[j % 4][:, 1, :], k_sin[(j - 2) * 128:(j - 1) * 128, :], w=[B_cs[j % 4]])
                        S.op("act", lambda e: e.activation(out=junk[:], in_=xt[i2][:], func=AF.Square,
                                                           accum_out=st[i2][:, 0:1]),
                             r=[B_xt[i2]], w=[B_junk, B_st[i2]])
                        S.op("act", lambda e: e.activation(out=st[i2][:, 1:2], in_=st[i2][:, 0:1], func=AF.Ln,
                                                           scale=1.0 / D, bias=epsc[:, 0:1]), r=[B_st[i2], B_const], w=[B_st[i2]])
                        S.op("act", lambda e: e.activation(out=st[i2][:, 2:3], in_=st[i2][:, 1:2], func=AF.Exp, scale=-0.5),
                             r=[B_st[i2]], w=[B_st[i2]])
                    if part == "dve":
                        S.op("dve", lambda e: e.scalar_tensor_tensor(out=tmp[:], in0=xt[i2][:], scalar=st[i2][:, 2:3],
                                                                     in1=G1[:], op0=ALU.mult, op1=ALU.mult),
                             r=[B_xt[i2], B_st[i2], B_G1], w=[B_tmp])
                        S.op("dve", lambda e: e.tensor_tensor(out=hb[i2][:], in0=tmp[:], in1=S1[:], op=ALU.add),
                             r=[B_tmp, B_G1], w=[B_hb[i2]])

                def a_s2(gi, jj, part):
                    j0, nj = groups[gi]
                    hTg = hT[gi % 2]
                    B_hTg = B_hT[gi % 2]
                    j = j0 + jj
                    i2 = j % 2
                    zt = zt2[i2]
                    B_zt = B_zt2[i2]
                    if part == "T":
                        for k in range(8):
                            S.op("pe", lambda e, k=k: e.transpose(out=tp[:, k, :], in_=hb[i2][:, k * 128:(k + 1) * 128],
                                                                  identity=ident[:]),
                                 r=[B_hb[i2], B_const], w=[B_tp], signal=(k == 7))
                    if part == "copy":
                        S.op("act", lambda e: e.activation(out=hTg[:, :, jj * 128:(jj + 1) * 128], in_=tp[:], func=AF.Copy),
                             r=[B_tp], w=[B_hTg])
                    if part == "mm":
                        for k in range(8):
                            for n in range(2):
                                S.op("pe", lambda e, k=k, n=n: e.matmul(zt[:, n * 512:(n + 1) * 512],
                                                                        lhsT=hTg[:, k, jj * 128:(jj + 1) * 128],
                                                                        rhs=w_in[:, k, n * 512:(n + 1) * 512],
                                                                        start=(k == 0), stop=(k == 7)),
                                     r=[B_hTg, B_win], w=[B_zt], signal=(k == 7 and n == 1))
                        if jj == nj - 1:
                            a_zf(gi)

                def a_s3(gi, jj, part):
                    j0, nj = groups[gi]
                    j = j0 + jj
                    i2 = j % 2
                    zt = zt2[i2]
                    B_zt = B_zt2[i2]
                    qk12 = qk12_2[i2]
                    B_qk12 = B_qk12_2[i2]
                    if part == "a":
                        S.op("act", lambda e: e.activation(out=sq[:], in_=zt[:, 0:384], func=AF.Square),
                             r=[B_zt], w=[B_sq])
                        S.op("act", lambda e: e.activation(out=qk12[:, 384:640].rearrange("p (b a d) -> p a b d", b=2, a=2, d=64),
                                                           in_=zt[:, 512:768].rearrange("p (a b d) -> p a b d", a=2, b=2, d=64),
                                                           func=AF.Copy), r=[B_zt], w=[B_qk12])
                        S.op("act", lambda e: e.activation(out=qk12[:, 640:768], in_=zt[:, 768:896], func=AF.Copy),
                             r=[B_zt], w=[B_qk12])
                        S.op("act", lambda e: e.activation(out=VA[:, j, :, 0:64],
                                                           in_=zt[:, 384:512].rearrange("p (a d) -> p a d", d=64), func=AF.Copy),
                             r=[B_zt], w=[B_VA[j]])
                        S.op("act", lambda e: e.activation(out=VB[:, j, :, 0:64],
                                                           in_=zt[:, 896:1024].rearrange("p (a d) -> p a d", d=64), func=AF.Copy),
                             r=[B_zt], w=[B_VB[j]])
                    if part == "red":
                        S.op("dve", lambda e: e.tensor_reduce(out=st6[:, 0:6], in_=sq[:].rearrange("p (h d) -> p h d", d=64),
                                                              axis=AX.X, op=ALU.add), r=[B_sq], w=[B_st6])
                    if part == "b":
                        S.op("act", lambda e: e.activation(out=st6[:, 8:14], in_=st6[:, 0:6], func=AF.Ln,
                                                           scale=1.0 / 64, bias=epsc[:, 0:1]), r=[B_st6, B_const], w=[B_st6])
                        S.op("act", lambda e: e.activation(out=st6[:, 16:22], in_=st6[:, 8:14], func=AF.Exp, scale=-0.5),
                             r=[B_st6], w=[B_st6])
                    if part == "qn":
                        S.op("dve", lambda e: e.tensor_tensor(out=qk12[:, 0:256].rearrange("p (b a d) -> p a b d", b=2, a=2, d=64),
                                                              in0=zt[:, 0:256].rearrange("p (a b d) -> p a b d", a=2, b=2, d=64),
                                                              in1=st6[:, 16:20].rearrange("p (a b) -> p a b", a=2).unsqueeze(3)
                                                              .to_broadcast([128, 2, 2, 64]),
                                                              op=ALU.mult), r=[B_zt, B_st6], w=[B_qk12])
                        S.op("dve", lambda e: e.tensor_tensor(out=qk12[:, 256:384].rearrange("p (h d) -> p h d", d=64),
                                                              in0=zt[:, 256:384].rearrange("p (h d) -> p h d", d=64),
                                                              in1=st6[:, 20:22].unsqueeze(2).to_broadcast([128, 2, 64]),
                                                              op=ALU.mult), r=[B_zt, B_st6], w=[B_qk12])

                def a_s4(gi, jj, part):
                    j0, nj = groups[gi]
                    is_ctx = (gi == 0)
                    j = j0 + jj
                    i2 = j % 2
                    qk12 = qk12_2[i2]
                    B_qk12 = B_qk12_2[i2]
                    sg = stg[i2]
                    B_sg = B_stg[i2]
                    if part == "ew" and is_ctx:
                        S.op("pool", lambda e: e.tensor_tensor(out=sg[:, 0:384].rearrange("p (h d) -> p h d", d=64),
                                                               in0=qk12[:, 0:384].rearrange("p (h d) -> p h d", d=64),
                                                               in1=gqk[:], op=ALU.mult),
                             r=[B_qk12, B_gqk], w=[B_sg])
                        S.op("pool", lambda e: e.tensor_copy(out=sg[:, 384:768], in_=qk12[:, 384:768]),
                             r=[B_qk12], w=[B_sg])
                    if part == "ew" and not is_ctx:
                        c4 = cs_t[j % 4]
                        B_c4 = B_cs[j % 4]
                        S.op("pool", lambda e: e.tensor_tensor(out=qk12[:, 0:384].rearrange("p (h d) -> p h d", d=64),
                                                               in0=qk12[:, 0:384].rearrange("p (h d) -> p h d", d=64),
                                                               in1=gqk[:], op=ALU.mult),
                             r=[B_qk12, B_gqk], w=[B_qk12])
                        cosb = c4[:, 0, :].unsqueeze(1).to_broadcast([128, 12, 64])
                        q3 = qk12[:].rearrange("p (h d) -> p h d", d=64)
                        S.op("dve", lambda e: e.tensor_tensor(out=rt1[:].rearrange("p (h d) -> p h d", d=64), in0=q3,
                                                              in1=cosb, op=ALU.mult),
                             r=[B_qk12, B_c4], w=[B_rt1])
                        q5 = qk12[:].rearrange("p (h s t) -> p h s t", s=4, t=16)
                        r5 = rt2[:].rearrange("p (h s t) -> p h s t", s=4, t=16)
                        s5 = c4[:, 1, :].rearrange("p (s t) -> p s t", t=16)
                        for (so, si) in ((0, 1), (1, 0), (2, 3), (3, 2)):
                            S.op("pool", lambda e, so=so, si=si: e.tensor_tensor(
                                out=r5[:, :, so, :], in0=q5[:, :, si, :],
                                in1=s5[:, so, :].unsqueeze(1).to_broadcast([128, 12, 16]), op=ALU.mult),
                                r=[B_qk12, B_c4], w=[B_rt2])
                        S.op("dve", lambda e: e.tensor_tensor(out=sg[:], in0=rt1[:], in1=rt2[:], op=ALU.add),
                             r=[B_rt1, B_rt2], w=[B_sg])
                    if part == "T":
                        for t6 in range(6):
                            S.op("pe", lambda e, t6=t6: e.transpose(out=tp2[:, t6, :], in_=sg[:, t6 * 128:(t6 + 1) * 128],
                                                                    identity=ident[:]),
                                 r=[B_sg, B_const], w=[B_tp2], signal=(t6 == 5))
                    if part == "copy":
                        S.op("dve", lambda e: e.tensor_copy(out=QA[:, j, :, :], in_=tp2[:, 0:2, :]), r=[B_tp2], w=[B_QA[j]])
                        S.op("dve", lambda e: e.tensor_copy(out=KzA[0:64, 0, j, :], in_=tp2[0:64, 2, :]), r=[B_tp2], w=[B_QA[j]])
                        S.op("dve", lambda e: e.tensor_copy(out=KzA[64:128, 1, j, :], in_=tp2[64:128, 2, :]), r=[B_tp2], w=[B_QA[j]])
                        S.op("dve", lambda e: e.tensor_copy(out=QB[:, j, :, :], in_=tp2[:, 3:6, :]), r=[B_tp2], w=[B_QB[j]])

                def a_zf(gi):
                    j0, nj = groups[gi]
                    is_ctx = (gi == 0)
                    hTg = hT[gi % 2]
                    B_hTg = B_hT[gi % 2]
                    N = nj * 128
                    tok0 = j0 * 128
                    for m in range(8):
                        zz = zf[m % 2]
                        B_zz = B_zf[m % 2]
                        for k in range(8):
                            S.op("pe", lambda e, k=k, m=m: e.matmul(zz[:, 0:N], lhsT=w_in[:, k, 1024 + m * 128:1024 + (m + 1) * 128],
                                                                    rhs=hTg[:, k, 0:N], start=(k == 0), stop=(k == 7)),
                                 r=[B_hTg, B_win], w=[B_zz], signal=(k == 7))
                        if m < 4:
                            off = (XO_C if is_ctx else XO_L - CTX) + tok0
                            S.op("dve", lambda e: e.tensor_copy(out=xr[:, m, off:off + N], in_=zz[:, 0:N]),
                                 r=[B_zz], w=[B_xr[m]])
                        else:
                            S.op("act", lambda e: e.activation(out=gg[:, m - 4, tok0:tok0 + N], in_=zz[:, 0:N],
                                                               func=AF.Gelu_apprx_tanh),
                                 r=[B_zz], w=[B_gg[m - 4]])

                nta = len(tiles_a)
                for ti in range(nta + 3):
                    t1 = tiles_a[ti] if ti < nta else None
                    t2 = tiles_a[ti - 1] if 1 <= ti < nta + 1 else None
                    t3 = tiles_a[ti - 2] if 2 <= ti < nta + 2 else None
                    t4 = tiles_a[ti - 3] if 3 <= ti else None
                    if t2:
                        a_s2(*t2, "T")
                    if t1:
                        a_s1(*t1, "act")
                    if t2:
                        a_s2(*t2, "copy")
                    if t3:
                        a_s3(*t3, "a")
                    if t4:
                        a_s4(*t4, "ew")
                    if t2:
                        a_s2(*t2, "mm")
                    if t1:
                        a_s1(*t1, "dve")
                    if t3:
                        a_s3(*t3, "red")
                        a_s3(*t3, "b")
                        a_s3(*t3, "qn")
                    if t4:
                        a_s4(*t4, "T")
                        a_s4(*t4, "copy")
                es_cur[0] = pes
                prev_bufs[0] = TA.bufs
            if stop == "A":
                raise _Stop()

            if dbg and "QA" in dbg_out and b == 0 and l == 0:
                pass

            fes = ExitStack()
            es_cur[0] = fes
            TF = Tracker()
            arena_cur[0] = Arena(A_FEAT)
            featT = sb("featT", [128, 8, NTOK], BF16)
            B_feat = [[TF.new("feat%d_%d" % (k, j)) for j in range(NT)] for k in range(8)]
            es_cur[0] = pes

            with ExitStack() as les:
                es_cur[0] = les
                TL = Tracker()
                arena_cur[0] = Arena(A_LW)
                cw = sb("cw", [128, 4, 4], F32)
                cb = sb("cb", [128, 4], F32)
                lb = sb("lb", [128, 2, 2, 4], F32)
                lam = sb("lam", [128, 2, 4], F32)
                cneg = sb("cneg", [128, 2, 2, 4], F32)
                lt = sb("lt", [128, 8], F32)
                wbd = sb("wbd", [128, 2, 2, 4, 128], BF16)
                B_lw = TL.new("lruw")
                S.op("dve", lambda e: e.memset(wbd[:], 0.0), w=[B_lw])
                for jt in range(4):
                    S.dma("sp", cw[:, :, jt], W["conv_w"][l, jt].rearrange("(m p) -> p m", p=128), w=[B_lw],
                          allow_slow_non_contiguous=True)
                S.dma("sp", cb[:], W["conv_b"][l].rearrange("(m p) -> p m", p=128), w=[B_lw], allow_slow_non_contiguous=True)
                for gi_, nm in enumerate(("lru_b_a", "lru_b_i")):
                    for d in range(2):
                        S.dma("sp", lb[:, gi_, d, :], W[nm][l, d].rearrange("(m p) -> p m", p=128), w=[B_lw],
                              allow_slow_non_contiguous=True)
                for d in range(2):
                    S.dma("sp", lam[:, d, :], W["lru_lambda"][l, d].rearrange("(m p) -> p m", p=128), w=[B_lw],
                          allow_slow_non_contiguous=True)
                for gi_, nm in enumerate(("lru_w_a", "lru_w_i")):
                    for d in range(2):
                        for half in range(2):
                            S.dma("pool", wbd[half * 64:(half + 1) * 64, gi_, d, :, half * 64:(half + 1) * 64],
                                  W[nm][l, d].rearrange("(m two) i j -> two i m j", two=2)[half], w=[B_lw])
                lam2 = lam[:].rearrange("p d m -> p (d m)")
                S.op("act", lambda e: e.activation(out=lt[:], in_=lam2, func=AF.Exp, scale=-1.0), r=[B_lw], w=[B_lw])
                S.op("act", lambda e: e.activation(out=lt[:], in_=lt[:], func=AF.Ln, bias=1.0, scale=1.0), r=[B_lw], w=[B_lw])
                S.op("dve", lambda e: e.tensor_scalar(out=cneg[:, 0, :, :].rearrange("p d m -> p (d m)"), in0=lt[:], scalar1=-8.0,
                                                      scalar2=None, op0=ALU.mult), r=[B_lw], w=[B_lw])
                S.op("dve", lambda e: e.tensor_scalar(out=cneg[:, 1, :, :].rearrange("p d m -> p (d m)"), in0=lt[:], scalar1=-16.0,
                                                      scalar2=None, op0=ALU.mult), r=[B_lw], w=[B_lw])
                nlb = sb("nlb", [128, 2, 2, 4], F32)
                S.op("dve", lambda e: e.tensor_scalar(out=nlb[:].rearrange("p a d m -> p (a d m)"),
                                                      in0=lb[:].rearrange("p a d m -> p (a d m)"),
                                                      scalar1=-1.0, scalar2=None, op0=ALU.mult), r=[B_lw], w=[B_lw])
                xc = sb("xc", [128, NTOK], F32)
                xcb = sb("xcb", [128, NTOK], BF16)
                hf = sb("hf", [128, NTOK], F32)
                B_xc, B_xcb = TL.new("xc"), TL.new("xcb")
                tgroups = [(0, 256)] + [(256 + 512 * g, 512) for g in range(4)]
                B_hfg = [TL.new("hf%d" % g) for g in range(5)]
                NS = 3
                gnames = ("er", "ei", "aa", "mm", "uu", "hb")
                G = {nm: [sb("%s%d" % (nm, i), [128, 512], F32) for i in range(NS)] for nm in gnames}
                BG = {nm: [TL.new("%s%d" % (nm, i)) for i in range(NS)] for nm in gnames}
                pg = [ps("pg%d" % i, [128, 512]) for i in range(2 * NS)]
                B_pg = [TL.new("pg%d" % i) for i in range(2 * NS)]
                cnt = [0]
                for m in range(4):
                    for (dst0, n, off) in ((0, CTX, XO_C), (CTX, SEQ, XO_L)):
                        S.op("dve", lambda e: e.tensor_scalar(out=xc[:, dst0:dst0 + n], in0=xr[:, m, off - 2:off - 2 + n],
                                                              scalar1=cw[:, m, 0:1], scalar2=cb[:, m:m + 1],
                                                              op0=ALU.mult, op1=ALU.add),
                             r=[B_xr[m], B_lw], w=[B_xc])
                        for jt in (1, 2, 3):
                            S.op("dve", lambda e, jt=jt: e.scalar_tensor_tensor(out=xc[:, dst0:dst0 + n],
                                                                               in0=xr[:, m, off - 2 + jt:off - 2 + jt + n],
                                                                               scalar=cw[:, m, jt:jt + 1], in1=xc[:, dst0:dst0 + n],
                                                                               op0=ALU.mult, op1=ALU.add),
                                 r=[B_xr[m], B_lw, B_xc], w=[B_xc])
                    S.op("dve", lambda e: e.tensor_copy(out=xcb[:], in_=xc[:]), r=[B_xc], w=[B_xcb])
                    for d in range(2):
                        order = list(range(5)) if d == 0 else [0, 4, 3, 2, 1]
                        steps = []
                        prev_i2 = None
                        for g in order:
                            i2 = cnt[0] % NS
                            cnt[0] += 1
                            steps.append((g, i2, prev_i2))
                            prev_i2 = i2

                        def stage1(g, i2, prv):
                            t0, n = tgroups[g]
                            pa, pi = pg[2 * i2], pg[2 * i2 + 1]
                            er, ei = G["er"][i2], G["ei"][i2]
                            S.op("pe", lambda e: e.matmul(pa[:, 0:n], lhsT=wbd[:, 0, d, m, :], rhs=xcb[:, t0:t0 + n],
                                                          start=True, stop=True), r=[B_lw, B_xcb], w=[B_pg[2 * i2]])
                            S.op("pe", lambda e: e.matmul(pi[:, 0:n], lhsT=wbd[:, 1, d, m, :], rhs=xcb[:, t0:t0 + n],
                                                          start=True, stop=True), r=[B_lw, B_xcb], w=[B_pg[2 * i2 + 1]])
                            S.op("act", lambda e: e.activation(out=er[:, 0:n], in_=pa[:, 0:n], func=AF.Exp, scale=-1.0,
                                                               bias=nlb[:, 0, d, m:m + 1]), r=[B_pg[2 * i2], B_lw], w=[BG["er"][i2]])
                            S.op("act", lambda e: e.activation(out=ei[:, 0:n], in_=pi[:, 0:n], func=AF.Exp, scale=-1.0,
                                                               bias=nlb[:, 1, d, m:m + 1]), r=[B_pg[2 * i2 + 1], B_lw], w=[BG["ei"][i2]])
                            S.op("act", lambda e: e.activation(out=er[:, 0:n], in_=er[:, 0:n], func=AF.Ln, scale=1.0, bias=1.0),
                                 r=[BG["er"][i2]], w=[BG["er"][i2]])
                            S.op("act", lambda e: e.activation(out=er[:, 0:n], in_=er[:, 0:n], func=AF.Exp, scale=-1.0),
                                 r=[BG["er"][i2]], w=[BG["er"][i2]])
                            S.op("dve", lambda e: e.tensor_scalar(out=ei[:, 0:n], in0=ei[:, 0:n], scalar1=1.0, scalar2=None,
                                                                  op0=ALU.add), r=[BG["ei"][i2]], w=[BG["ei"][i2]])
                            S.op("dve", lambda e: e.reciprocal(out=ei[:, 0:n], in_=ei[:, 0:n]), r=[BG["ei"][i2]], w=[BG["ei"][i2]])

                        def stage2(g, i2, prv):
                            t0, n = tgroups[g]
                            er, ei, aa, mmt, uu, hb_ = (G[k_][i2] for k_ in gnames)
                            S.op("act", lambda e: e.activation(out=aa[:, 0:n], in_=er[:, 0:n], func=AF.Exp,
                                                               scale=cneg[:, 0, d, m:m + 1]), r=[BG["er"][i2], B_lw], w=[BG["aa"][i2]])
                            S.op("act", lambda e: e.activation(out=mmt[:, 0:n], in_=er[:, 0:n], func=AF.Exp,
                                                               scale=cneg[:, 1, d, m:m + 1]), r=[BG["er"][i2], B_lw], w=[BG["mm"][i2]])
                            S.op("act", lambda e: e.activation(out=mmt[:, 0:n], in_=mmt[:, 0:n], func=AF.Ln, scale=-1.0, bias=1.0),
                                 r=[BG["mm"][i2]], w=[BG["mm"][i2]])
                            S.op("act", lambda e: e.activation(out=mmt[:, 0:n], in_=mmt[:, 0:n], func=AF.Exp, scale=0.5),
                                 r=[BG["mm"][i2]], w=[BG["mm"][i2]])
                            S.op("pool", lambda e: e.tensor_tensor(out=mmt[:, 0:n], in0=mmt[:, 0:n], in1=ei[:, 0:n], op=ALU.mult),
                                 r=[BG["mm"][i2], BG["ei"][i2]], w=[BG["mm"][i2]])
                            S.op("pool", lambda e: e.tensor_tensor(out=uu[:, 0:n], in0=mmt[:, 0:n], in1=xc[:, t0:t0 + n], op=ALU.mult),
                                 r=[BG["mm"][i2], B_xc], w=[BG["uu"][i2]])
                            if d == 0:
                                init = 0.0 if g == 0 else hf[:, t0 - 1:t0]
                                rdeps = [BG["aa"][i2], BG["uu"][i2]] + ([B_hfg[g - 1]] if g > 0 else [])
                                S.op("dve", lambda e: e.tensor_tensor_scan(out=hf[:, t0:t0 + n], data0=aa[:, 0:n], data1=uu[:, 0:n],
                                                                           initial=init, op0=ALU.mult, op1=ALU.add),
                                     r=rdeps, w=[B_hfg[g]])
                            else:
                                init = 0.0 if prv is None else G["hb"][prv][:, 0:1]
                                rdeps = [BG["aa"][i2], BG["uu"][i2]] + ([BG["hb"][prv]] if prv is not None else [])
                                S.op("dve", lambda e: e.tensor_tensor_scan(out=hb_[:, 0:n][:, ::-1], data0=aa[:, 0:n][:, ::-1],
                                                                           data1=uu[:, 0:n][:, ::-1], initial=init,
                                                                           op0=ALU.mult, op1=ALU.add),
                                     r=rdeps, w=[BG["hb"][i2]])
                                S.op("pool", lambda e: e.tensor_tensor(out=uu[:, 0:n], in0=hb_[:, 0:n], in1=hf[:, t0:t0 + n], op=ALU.add),
                                     r=[BG["hb"][i2], B_hfg[g]], w=[BG["uu"][i2]])
                                S.op("pool", lambda e: e.tensor_tensor(out=featT[:, 4 + m, t0:t0 + n], in0=uu[:, 0:n],
                                                                       in1=gg[:, m, t0:t0 + n], op=ALU.mult),
                                     r=[BG["uu"][i2], B_gg[m]], w=[B_feat[4 + m][t0 // 128 + t] for t in range(n // 128)])

                        for si in range(len(steps) + 1):
                            if si < len(steps):
                                stage1(*steps[si])
                            if si >= 1:
                                stage2(*steps[si - 1])
                es_cur[0] = pes
                prev_bufs[0] = TL.bufs
            if stop == "lru":
                raise _Stop()

            with ExitStack() as tes:
                es_cur[0] = tes
                TT = Tracker()
                load_w_out(l)
                if b == NB - 1:
                    load_w1(l, 0)
                else:
                    load_w_in(l)
                arena_cur[0] = Arena(A_TW)
                sink = sb("sink", [128, 4], F32)
                esk = sb("esk", [128, 4], F32)
                B_sink = TT.new("sink")
                S.dma("sp", sink[:], W["sink_b"][l, :].partition_broadcast(128), w=[B_sink])
                S.op("act", lambda e: e.activation(out=esk[:], in_=sink[:], func=AF.Exp), r=[B_sink], w=[B_sink])
                NSA = 4
                pT = [sb("pT%d" % i, [128, 512], BF16) for i in range(NSA)]
                B_pT = [TT.new("pT%d" % i) for i in range(NSA)]
                rd = [sb("rd%d" % i, [64, 512], F32) for i in range(2)]
                B_rd = [TT.new("rd%d" % i) for i in range(2)]
                sps = [ps("sps%d" % i, [128, 512]) for i in range(NSA)]
                B_sps = [TT.new("sps%d" % i) for i in range(NSA)]
                ops_ = [ps("ops%d" % i, [128, 512]) for i in range(2)]
                B_ops = [TT.new("ops%d" % i) for i in range(2)]
                cs = [0]
                co = [0]
                asteps = []

                def attn_block(Q, Kt, V, B_Q, B_V, qtiles, pair_cols, rows_a, ktiles, masks, n_per, out_specs, sink_cols):
                    a = rows_a
                    rws = slice(a * 64, (a + 1) * 64)
                    io = co[0] % 2
                    co[0] += 1
                    O = ops_[io]
                    B_O = B_ops[io]
                    nq = len(qtiles) * len(pair_cols) * 128
                    kz = sink_cols
                    qrows = slice(0, 128) if kz is not None else rws
                    if len(pair_cols) == 1:
                        rhs = Q[qrows, qtiles[0]:qtiles[0] + len(qtiles), pair_cols[0], :]
                    else:
                        rhs = Q[qrows, qtiles[0], 0:2, :]
                    nk = len(ktiles)
                    for ki, kt in enumerate(ktiles):
                        isx = cs[0] % NSA
                        cs[0] += 1

                        def front(isx=isx, kt=kt, ki=ki):
                            S.op("pe", lambda e: e.matmul(sps[isx][:, 0:nq],
                                                          lhsT=(kz[:, a, kt, :] if kz is not None else Kt[rws, kt, 2, :]),
                                                          rhs=rhs, start=True, stop=True),
                                 r=[B_Q[kt]] + [B_Q[t] for t in qtiles], w=[B_sps[isx]])
                            S.op("act", lambda e: e.activation(out=pT[isx][:, 0:nq], in_=sps[isx][:, 0:nq], func=AF.Exp, scale=0.125),
                                 r=[B_sps[isx]], w=[B_pT[isx]])
                            if masks[ki] is not None:
                                mk = masks[ki]
                                S.op("pool", lambda e: e.tensor_tensor(out=pT[isx][:, 0:nq].rearrange("p (h q) -> p h q", q=128),
                                                                       in0=pT[isx][:, 0:nq].rearrange("p (h q) -> p h q", q=128),
                                                                       in1=mk[:].unsqueeze(1).to_broadcast([128, nq // 128, 128]),
                                                                       op=ALU.mult), r=[B_pT[isx], B_const], w=[B_pT[isx]])

                        def back(isx=isx, kt=kt, ki=ki):
                            S.op("pe", lambda e: e.matmul(O[:, 0:nq], lhsT=V[:, kt, a, :], rhs=pT[isx][:, 0:nq],
                                                          start=(ki == 0), stop=(ki == nk - 1)),
                                 r=[B_V[kt], B_pT[isx]], w=[B_O], signal=(ki == nk - 1))
                            if ki != nk - 1:
                                return
                            rdt = rd[io]
                            B_rdt = B_rd[io]
                            for (c0, ncol, chunk, poff, tt0, sk) in out_specs:
                                if sk is None:
                                    S.op("dve", lambda e: e.reciprocal(out=rdt[:, c0:c0 + ncol], in_=O[64:128, c0:c0 + ncol]),
                                         r=[B_O], w=[B_rdt])
                                else:
                                    S.op("dve", lambda e: e.tensor_scalar(out=rdt[:, c0:c0 + ncol], in0=O[64:128, c0:c0 + ncol],
                                                                          scalar1=esk[64:128, sk:sk + 1], scalar2=None, op0=ALU.add),
                                         r=[B_O, B_sink], w=[B_rdt])
                                    S.op("dve", lambda e: e.reciprocal(out=rdt[:, c0:c0 + ncol], in_=rdt[:, c0:c0 + ncol]),
                                         r=[B_rdt], w=[B_rdt])
                                ntile = ncol // 128
                                S.op("dve", lambda e: e.tensor_tensor(out=featT[poff:poff + 64, chunk, tt0 * 128:tt0 * 128 + ncol],
                                                                      in0=O[0:64, c0:c0 + ncol], in1=rdt[:, c0:c0 + ncol], op=ALU.mult),
                                     r=[B_O, B_rdt], w=[B_feat[chunk][tt0 + t] for t in range(ntile)])

                        asteps.append((front, back))

                for a in range(2):
                    for bq in range(2):
                        for qb in range(4):
                            qt = list(range(2 + 4 * qb, 6 + 4 * qb))
                            attn_block(QA, QA, VA, B_QA, B_VA, qt, [bq], a, list(range(NT)), [None] * NT, 512,
                                       [(0, 512, a, bq * 64, qt[0], None)], KzA)
                        if do_ctx:
                            attn_block(QA, QA, VA, B_QA, B_VA, [0, 1], [bq], a, [0, 1], [None, None], 256,
                                       [(0, 256, a, bq * 64, 0, None)], KzA)
                for a in range(2):
                    for i in range(16):
                        j = 2 + i
                        kts, mks = [0, 1], [None, None]
                        if i > 0:
                            kts.append(j - 1)
                            mks.append(mlo)
                        kts.append(j)
                        mks.append(None)
                        if i < 15:
                            kts.append(j + 1)
                            mks.append(mhi)
                        attn_block(QB, QB, VB, B_QB, B_VB, [j], [0, 1], a, kts, mks, 256,
                                   [(0, 128, 2 + a, 0, j, 2 * a), (128, 128, 2 + a, 64, j, 2 * a + 1)], None)
                    if do_ctx:
                        for j in (0, 1):
                            attn_block(QB, QB, VB, B_QB, B_VB, [j], [0, 1], a, [0, 1], [None, None], 256,
                                       [(0, 128, 2 + a, 0, j, 2 * a), (128, 128, 2 + a, 64, j, 2 * a + 1)], None)
                LOOK = NSA - 1
                for si in range(len(asteps) + LOOK):
                    if si < len(asteps):
                        asteps[si][0]()
                    if si >= LOOK:
                        asteps[si - LOOK][1]()
                es_cur[0] = pes
                prev_bufs[0] = TT.bufs
            if stop == "attn":
                raise _Stop()
            es_cur[0] = es
            prev_bufs[0] = prev_bufs[0] + T0.bufs

        if dbg and "feat" in dbg_out and l == dbg.get("_l", 0) and b == 0:
            with ExitStack() as des:
                es_cur[0] = des
                arena_cur[0] = Arena(A_DBG)
                f32t = sb("dbgf", [128, NTOK], F32)
                B_d = Buf("dbgf")
                S.barrier_bufs(prev_bufs[0], [B_d])
                for k in range(8):
                    S.op("dve", lambda e, k=k: e.tensor_copy(out=f32t[:], in_=featT[:, k, :]),
                         r=[bb for bb in B_feat[k]], w=[B_d])
                    S.dma("sp", dbg_out["feat"][k], f32t[:], r=[B_d])
                es_cur[0] = es
                prev_bufs[0] = prev_bufs[0] + [B_d]

        tiles_c = list(range(NT)) if do_ctx else list(range(2, NT))
        with ExitStack() as ces:
            es_cur[0] = ces
            TC = Tracker()
            if b == NB - 1:
                load_w1(l, 1)
                load_w2(l, 0)
            w_out, B_wout = WTS.w_out
            arena_cur[0] = Arena(A_CAW[1 if b == NB - 1 else 0])
            GT1 = sb("GT1", [128, D], F32)
            G2 = sb("G2", [128, D], F32)
            S2 = sb("S2", [128, D], F32)
            B_GC = TC.new("GC")
            xt = [sb("xtc%d" % i, [128, D], F32) for i in range(3)]
            B_xt = [TC.new("xtc%d" % i) for i in range(3)]
            xn = [sb("xn%d" % i, [128, D], F32) for i in range(3)]
            B_xn = [TC.new("xn%d" % i) for i in range(3)]
            junk = sb("junkc", [128, D], BF16)
            B_junk = TC.new("junkc")
            st = [sb("stc%d" % i, [128, 8], F32) for i in range(3)]
            B_st = [TC.new("stc%d" % i) for i in range(3)]
            tmp = sb("tmpC", [128, D], F32)
            B_tmp = TC.new("tmpC")
            hb = [sb("hbc%d" % i, [128, D], BF16) for i in range(2)]
            B_hb = [TC.new("hbc%d" % i) for i in range(2)]
            h2t = [sb("h2t%d" % i, [128, 8, 128], BF16) for i in range(2)]
            B_h2t = [TC.new("h2t%d" % i) for i in range(2)]
            yp = [ps("yp%d" % i, [128, D]) for i in range(3)]
            B_yp = [TC.new("yp%d" % i) for i in range(3)]
            tp = ps("tpC", [128, 8, 128], BF16)
            B_tp = TC.new("tpC")
            cur_who = [None]

            def c_1(j):
                i2 = tiles_c.index(j) % 3
                src, stok = src_tile(l, b, j)
                S.dma("sp", xt[i2][:], src, r=([stok] if stok else []), w=[B_xt[i2]])
                y = yp[i2]
                for k in range(8):
                    for n in range(2):
                        S.op("pe", lambda e, k=k, n=n: e.matmul(y[:, n * 512:(n + 1) * 512],
                                                                lhsT=featT[:, k, j * 128:(j + 1) * 128],
                                                                rhs=w_out[:, k, n * 512:(n + 1) * 512],
                                                                start=(k == 0), stop=(k == 7)),
                             r=[B_feat[k][j], B_wout], w=[B_yp[i2]], signal=(k == 7 and n == 1))

            def c_2(j, part):
                i2 = tiles_c.index(j) % 3
                y = yp[i2]
                if part == "act":
                    S.op("act", lambda e: e.activation(out=junk[:], in_=y[:], func=AF.Square, accum_out=st[i2][:, 0:1]),
                         r=[B_yp[i2]], w=[B_junk, B_st[i2]])
                    S.op("act", lambda e: e.activation(out=st[i2][:, 1:2], in_=st[i2][:, 0:1], func=AF.Ln,
                                                       scale=1.0 / D, bias=epsc[:, 0:1]), r=[B_st[i2], B_const], w=[B_st[i2]])
                    S.op("act", lambda e: e.activation(out=st[i2][:, 2:3], in_=st[i2][:, 1:2], func=AF.Exp, scale=-0.5),
                         r=[B_st[i2]], w=[B_st[i2]])
                if part == "dve":
                    who = 2 if j < 2 else b
                    if cur_who[0] != who:
                        cur_who[0] = who
                        S.dma("sp", GT1[:], modrows[l, who, 2, :].partition_broadcast(128), r=[B_mod[l]], w=[B_GC])
                    S.op("dve", lambda e: e.scalar_tensor_tensor(out=tmp[:], in0=y[:], scalar=st[i2][:, 2:3], in1=GT1[:],
                                                                 op0=ALU.mult, op1=ALU.mult),
                         r=[B_yp[i2], B_st[i2], B_GC], w=[B_tmp])
                    S.op("dve", lambda e: e.tensor_tensor(out=xn[i2][:], in0=tmp[:], in1=xt[i2][:], op=ALU.add),
                         r=[B_tmp, B_xt[i2]], w=[B_xn[i2]])
                    S.dma("sp", xmid[b, j * 128:(j + 1) * 128, :], xn[i2][:], r=[B_xn[i2]], w=[B_xmid[b][j]])

            def c_3(j, part):
                i2 = tiles_c.index(j) % 3
                ih = j % 2
                s3 = st3[i2]
                B_s3 = B_st3[i2]
                if part == "act":
                    S.op("act", lambda e: e.activation(out=junk[:], in_=xn[i2][:], func=AF.Square, accum_out=s3[:, 0:1]),
                         r=[B_xn[i2]], w=[B_junk, B_s3])
                    S.op("act", lambda e: e.activation(out=s3[:, 1:2], in_=s3[:, 0:1], func=AF.Ln,
                                                       scale=1.0 / D, bias=epsc[:, 0:1]), r=[B_s3, B_const], w=[B_s3])
                    S.op("act", lambda e: e.activation(out=s3[:, 2:3], in_=s3[:, 1:2], func=AF.Exp, scale=-0.5),
                         r=[B_s3], w=[B_s3])
                if part == "dve":
                    who = 2 if j < 2 else b
                    if cur_who3[0] != who:
                        cur_who3[0] = who
                        S.dma("sp", G2[:], modrows[l, who, 4, :].partition_broadcast(128), r=[B_mod[l]], w=[B_GC3])
                        S.dma("sp", S2[:], modrows[l, who, 3, :].partition_broadcast(128), r=[B_mod[l]], w=[B_GC3])
                    S.op("dve", lambda e: e.scalar_tensor_tensor(out=tmp[:], in0=xn[i2][:], scalar=s3[:, 2:3], in1=G2[:],
                                                                 op0=ALU.mult, op1=ALU.mult),
                         r=[B_xn[i2], B_s3, B_GC3], w=[B_tmp])
                    S.op("dve", lambda e: e.tensor_tensor(out=hb[ih][:], in0=tmp[:], in1=S2[:], op=ALU.add),
                         r=[B_tmp, B_GC3], w=[B_hb[ih]])

            def c_4(j, part):
                i2 = j % 2
                if part == "T":
                    for k in range(8):
                        S.op("pe", lambda e, k=k: e.transpose(out=tp[:, k, :], in_=hb[i2][:, k * 128:(k + 1) * 128],
                                                              identity=ident[:]),
                             r=[B_hb[i2], B_const], w=[B_tp], signal=(k == 7))
                if part == "copy":
                    S.op("act", lambda e: e.activation(out=h2t[i2][:], in_=tp[:], func=AF.Copy), r=[B_tp], w=[B_h2t[i2]])
                    S.dma("sp", h2s[b, j].rearrange("p (k t) -> p k t", k=8), h2t[i2][:], r=[B_h2t[i2]], w=[B_h2s[b][j]])

            cur_who3 = [None]
            B_GC3 = TC.new("GC3")
            st3 = [sb("st3c%d" % i, [128, 8], F32) for i in range(3)]
            B_st3 = [TC.new("st3c%d" % i) for i in range(3)]
            ntc = len(tiles_c)

            def tl(k_):
                return tiles_c[k_] if 0 <= k_ < ntc else None
            for ti in range(ntc + 5):
                j1, j2a, j2d, j3a, j3d, j4 = tl(ti), tl(ti - 1), tl(ti - 2), tl(ti - 3), tl(ti - 4), tl(ti - 5)
                if j4 is not None:
                    c_4(j4, "T")
                if j1 is not None:
                    c_1(j1)
                if j2a is not None:
                    c_2(j2a, "act")
                if j3a is not None:
                    c_3(j3a, "act")
                if j4 is not None:
                    c_4(j4, "copy")
                if j2d is not None:
                    c_2(j2d, "dve")
                if j3d is not None:
                    c_3(j3d, "dve")
            es_cur[0] = es
            prev_bufs[0] = TC.bufs
        if stop == "Ca":
            raise _Stop()
        fes.close()
        prev_bufs[0] = prev_bufs[0] + TF.bufs

    def mlp_phase(l, last_layer):
        do_ctx = not last_layer
        tiles_c = list(range(NT)) if do_ctx else list(range(2, NT))
        with ExitStack() as mes:
            es_cur[0] = mes
            TM = Tracker()
            load_w2(l, 1)
            w1, B_w1 = WTS.w1
            w2, B_w2 = WTS.w2
            arena_cur[0] = Arena(A_MW)
            ck("M:w")
            GT2 = sb("GT2", [128, D], F32)
            B_GT2 = TM.new("GT2")
            h2g = [sb("h2g0", [128, 4, 8, 128], BF16)]
            B_h2g = [TM.new("h2g0")]
            aT = sb("aT", [128, 32, 512], BF16)
            B_aT = TM.new("aT")
            rl = [sb("rl%d" % i, [128, 512], F32) for i in range(3)]
            B_rl = [TM.new("rl%d" % i) for i in range(3)]
            xm = [sb("xm%d" % i, [128, D], F32) for i in range(2)]
            B_xm = [TM.new("xm%d" % i) for i in range(2)]
            xo = [sb("xo%d" % i, [128, D], F32) for i in range(2)]
            B_xo = [TM.new("xo%d" % i) for i in range(2)]
            junk = sb("junkm", [128, D], BF16)
            B_junk = TM.new("junkm")
            st = [sb("stm%d" % i, [128, 8], F32) for i in range(2)]
            B_st = [TM.new("stm%d" % i) for i in range(2)]
            tmp = sb("tmpM", [128, D], F32)
            B_tmp = TM.new("tmpM")
            pu = [ps("pu%d" % i, [128, 512]) for i in range(4)]
            B_pu = [TM.new("pu%d" % i) for i in range(4)]
            y2 = [ps("y2_%d" % i, [128, D]) for i in range(2)]
            B_y2 = [TM.new("y2_%d" % i) for i in range(2)]
            cur_who = [None]
            mgroups = ([(0, 2)] if do_ctx else []) + [(2 + 4 * g_, 4) for g_ in range(4)]
            for b in range(NB):
                for (j0, ng) in mgroups:
                    who = 2 if j0 < 2 else b
                    if cur_who[0] != who:
                        cur_who[0] = who
                        S.dma("sp", GT2[:], modrows[l, who, 5, :].partition_broadcast(128), r=[B_mod[l]], w=[B_GT2])
                    hg = h2g[0]
                    B_hg = B_h2g[0]
                    nn = ng * 128
                    for t in range(ng):
                        S.dma("sp", hg[:, t, :, :], h2s[b, j0 + t].rearrange("p (k t) -> p k t", k=8), r=[B_h2s[b][j0 + t]], w=[B_hg])
                    for f in range(32):
                        p_ = pu[f % 4]
                        for k in range(8):
                            S.op("pe", lambda e, k=k, f=f: e.matmul(p_[:, 0:nn].rearrange("p (t q) -> p t q", t=ng),
                                                                    lhsT=w1[:, k, f * 128:(f + 1) * 128], rhs=hg[:, 0:ng, k, :],
                                                                    start=(k == 0), stop=(k == 7)),
                                 r=[B_hg, B_w1[k]], w=[B_pu[f % 4]], signal=(k == 7))
                        r_ = rl[f % 3]
                        S.op("act", lambda e: e.activation(out=r_[:, 0:nn], in_=p_[:, 0:nn], func=AF.Relu),
                             r=[B_pu[f % 4]], w=[B_rl[f % 3]])
                        S.op("pool", lambda e, f=f: e.tensor_tensor(out=aT[:, f, 0:nn], in0=r_[:, 0:nn], in1=r_[:, 0:nn], op=ALU.mult),
                             r=[B_rl[f % 3]], w=[B_aT])
                    for t in range(ng):
                        j = j0 + t
                        i2 = j % 2
                        y = y2[i2]
                        S.dma("sp", xm[i2][:], xmid[b, j * 128:(j + 1) * 128, :], r=[B_xmid[b][j]], w=[B_xm[i2]])
                        for f in range(32):
                            for n in range(2):
                                S.op("pe", lambda e, f=f, n=n: e.matmul(y[:, n * 512:(n + 1) * 512], lhsT=aT[:, f, t * 128:(t + 1) * 128],
                                                                        rhs=w2[:, f, n * 512:(n + 1) * 512],
                                                                        start=(f == 0), stop=(f == 31)),
                                     r=[B_aT, B_w2[f // 4]], w=[B_y2[i2]], signal=(f == 31 and n == 1))
                        S.op("act", lambda e: e.activation(out=junk[:], in_=y[:], func=AF.Square, accum_out=st[i2][:, 0:1]),
                             r=[B_y2[i2]], w=[B_junk, B_st[i2]])
                        S.op("act", lambda e: e.activation(out=st[i2][:, 1:2], in_=st[i2][:, 0:1], func=AF.Ln,
                                                           scale=1.0 / D, bias=epsc[:, 0:1]), r=[B_st[i2], B_const], w=[B_st[i2]])
                        S.op("act", lambda e: e.activation(out=st[i2][:, 2:3], in_=st[i2][:, 1:2], func=AF.Exp, scale=-0.5),
                             r=[B_st[i2]], w=[B_st[i2]])
                        S.op("dve", lambda e: e.scalar_tensor_tensor(out=tmp[:], in0=y[:], scalar=st[i2][:, 2:3], in1=GT2[:],
                                                                     op0=ALU.mult, op1=ALU.mult),
                             r=[B_y2[i2], B_st[i2], B_GT2], w=[B_tmp])
                        S.op("dve", lambda e: e.tensor_tensor(out=xo[i2][:], in0=tmp[:], in1=xm[i2][:], op=ALU.add),
                             r=[B_tmp, B_xm[i2]], w=[B_xo[i2]])
                        if last_layer and final:
                            out_toks.append(S.dma("sp", out_d[b, (j - 2) * 128:(j - 1) * 128, :], xo[i2][:], r=[B_xo[i2]]))
                        else:
                            S.dma("sp", xs[b, j * 128:(j + 1) * 128, :], xo[i2][:], r=[B_xo[i2]], w=[B_xs[b][j]])

            es_cur[0] = es
            prev_bufs[0] = TM.bufs


    stopped = [False]
    try:
        for l in layers:
            mb = modulation(l, preload=lambda l=l: load_w_in(l))
            prev_bufs[0] = prev_bufs[0] + mb
            for b in range(NB):
                layer_batch(l, b, last_layer=(l == DEPTH - 1))
            mlp_phase(l, last_layer=(l == DEPTH - 1))
            if stop == "Cb":
                raise _Stop()
    except _Stop:
        es_cur[0] = es
        stopped[0] = True
    if not final:
        pass
    for t in out_toks:
        S._wait("sp", t)
    for e in ("pe", "act", "dve", "pool"):
        if S.cnt[e] > 0:
            S._wait("sp", (e, S.cnt[e]))
    if not stopped[0]:
        es.close()
    return nc, S


_PROG = {}


def kernel(**inputs):
    n_cores = 8
    if "p" not in _PROG:
        _PROG["p"] = build_program()
    nc, _ = _PROG["p"]
    consts = _const_inputs()
    in_maps = []
    f = lambda a: np.ascontiguousarray(np.asarray(a, dtype=np.float32))
    wts = {k: f(inputs[k]) for k in W_SHAPES}
    x = f(inputs["x"])
    ctx = f(inputs["ctx"])
    c = f(inputs["c"])
    c_ctx = f(inputs["c_ctx"])
    for i in range(n_cores):
        m = {"x": x[NB * i:NB * (i + 1)], "ctx": ctx[NB * i:NB * (i + 1)], "c": c[NB * i:NB * (i + 1)], "c_ctx": c_ctx}
        m.update(wts)
        m.update(consts)
        in_maps.append(m)
    res = run_bass_kernel_spmd(nc, in_maps, core_ids=list(range(n_cores)))
    return np.concatenate([r["out"] for r in res.results], axis=0)
```

```python
import numpy as np
from contextlib import ExitStack
import concourse.bass as bass
import concourse.mybir as mybir
from concourse.bass_utils import run_bass_kernel_spmd

F32 = mybir.dt.float32
BF16 = mybir.dt.bfloat16
AF = mybir.ActivationFunctionType
ALU = mybir.AluOpType
AX = mybir.AxisListType

D = 1024
NB = 2
SEQ = 2048
CTX = 256
NT = (SEQ + CTX) // 128
NTOK = SEQ + CTX
DEPTH = 2
EPS = 1e-6
DFF = 4096
XO_C = 2
XO_L = 262
XW = 2312


class Buf:
    __slots__ = ("name", "w", "r")

    def __init__(self, name):
        self.name = name
        self.w = None
        self.r = []


class Sched:
    KD = 8

    def __init__(self, nc, es):
        self.nc = nc
        self.eng = {"pe": nc.tensor, "act": nc.scalar, "dve": nc.vector, "pool": nc.gpsimd, "sp": nc.sync}
        self.semh = {}
        self.cnt = {}
        self.pending = {}
        self.waited = {}
        for e in self.eng:
            self.semh[e] = es.enter_context(nc.semaphore("s_" + e))
            self.cnt[e] = 0
            self.pending[e] = []
            self.waited[e] = {}
        self.dq = {}
        for q in ("sp", "pool"):
            sems = []
            for i in range(self.KD):
                k = "d_%s%d" % (q, i)
                self.semh[k] = es.enter_context(nc.semaphore(k))
                sems.append(k)
            self.dq[q] = {"sems": sems, "cnt": [0] * self.KD, "next": 0}
        self.n_instr = 0

    def _wait(self, e, tok):
        key, val = tok
        if self.waited[e].get(key, 0) >= val:
            return
        self.eng[e].wait_ge(self.semh[key], val)
        self.waited[e][key] = val

    def _deps(self, e, r, w, is_dma):
        deps = set()
        for b in r:
            if b.w is not None:
                deps.add(b.w)
        for b in w:
            if b.w is not None:
                deps.add(b.w)
            for t in b.r:
                deps.add(t)
        out = []
        for t in deps:
            if t == "PENDING":
                raise RuntimeError("dependency on unsignaled op")
            out.append(t)
        return out

    def op(self, e, fn, r=(), w=(), signal=True):
        r = list(r)
        w = list(w)
        deps = set()
        for b in r:
            if b.w is not None:
                deps.add(b.w)
        for b in w:
            if b.w is not None and b.w[0] != e:
                deps.add(b.w)
            for t in b.r:
                if t[0] != e:
                    deps.add(t)
        for t in deps:
            if t[1] is None:
                raise RuntimeError("dependency on unsignaled op: %s" % (t,))
            self._wait(e, t)
        ins = fn(self.eng[e])
        self.n_instr += 1
        self.pending[e].append((r, w))
        if signal:
            self.cnt[e] += 1
            ins.then_inc(self.semh[e], 1)
            tok = (e, self.cnt[e])
            for (rr, ww) in self.pending[e]:
                for b in rr:
                    b.r.append(tok)
                for b in ww:
                    b.w = tok
                    b.r = []
            self.pending[e] = []
            return tok
        else:
            for b in w:
                b.w = (e, None)
                b.r = []
            return None

    def dma(self, q, out, in_, r=(), w=(), **kw):
        r = list(r)
        w = list(w)
        deps = set()
        for b in r:
            if b.w is not None:
                deps.add(b.w)
        for b in w:
            if b.w is not None:
                deps.add(b.w)
            for t in b.r:
                deps.add(t)
        for t in deps:
            if t[1] is None:
                raise RuntimeError("dma dependency on unsignaled op")
            self._wait(q, t)
        st = self.dq[q]
        i = st["next"]
        st["next"] = (i + 1) % self.KD
        key = st["sems"][i]
        if st["cnt"][i] > 0:
            self._wait(q, (key, st["cnt"][i]))
        self.eng[q].dma_start(out=out, in_=in_, **kw).then_inc(self.semh[key], 16)
        self.n_instr += 1
        st["cnt"][i] += 16
        tok = (key, st["cnt"][i])
        for b in r:
            b.r.append(tok)
        for b in w:
            b.w = tok
            b.r = []
        return tok

    def barrier_bufs(self, bufs_old, bufs_new):
        toks = set()
        for b in bufs_old:
            if b.w is not None:
                toks.add(b.w)
            for t in b.r:
                toks.add(t)
        for b in bufs_new:
            b.r = list(toks)


def _rope_tables():
    n = SEQ
    rows = n // 64
    row = np.repeat(np.arange(rows, dtype=np.float32), 64)
    col = np.tile(np.arange(64, dtype=np.float32), rows)
    half = 32
    inv_freq = (np.float32(10000.0) ** (-np.arange(0, half, 2, dtype=np.float32) / np.float32(half))).astype(np.float32)
    ang_r = (row[:, None] * inv_freq).astype(np.float32)
    ang_c = (col[:, None] * inv_freq).astype(np.float32)
    cr, sr, cc, sc = np.cos(ang_r), np.sin(ang_r), np.cos(ang_c), np.sin(ang_c)
    cos12 = np.concatenate([cr, cr, cc, cc], axis=1).astype(np.float32)
    sin12 = np.concatenate([-sr, sr, -sc, sc], axis=1).astype(np.float32)
    return np.ascontiguousarray(cos12), np.ascontiguousarray(sin12)


def _const_inputs():
    cos12, sin12 = _rope_tables()
    kk = np.arange(128)[:, None]
    qq = np.arange(128)[None, :]
    return {
        "k_ident": np.eye(128, dtype=np.float32),
        "k_mlo": (kk >= qq).astype(np.float32),
        "k_mhi": (kk <= qq).astype(np.float32),
        "k_cos": cos12,
        "k_sin": sin12,
    }


W_SHAPES = {
    "w_mod": [DEPTH, D, 6 * D], "b_mod": [DEPTH, 6 * D],
    "g_pre_mix": [DEPTH, D], "g_post_mix": [DEPTH, D], "g_pre_mlp": [DEPTH, D], "g_post_mlp": [DEPTH, D],
    "w_in": [DEPTH, D, 2048], "g_q_a": [DEPTH, 64], "g_k_a": [DEPTH, 64], "sink_b": [DEPTH, 4],
    "conv_w": [DEPTH, 4, 512], "conv_b": [DEPTH, 512],
    "lru_w_a": [DEPTH, 2, 8, 64, 64], "lru_b_a": [DEPTH, 2, 512],
    "lru_w_i": [DEPTH, 2, 8, 64, 64], "lru_b_i": [DEPTH, 2, 512], "lru_lambda": [DEPTH, 2, 512],
    "w_out": [DEPTH, D, D], "w_mlp_in": [DEPTH, D, DFF], "w_mlp_out": [DEPTH, DFF, D],
}


class _Stop(Exception):
    pass


def build_program(layers=(0, 1), final=True, dbg=None, stop=None):
    nc = bass.Bass("TRN2", target_bir_lowering=False)
    es = ExitStack()
    dram = {}

    def din(name, shape):
        dram[name] = nc.dram_tensor(name, list(shape), F32, kind="ExternalInput").ap()
        return dram[name]

    x_in = din("x", [NB, SEQ, D])
    ctx_in = din("ctx", [NB, CTX, D])
    c_in = din("c", [NB, D])
    cctx_in = din("c_ctx", [D])
    W = {k: din(k, s) for k, s in W_SHAPES.items()}
    k_ident = din("k_ident", [128, 128])
    k_mlo = din("k_mlo", [128, 128])
    k_mhi = din("k_mhi", [128, 128])
    k_cos = din("k_cos", [SEQ, 64])
    k_sin = din("k_sin", [SEQ, 64])
    out_d = nc.dram_tensor("out", [NB, SEQ, D], F32, kind="ExternalOutput").ap()
    ikind = "ExternalOutput" if dbg else "Internal"
    xmid = nc.dram_tensor("xmid", [NB, NTOK, D], F32, kind=ikind).ap()
    xs = nc.dram_tensor("xs", [NB, NTOK, D], F32, kind=ikind).ap()
    h2s = nc.dram_tensor("h2s", [NB, NT, 128, D], BF16, kind="Internal").ap()
    modrows = nc.dram_tensor("modrows", [DEPTH, 3, 6, D], F32, kind=ikind).ap()
    dbg_out = {}
    if dbg:
        for name, shape in dbg.items():
            if name.startswith("_"):
                continue
            dbg_out[name] = nc.dram_tensor("dbg_" + name, list(shape), F32, kind="ExternalOutput").ap()

    S = Sched(nc, es)

    ckn = [0]

    def ck(tag):
        if stop is not None and stop.startswith("ck:"):
            ckn[0] += 1
            if ckn[0] == int(stop[3:]):
                print("STOP at checkpoint", ckn[0], tag)
                raise _Stop()

    uid = [0]

    SB_TOT = 207 * 1024
    Mbig = es.enter_context(nc.sbuf_tensor("Mbig", [128, SB_TOT], mybir.dt.uint8))
    DTSZ = {F32: 4, BF16: 2}

    class Arena:
        def __init__(self, ranges_kb):
            self.ranges = [(int(a * 1024), int(b_ * 1024)) for a, b_ in ranges_kb]
            self.i = 0
            self.p = self.ranges[0][0]

        def alloc(self, nbytes):
            nbytes = (nbytes + 31) // 32 * 32
            while True:
                lo, hi = self.ranges[self.i]
                if self.p + nbytes <= hi:
                    off = self.p
                    self.p += nbytes
                    return off
                self.i += 1
                if self.i >= len(self.ranges):
                    raise RuntimeError("arena full")
                self.p = self.ranges[self.i][0]

    arena_cur = [None]

    def sb(name, shape, dt, side=None):
        shape = list(shape)
        n = 1
        for d_ in shape[1:]:
            n *= d_
        nbytes = n * DTSZ[dt]
        off = arena_cur[0].alloc(nbytes)
        v = Mbig[0:shape[0], off:off + nbytes].bitcast(dt)
        if len(shape) > 2:
            names = ["d%d" % i for i in range(len(shape) - 1)]
            pat = "p (%s) -> p %s" % (" ".join(names), " ".join(names))
            v = v.rearrange(pat, **{nm: sz for nm, sz in zip(names[:-1], shape[1:-1])})
        return v

    A_CONST = [(0, 1)]
    A_MOD = [(1, 65)]
    A_T0B = [(1, 56)]
    A_T0A = [(61, 111)]
    A_WIN = [(138, 170)]
    A_AW = [(56, 61), (111, 138), (170, 207)]
    A_FEAT = [(170, 207)]
    A_LW = [(56, 61), (111, 170)]
    A_WOUT = [(111, 127)]
    A_TW = [(127, 138)]
    A_W1 = [(1, 65)]
    A_W2 = [(65, 129)]
    A_CAW = {0: [(1, 65)], 1: [(97, 111), (127, 170)]}
    A_MW = [(129, 207)]
    A_DBG = [(1, 56)]

    class Wt:
        pass
    WTS = Wt()
    WTS.w_in = None
    WTS.w_out = None
    WTS.w1 = None
    WTS.w2 = None

    def load_w_in(l):
        arena_cur[0] = Arena(A_WIN)
        T_ = Tracker()
        w = sb("w_in", [128, 8, 2048], BF16)
        B_ = T_.new("w_in")
        wl = W["w_in"][l]
        for k in range(8):
            for hh in range(2):
                S.dma("pool", w[:, k, hh * 1024:(hh + 1) * 1024], wl[k * 128:(k + 1) * 128, hh * 1024:(hh + 1) * 1024], w=[B_])
        WTS.w_in = (w, B_)

    def load_w_out(l):
        arena_cur[0] = Arena(A_WOUT)
        T_ = Tracker()
        w = sb("w_out", [128, 8, D], BF16)
        B_ = T_.new("w_out")
        for k in range(8):
            S.dma("pool", w[:, k, :], W["w_out"][l, k * 128:(k + 1) * 128, :], w=[B_])
        WTS.w_out = (w, B_)

    def load_w1(l, part):
        if part == 0:
            arena_cur[0] = Arena(A_W1)
            w = sb("w1", [128, 8, DFF], BF16)
            WTS.w1 = (w, [None] * 8)
        w, Bs = WTS.w1
        T_ = Tracker()
        for k in (range(7) if part == 0 else [7]):
            Bs[k] = T_.new("w1_%d" % k)
            for cq in range(4):
                S.dma("pool", w[:, k, cq * 1024:(cq + 1) * 1024],
                      W["w_mlp_in"][l, k * 128:(k + 1) * 128, cq * 1024:(cq + 1) * 1024], w=[Bs[k]])

    def load_w2(l, half):
        if half == 0:
            arena_cur[0] = Arena(A_W2)
            w = sb("w2", [128, 32, D], BF16)
            WTS.w2 = (w, [None] * 8)
        w, Bs = WTS.w2
        T_ = Tracker()
        for k in range(4 * half, 4 * half + 4):
            Bs[k] = T_.new("w2_%d" % k)
            for f4 in range(4):
                f = 4 * k + f4
                S.dma("pool", w[:, f, :], W["w_mlp_out"][l, f * 128:(f + 1) * 128, :], w=[Bs[k]])

    def ps(name, shape, dt=F32):
        uid[0] += 1
        return es_cur[0].enter_context(nc.psum_tensor("%s_p%d" % (name, uid[0]), list(shape), dt))

    es_cur = [es]

    B_xmid = [[Buf("xmid%d_%d" % (b, j)) for j in range(NT)] for b in range(NB)]
    B_xs = [[Buf("xs%d_%d" % (b, j)) for j in range(NT)] for b in range(NB)]
    B_h2s = [[Buf("h2s%d_%d" % (bb, j)) for j in range(NT)] for bb in range(NB)]
    B_mod = [Buf("mod%d" % l) for l in range(DEPTH)]
    out_toks = []

    arena_cur[0] = Arena(A_CONST)
    ident = sb("ident", [128, 128], BF16)
    mlo = sb("mlo", [128, 128], BF16)
    mhi = sb("mhi", [128, 128], BF16)
    nhalf = sb("nhalf", [128, 8], F32)
    B_const = Buf("const")
    S.dma("pool", ident[:], k_ident[:, :], w=[B_const])
    S.dma("pool", mlo[:], k_mlo[:, :], w=[B_const])
    S.dma("pool", mhi[:], k_mhi[:, :], w=[B_const])
    S.op("dve", lambda e: e.memset(nhalf[:], -0.5), w=[B_const])
    epsc = sb("epsc", [128, 8], F32)
    S.op("dve", lambda e: e.memset(epsc[:], float(EPS)), w=[B_const])

    def rstd_from_ss(ss_ap, ss_buf, n, inv_n, out_ap, out_buf, tmp_ap, tmp_buf):
        S.op("dve", lambda e: e.tensor_scalar(out=tmp_ap, in0=ss_ap, scalar1=float(inv_n), scalar2=float(EPS),
                                               op0=ALU.mult, op1=ALU.add), r=[ss_buf], w=[tmp_buf])
        S.op("pool", lambda e: e.tensor_tensor(out=out_ap, in0=tmp_ap, in1=nhalf[:, 0:n], op=ALU.pow),
             r=[tmp_buf, B_const], w=[out_buf])

    def modulation(l, preload=None):
        with ExitStack() as les:
            es_cur[0] = les
            TMod = Tracker()
            if preload is not None:
                preload()
            arena_cur[0] = Arena(A_MOD)
            cT = sb("cT", [128, 8, 4], F32)
            cTb = sb("cTb", [128, 8, 4], BF16)
            bmod = sb("bmod", [3, 6 * D], F32)
            g4 = sb("g4", [3, 4, D], F32)
            wm = [sb("wm%d" % i, [128, 8, 512], BF16) for i in range(2)]
            rows = [sb("mrow%d" % i, [3, 512], F32) for i in range(2)]
            pm = [ps("pm%d" % i, [128, 512]) for i in range(2)]
            B_cT, B_cTb, B_bmod, B_g4 = (TMod.new(n_) for n_ in ("cT", "cTb", "bmod", "g4"))
            B_wm = [TMod.new("wm0"), TMod.new("wm1")]
            B_rows = [TMod.new("r0"), TMod.new("r1")]
            B_pm = [TMod.new("pm0"), TMod.new("pm1")]
            S.op("dve", lambda e: e.memset(cT[:], 0.0), w=[B_cT])
            for b in range(NB):
                S.dma("sp", cT[:, :, b], c_in[b, :].rearrange("(k p) -> p k", p=128), w=[B_cT],
                      allow_slow_non_contiguous=True)
            S.dma("sp", cT[:, :, 2], cctx_in.rearrange("(k p) -> p k", p=128), w=[B_cT],
                  allow_slow_non_contiguous=True)
            S.op("act", lambda e: e.activation(out=cTb[:], in_=cT[:], func=AF.Silu), r=[B_cT], w=[B_cTb])
            S.dma("sp", bmod[:], W["b_mod"][l, :].partition_broadcast(3), w=[B_bmod])
            for i, gname in enumerate(("g_pre_mix", "g_post_mix", "g_pre_mlp", "g_post_mlp")):
                S.dma("sp", g4[:, i, :], W[gname][l, :].partition_broadcast(3), w=[B_g4])
            for j in range(12):
                i = j % 2
                S.dma("pool", wm[i][:], W["w_mod"][l, :, j * 512:(j + 1) * 512].rearrange("(k p) n -> p k n", p=128),
                      w=[B_wm[i]])
                for k in range(8):
                    S.op("pe", lambda e, k=k: e.matmul(pm[i][0:3, :], lhsT=cTb[:, k, 0:3], rhs=wm[i][:, k, :],
                                                       start=(k == 0), stop=(k == 7)),
                         r=[B_cTb, B_wm[i]], w=[B_pm[i]], signal=(k == 7))
                seg = j // 2
                cs = slice((j % 2) * 512, (j % 2) * 512 + 512)
                S.op("dve", lambda e: e.tensor_tensor(out=rows[i][:], in0=pm[i][0:3, :], in1=bmod[:, j * 512:(j + 1) * 512],
                                                      op=ALU.add), r=[B_pm[i], B_bmod], w=[B_rows[i]])
                if seg in (1, 4):
                    gi = 0 if seg == 1 else 2
                    S.op("dve", lambda e: e.scalar_tensor_tensor(out=rows[i][:], in0=rows[i][:], scalar=1.0,
                                                                 in1=g4[:, gi, cs], op0=ALU.add, op1=ALU.mult),
                         r=[B_rows[i], B_g4], w=[B_rows[i]])
                elif seg in (2, 5):
                    gi = 1 if seg == 2 else 3
                    S.op("dve", lambda e: e.tensor_tensor(out=rows[i][:], in0=rows[i][:], in1=g4[:, gi, cs], op=ALU.mult),
                         r=[B_rows[i], B_g4], w=[B_rows[i]])
                S.dma("sp", modrows[l, :, seg, cs], rows[i][:], r=[B_rows[i]], w=[B_mod[l]])
            es_cur[0] = es
        if stop == "mod":
            raise _Stop()
        return [B_cT, B_cTb, B_bmod, B_g4] + B_wm + B_rows + B_pm

    def src_tile(l, b, j):
        if l == 0:
            if j < 2:
                return ctx_in[b, j * 128:(j + 1) * 128, :], None
            return x_in[b, (j - 2) * 128:(j - 1) * 128, :], None
        return xs[b, j * 128:(j + 1) * 128, :], B_xs[b][j]

    prev_bufs = [[]]

    def phase_scope():
        return ExitStack()

    class Tracker:
        def __init__(self):
            self.bufs = []
            fr = [(e, S.cnt[e]) for e in S.cnt if S.cnt[e] > 0]
            for q, st in S.dq.items():
                for i, key in enumerate(st["sems"]):
                    if st["cnt"][i] > 0:
                        fr.append((key, st["cnt"][i]))
            self.frontier = fr

        def new(self, name):
            b = Buf(name)
            b.r = list(self.frontier)
            self.bufs.append(b)
            return b

    def layer_batch(l, b, last_layer):
        do_ctx = not last_layer
        with ExitStack() as pes:
            es_cur[0] = pes
            T0 = Tracker()
            arena_cur[0] = Arena(A_T0A)
            QA = sb("QA", [128, NT, 2, 128], BF16)
            KzA = sb("KzA", [128, 2, NT, 128], BF16)
            QB = sb("QB", [128, NT, 3, 128], BF16)
            VA = sb("VA", [128, NT, 2, 128], BF16)
            VB = sb("VB", [128, NT, 2, 128], BF16)
            arena_cur[0] = Arena(A_T0B)
            xr = sb("xr", [128, 4, XW], F32)
            gg = sb("gg", [128, 4, NTOK], BF16)
            B_QA = [T0.new("QA%d" % j) for j in range(NT)]
            B_QB = [T0.new("QB%d" % j) for j in range(NT)]
            B_VA = [T0.new("VA%d" % j) for j in range(NT)]
            B_VB = [T0.new("VB%d" % j) for j in range(NT)]
            B_xr = [T0.new("xr%d" % m) for m in range(4)]
            B_gg = [T0.new("gg%d" % m) for m in range(4)]
            S.op("pool", lambda e: e.memset(KzA[:], 0.0), w=B_QA)
            for j in range(NT):
                S.op("pool", lambda e, j=j: e.memset(VA[:, j, :, 64:128], 1.0), w=[B_VA[j]])
                S.op("pool", lambda e, j=j: e.memset(VB[:, j, :, 64:128], 1.0), w=[B_VB[j]])
            for m in range(4):
                S.op("pool", lambda e, m=m: e.memset(xr[:, m, :], 0.0), w=[B_xr[m]])

            with ExitStack() as aes:
                es_cur[0] = aes
                TA = Tracker()
                w_in, B_win = WTS.w_in
                arena_cur[0] = Arena(A_AW)
                G1 = sb("G1", [128, D], F32)
                S1 = sb("S1", [128, D], F32)
                B_G1 = TA.new("G1")
                gqk = sb("gqk", [128, 6, 64], F32)
                B_gqk = TA.new("gqk")
                S.dma("sp", gqk[:, 0, :], W["g_q_a"][l, :].partition_broadcast(128), w=[B_gqk])
                S.dma("sp", gqk[:, 4, :], W["g_k_a"][l, :].partition_broadcast(128), w=[B_gqk])
                for hh in (1, 2, 3):
                    S.op("dve", lambda e, hh=hh: e.tensor_copy(out=gqk[:, hh, :], in_=gqk[:, 0, :]), r=[B_gqk], w=[B_gqk])
                S.op("dve", lambda e: e.tensor_copy(out=gqk[:, 5, :], in_=gqk[:, 4, :]), r=[B_gqk], w=[B_gqk])
                xt = [sb("xt%d" % i, [128, D], F32) for i in range(2)]
                B_xt = [TA.new("xt%d" % i) for i in range(2)]
                junk = sb("junk", [128, D], BF16)
                B_junk = TA.new("junk")
                st = [sb("st%d" % i, [128, 8], F32) for i in range(2)]
                B_st = [TA.new("st%d" % i) for i in range(2)]
                tmp = sb("tmpA", [128, D], F32)
                B_tmp = TA.new("tmpA")
                hb = [sb("hb%d" % i, [128, D], BF16) for i in range(2)]
                B_hb = [TA.new("hb%d" % i) for i in range(2)]
                hT = [sb("hT%d" % i, [128, 8, 512], BF16) for i in range(2)]
                B_hT = [TA.new("hT%d" % i) for i in range(2)]
                sq = sb("sq", [128, 384], F32)
                B_sq = TA.new("sq")
                st6 = sb("st6", [128, 24], F32)
                B_st6 = TA.new("st6")
                qk12_2 = [sb("qk12_%d" % i, [128, 768], F32) for i in range(2)]
                B_qk12_2 = [TA.new("qk12_%d" % i) for i in range(2)]
                rt1 = sb("rt1", [128, 768], F32)
                rt2 = sb("rt2", [128, 768], F32)
                B_rt1, B_rt2 = TA.new("rt1"), TA.new("rt2")
                stg = [sb("stg%d" % i, [128, 768], BF16) for i in range(2)]
                B_stg = [TA.new("stg%d" % i) for i in range(2)]
                cs_t = [sb("cs%d" % i, [128, 2, 64], F32) for i in range(4)]
                B_cs = [TA.new("cs%d" % i) for i in range(4)]
                tp = ps("tpA", [128, 8, 128], BF16)
                B_tp = TA.new("tpA")
                zt2 = [ps("zt%d" % i, [128, 1024]) for i in range(2)]
                B_zt2 = [TA.new("zt%d" % i) for i in range(2)]
                zf = [ps("zf%d" % i, [128, 512]) for i in range(2)]
                B_zf = [TA.new("zf%d" % i) for i in range(2)]
                tp2 = ps("tp2", [128, 6, 128], BF16)
                B_tp2 = TA.new("tp2")

                ck("A:setup")
                groups = [(0, 2)] + [(2 + 4 * g, 4) for g in range(4)]
                tiles_a = [(gi, jj) for gi, (j0, nj) in enumerate(groups) for jj in range(nj)]

                def a_s1(gi, jj, part):
                    j0, nj = groups[gi]
                    is_ctx = (gi == 0)
                    j = j0 + jj
                    i2 = j % 2
                    if part == "act":
                        if jj == 0 and gi in (0, 1):
                            who = 2 if is_ctx else b
                            S.dma("sp", G1[:], modrows[l, who, 1, :].partition_broadcast(128), r=[B_mod[l]], w=[B_G1])
                            S.dma("sp", S1[:], modrows[l, who, 0, :].partition_broadcast(128), r=[B_mod[l]], w=[B_G1])
                        src, sbuf_tok = src_tile(l, b, j)
                        S.dma("sp", xt[i2][:], src, r=([sbuf_tok] if sbuf_tok else []), w=[B_xt[i2]])
                        if not is_ctx:
                            S.dma("sp", cs_t[j % 4][:, 0, :], k_cos[(j - 2) * 128:(j - 1) * 128, :], w=[B_cs[j % 4]])
                            S.dma("sp", cs_t[j % 4][:, 1, :], k_sin[(j - 2) * 128:(j - 1) * 128, :], w=[B_cs[j % 4]])
                        S.op("act", lambda e: e.activation(out=junk[:], in_=xt[i2][:], func=AF.Square,
                                                           accum_out=st[i2][:, 0:1]),
                             r=[B_xt[i2]], w=[B_junk, B_st[i2]])
                        S.op("act", lambda e: e.activation(out=st[i2][:, 1:2], in_=st[i2][:, 0:1], func=AF.Ln,
                                                           scale=1.0 / D, bias=epsc[:, 0:1]), r=[B_st[i2], B_const], w=[B_st[i2]])
                        S.op("act", lambda e: e.activation(out=st[i2][:, 2:3], in_=st[i2][:, 1:2], func=AF.Exp, scale=-0.5),
                             r=[B_st[i2]], w=[B_st[i2]])
                    if part == "dve":
                        S.op("dve", lambda e: e.scalar_tensor_tensor(out=tmp[:], in0=xt[i2][:], scalar=st[i2][:, 2:3],
                                                                     in1=G1[:], op0=ALU.mult, op1=ALU.mult),
                             r=[B_xt[i2], B_st[i2], B_G1], w=[B_tmp])
                        S.op("dve", lambda e: e.tensor_tensor(out=hb[i2][:], in0=tmp[:], in1=S1[:], op=ALU.add),
                             r=[B_tmp, B_G1], w=[B_hb[i2]])

                def a_s2(gi, jj, part):
                    j0, nj = groups[gi]
                    hTg = hT[gi % 2]
                    B_hTg = B_hT[gi % 2]
                    j = j0 + jj
                    i2 = j % 2
                    zt = zt2[i2]
                    B_zt = B_zt2[i2]
                    if part == "T":
                        for k in range(8):
                            S.op("pe", lambda e, k=k: e.transpose(out=tp[:, k, :], in_=hb[i2][:, k * 128:(k + 1) * 128],
                                                                  identity=ident[:]),
                                 r=[B_hb[i2], B_const], w=[B_tp], signal=(k == 7))
                    if part == "copy":
                        S.op("act", lambda e: e.activation(out=hTg[:, :, jj * 128:(jj + 1) * 128], in_=tp[:], func=AF.Copy),
                             r=[B_tp], w=[B_hTg])
                    if part == "mm":
                        for k in range(8):
                            for n in range(2):
                                S.op("pe", lambda e, k=k, n=n: e.matmul(zt[:, n * 512:(n + 1) * 512],
                                                                        lhsT=hTg[:, k, jj * 128:(jj + 1) * 128],
                                                                        rhs=w_in[:, k, n * 512:(n + 1) * 512],
                                                                        start=(k == 0), stop=(k == 7)),
                                     r=[B_hTg, B_win], w=[B_zt], signal=(k == 7 and n == 1))
                        if jj == nj - 1:
                            a_zf(gi)

                def a_s3(gi, jj, part):
                    j0, nj = groups[gi]
                    j = j0 + jj
                    i2 = j % 2
                    zt = zt2[i2]
                    B_zt = B_zt2[i2]
                    qk12 = qk12_2[i2]
                    B_qk12 = B_qk12_2[i2]
                    if part == "a":
                        S.op("act", lambda e: e.activation(out=sq[:], in_=zt[:, 0:384], func=AF.Square),
                             r=[B_zt], w=[B_sq])
                        S.op("act", lambda e: e.activation(out=qk12[:, 384:640].rearrange("p (b a d) -> p a b d", b=2, a=2, d=64),
                                                           in_=zt[:, 512:768].rearrange("p (a b d) -> p a b d", a=2, b=2, d=64),
                                                           func=AF.Copy), r=[B_zt], w=[B_qk12])
                        S.op("act", lambda e: e.activation(out=qk12[:, 640:768], in_=zt[:, 768:896], func=AF.Copy),
                             r=[B_zt], w=[B_qk12])
                        S.op("act", lambda e: e.activation(out=VA[:, j, :, 0:64],
                                                           in_=zt[:, 384:512].rearrange("p (a d) -> p a d", d=64), func=AF.Copy),
                             r=[B_zt], w=[B_VA[j]])
                        S.op("act", lambda e: e.activation(out=VB[:, j, :, 0:64],
                                                           in_=zt[:, 896:1024].rearrange("p (a d) -> p a d", d=64), func=AF.Copy),
                             r=[B_zt], w=[B_VB[j]])
                    if part == "red":
                        S.op("dve", lambda e: e.tensor_reduce(out=st6[:, 0:6], in_=sq[:].rearrange("p (h d) -> p h d", d=64),
                                                              axis=AX.X, op=ALU.add), r=[B_sq], w=[B_st6])
                    if part == "b":
                        S.op("act", lambda e: e.activation(out=st6[:, 8:14], in_=st6[:, 0:6], func=AF.Ln,
                                                           scale=1.0 / 64, bias=epsc[:, 0:1]), r=[B_st6, B_const], w=[B_st6])
                        S.op("act", lambda e: e.activation(out=st6[:, 16:22], in_=st6[:, 8:14], func=AF.Exp, scale=-0.5),
                             r=[B_st6], w=[B_st6])
                    if part == "qn":
                        S.op("dve", lambda e: e.tensor_tensor(out=qk12[:, 0:256].rearrange("p (b a d) -> p a b d", b=2, a=2, d=64),
                                                              in0=zt[:, 0:256].rearrange("p (a b d) -> p a b d", a=2, b=2, d=64),
                                                              in1=st6[:, 16:20].rearrange("p (a b) -> p a b", a=2).unsqueeze(3)
                                                              .to_broadcast([128, 2, 2, 64]),
                                                              op=ALU.mult), r=[B_zt, B_st6], w=[B_qk12])
                        S.op("dve", lambda e: e.tensor_tensor(out=qk12[:, 256:384].rearrange("p (h d) -> p h d", d=64),
                                                              in0=zt[:, 256:384].rearrange("p (h d) -> p h d", d=64),
                                                              in1=st6[:, 20:22].unsqueeze(2).to_broadcast([128, 2, 64]),
                                                              op=ALU.mult), r=[B_zt, B_st6], w=[B_qk12])

                def a_s4(gi, jj, part):
                    j0, nj = groups[gi]
                    is_ctx = (gi == 0)
                    j = j0 + jj
                    i2 = j % 2
                    qk12 = qk12_2[i2]
                    B_qk12 = B_qk12_2[i2]
                    sg = stg[i2]
                    B_sg = B_stg[i2]
                    if part == "ew" and is_ctx:
                        S.op("pool", lambda e: e.tensor_tensor(out=sg[:, 0:384].rearrange("p (h d) -> p h d", d=64),
                                                               in0=qk12[:, 0:384].rearrange("p (h d) -> p h d", d=64),
                                                               in1=gqk[:], op=ALU.mult),
                             r=[B_qk12, B_gqk], w=[B_sg])
                        S.op("pool", lambda e: e.tensor_copy(out=sg[:, 384:768], in_=qk12[:, 384:768]),
                             r=[B_qk12], w=[B_sg])
                    if part == "ew" and not is_ctx:
                        c4 = cs_t[j % 4]
                        B_c4 = B_cs[j % 4]
                        S.op("pool", lambda e: e.tensor_tensor(out=qk12[:, 0:384].rearrange("p (h d) -> p h d", d=64),
                                                               in0=qk12[:, 0:384].rearrange("p (h d) -> p h d", d=64),
                                                               in1=gqk[:], op=ALU.mult),
                             r=[B_qk12, B_gqk], w=[B_qk12])
                        cosb = c4[:, 0, :].unsqueeze(1).to_broadcast([128, 12, 64])
                        q3 = qk12[:].rearrange("p (h d) -> p h d", d=64)
                        S.op("dve", lambda e: e.tensor_tensor(out=rt1[:].rearrange("p (h d) -> p h d", d=64), in0=q3,
                                                              in1=cosb, op=ALU.mult),
                             r=[B_qk12, B_c4], w=[B_rt1])
                        q5 = qk12[:].rearrange("p (h s t) -> p h s t", s=4, t=16)
                        r5 = rt2[:].rearrange("p (h s t) -> p h s t", s=4, t=16)
                        s5 = c4[:, 1, :].rearrange("p (s t) -> p s t", t=16)
                        for (so, si) in ((0, 1), (1, 0), (2, 3), (3, 2)):
                            S.op("pool", lambda e, so=so, si=si: e.tensor_tensor(
                                out=r5[:, :, so, :], in0=q5[:, :, si, :],
                                in1=s5[:, so, :].unsqueeze(1).to_broadcast([128, 12, 16]), op=ALU.mult),
                                r=[B_qk12, B_c4], w=[B_rt2])
                        S.op("dve", lambda e: e.tensor_tensor(out=sg[:], in0=rt1[:], in1=rt2[:], op=ALU.add),
                             r=[B_rt1, B_rt2], w=[B_sg])
                    if part == "T":
                        for t6 in range(6):
                            S.op("pe", lambda e, t6=t6: e.transpose(out=tp2[:, t6, :], in_=sg[:, t6 * 128:(t6 + 1) * 128],
                                                                    identity=ident[:]),
                                 r=[B_sg, B_const], w=[B_tp2], signal=(t6 == 5))
                    if part == "copy":
                        S.op("dve", lambda e: e.tensor_copy(out=QA[:, j, :, :], in_=tp2[:, 0:2, :]), r=[B_tp2], w=[B_QA[j]])
                        S.op("dve", lambda e: e.tensor_copy(out=KzA[0:64, 0, j, :], in_=tp2[0:64, 2, :]), r=[B_tp2], w=[B_QA[j]])
                        S.op("dve", lambda e: e.tensor_copy(out=KzA[64:128, 1, j, :], in_=tp2[64:128, 2, :]), r=[B_tp2], w=[B_QA[j]])
                        S.op("dve", lambda e: e.tensor_copy(out=QB[:, j, :, :], in_=tp2[:, 3:6, :]), r=[B_tp2], w=[B_QB[j]])

                def a_zf(gi):
                    j0, nj = groups[gi]
                    is_ctx = (gi == 0)
                    hTg = hT[gi % 2]
                    B_hTg = B_hT[gi % 2]
                    N = nj * 128
                    tok0 = j0 * 128
                    for m in range(8):
                        zz = zf[m % 2]
                        B_zz = B_zf[m % 2]
                        for k in range(8):
                            S.op("pe", lambda e, k=k, m=m: e.matmul(zz[:, 0:N], lhsT=w_in[:, k, 1024 + m * 128:1024 + (m + 1) * 128],
                                                                    rhs=hTg[:, k, 0:N], start=(k == 0), stop=(k == 7)),
                                 r=[B_hTg, B_win], w=[B_zz], signal=(k == 7))
                        if m < 4:
                            off = (XO_C if is_ctx else XO_L - CTX) + tok0
                            S.op("dve", lambda e: e.tensor_copy(out=xr[:, m, off:off + N], in_=zz[:, 0:N]),
                                 r=[B_zz], w=[B_xr[m]])
                        else:
                            S.op("act", lambda e: e.activation(out=gg[:, m - 4, tok0:tok0 + N], in_=zz[:, 0:N],
                                                               func=AF.Gelu_apprx_tanh),
                                 r=[B_zz], w=[B_gg[m - 4]])

                nta = len(tiles_a)
                for ti in range(nta + 3):
                    t1 = tiles_a[ti] if ti < nta else None
                    t2 = tiles_a[ti - 1] if 1 <= ti < nta + 1 else None
                    t3 = tiles_a[ti - 2] if 2 <= ti < nta + 2 else None
                    t4 = tiles_a[ti - 3] if 3 <= ti else None
                    if t2:
                        a_s2(*t2, "T")
                    if t1:
                        a_s1(*t1, "act")
                    if t2:
                        a_s2(*t2, "copy")
                    if t3:
                        a_s3(*t3, "a")
                    if t4:
                        a_s4(*t4, "ew")
                    if t2:
                        a_s2(*t2, "mm")
                    if t1:
                        a_s1(*t1, "dve")
                    if t3:
                        a_s3(*t3, "red")
                        a_s3(*t3, "b")
                        a_s3(*t3, "qn")
                    if t4:
                        a_s4(*t4, "T")
                        a_s4(*t4, "copy")
                es_cur[0] = pes
                prev_bufs[0] = TA.bufs
            if stop == "A":
                raise _Stop()

            if dbg and "QA" in dbg_out and b == 0 and l == 0:
                pass

            fes = ExitStack()
            es_cur[0] = fes
            TF = Tracker()
            arena_cur[0] = Arena(A_FEAT)
            featT = sb("featT", [128, 8, NTOK], BF16)
            B_feat = [[TF.new("feat%d_%d" % (k, j)) for j in range(NT)] for k in range(8)]
            es_cur[0] = pes

            with ExitStack() as les:
                es_cur[0] = les
                TL = Tracker()
                arena_cur[0] = Arena(A_LW)
                cw = sb("cw", [128, 4, 4], F32)
                cb = sb("cb", [128, 4], F32)
                lb = sb("lb", [128, 2, 2, 4], F32)
                lam = sb("lam", [128, 2, 4], F32)
                cneg = sb("cneg", [128, 2, 2, 4], F32)
                lt = sb("lt", [128, 8], F32)
                wbd = sb("wbd", [128, 2, 2, 4, 128], BF16)
                B_lw = TL.new("lruw")
                S.op("dve", lambda e: e.memset(wbd[:], 0.0), w=[B_lw])
                for jt in range(4):
                    S.dma("sp", cw[:, :, jt], W["conv_w"][l, jt].rearrange("(m p) -> p m", p=128), w=[B_lw],
                          allow_slow_non_contiguous=True)
                S.dma("sp", cb[:], W["conv_b"][l].rearrange("(m p) -> p m", p=128), w=[B_lw], allow_slow_non_contiguous=True)
                for gi_, nm in enumerate(("lru_b_a", "lru_b_i")):
                    for d in range(2):
                        S.dma("sp", lb[:, gi_, d, :], W[nm][l, d].rearrange("(m p) -> p m", p=128), w=[B_lw],
                              allow_slow_non_contiguous=True)
                for d in range(2):
                    S.dma("sp", lam[:, d, :], W["lru_lambda"][l, d].rearrange("(m p) -> p m", p=128), w=[B_lw],
                          allow_slow_non_contiguous=True)
                for gi_, nm in enumerate(("lru_w_a", "lru_w_i")):
                    for d in range(2):
                        for half in range(2):
                            S.dma("pool", wbd[half * 64:(half + 1) * 64, gi_, d, :, half * 64:(half + 1) * 64],
                                  W[nm][l, d].rearrange("(m two) i j -> two i m j", two=2)[half], w=[B_lw])
                lam2 = lam[:].rearrange("p d m -> p (d m)")
                S.op("act", lambda e: e.activation(out=lt[:], in_=lam2, func=AF.Exp, scale=-1.0), r=[B_lw], w=[B_lw])
                S.op("act", lambda e: e.activation(out=lt[:], in_=lt[:], func=AF.Ln, bias=1.0, scale=1.0), r=[B_lw], w=[B_lw])
                S.op("dve", lambda e: e.tensor_scalar(out=cneg[:, 0, :, :].rearrange("p d m -> p (d m)"), in0=lt[:], scalar1=-8.0,
                                                      scalar2=None, op0=ALU.mult), r=[B_lw], w=[B_lw])
                S.op("dve", lambda e: e.tensor_scalar(out=cneg[:, 1, :, :].rearrange("p d m -> p (d m)"), in0=lt[:], scalar1=-16.0,
                                                      scalar2=None, op0=ALU.mult), r=[B_lw], w=[B_lw])
                nlb = sb("nlb", [128, 2, 2, 4], F32)
                S.op("dve", lambda e: e.tensor_scalar(out=nlb[:].rearrange("p a d m -> p (a d m)"),
                                                      in0=lb[:].rearrange("p a d m -> p (a d m)"),
                                                      scalar1=-1.0, scalar2=None, op0=ALU.mult), r=[B_lw], w=[B_lw])
                xc = sb("xc", [128, NTOK], F32)
                xcb = sb("xcb", [128, NTOK], BF16)
                hf = sb("hf", [128, NTOK], F32)
                B_xc, B_xcb = TL.new("xc"), TL.new("xcb")
                tgroups = [(0, 256)] + [(256 + 512 * g, 512) for g in range(4)]
                B_hfg = [TL.new("hf%d" % g) for g in range(5)]
                NS = 3
                gnames = ("er", "ei", "aa", "mm", "uu", "hb")
                G = {nm: [sb("%s%d" % (nm, i), [128, 512], F32) for i in range(NS)] for nm in gnames}
                BG = {nm: [TL.new("%s%d" % (nm, i)) for i in range(NS)] for nm in gnames}
                pg = [ps("pg%d" % i, [128, 512]) for i in range(2 * NS)]
                B_pg = [TL.new("pg%d" % i) for i in range(2 * NS)]
                cnt = [0]
                for m in range(4):
                    for (dst0, n, off) in ((0, CTX, XO_C), (CTX, SEQ, XO_L)):
                        S.op("dve", lambda e: e.tensor_scalar(out=xc[:, dst0:dst0 + n], in0=xr[:, m, off - 2:off - 2 + n],
                                                              scalar1=cw[:, m, 0:1], scalar2=cb[:, m:m + 1],
                                                              op0=ALU.mult, op1=ALU.add),
                             r=[B_xr[m], B_lw], w=[B_xc])
                        for jt in (1, 2, 3):
                            S.op("dve", lambda e, jt=jt: e.scalar_tensor_tensor(out=xc[:, dst0:dst0 + n],
                                                                               in0=xr[:, m, off - 2 + jt:off - 2 + jt + n],
                                                                               scalar=cw[:, m, jt:jt + 1], in1=xc[:, dst0:dst0 + n],
                                                                               op0=ALU.mult, op1=ALU.add),
                                 r=[B_xr[m], B_lw, B_xc], w=[B_xc])
                    S.op("dve", lambda e: e.tensor_copy(out=xcb[:], in_=xc[:]), r=[B_xc], w=[B_xcb])
                    for d in range(2):
                        order = list(range(5)) if d == 0 else [0, 4, 3, 2, 1]
                        steps = []
                        prev_i2 = None
                        for g in order:
                            i2 = cnt[0] % NS
                            cnt[0] += 1
                            steps.append((g, i2, prev_i2))
                            prev_i2 = i2

                        def stage1(g, i2, prv):
                            t0, n = tgroups[g]
                            pa, pi = pg[2 * i2], pg[2 * i2 + 1]
                            er, ei = G["er"][i2], G["ei"][i2]
                            S.op("pe", lambda e: e.matmul(pa[:, 0:n], lhsT=wbd[:, 0, d, m, :], rhs=xcb[:, t0:t0 + n],
                                                          start=True, stop=True), r=[B_lw, B_xcb], w=[B_pg[2 * i2]])
                            S.op("pe", lambda e: e.matmul(pi[:, 0:n], lhsT=wbd[:, 1, d, m, :], rhs=xcb[:, t0:t0 + n],
                                                          start=True, stop=True), r=[B_lw, B_xcb], w=[B_pg[2 * i2 + 1]])
                            S.op("act", lambda e: e.activation(out=er[:, 0:n], in_=pa[:, 0:n], func=AF.Exp, scale=-1.0,
                                                               bias=nlb[:, 0, d, m:m + 1]), r=[B_pg[2 * i2], B_lw], w=[BG["er"][i2]])
                            S.op("act", lambda e: e.activation(out=ei[:, 0:n], in_=pi[:, 0:n], func=AF.Exp, scale=-1.0,
                                                               bias=nlb[:, 1, d, m:m + 1]), r=[B_pg[2 * i2 + 1], B_lw], w=[BG["ei"][i2]])
                            S.op("act", lambda e: e.activation(out=er[:, 0:n], in_=er[:, 0:n], func=AF.Ln, scale=1.0, bias=1.0),
                                 r=[BG["er"][i2]], w=[BG["er"][i2]])
                            S.op("act", lambda e: e.activation(out=er[:, 0:n], in_=er[:, 0:n], func=AF.Exp, scale=-1.0),
                                 r=[BG["er"][i2]], w=[BG["er"][i2]])
                            S.op("dve", lambda e: e.tensor_scalar(out=ei[:, 0:n], in0=ei[:, 0:n], scalar1=1.0, scalar2=None,
                                                                  op0=ALU.add), r=[BG["ei"][i2]], w=[BG["ei"][i2]])
                            S.op("dve", lambda e: e.reciprocal(out=ei[:, 0:n], in_=ei[:, 0:n]), r=[BG["ei"][i2]], w=[BG["ei"][i2]])

                        def stage2(g, i2, prv):
                            t0, n = tgroups[g]
                            er, ei, aa, mmt, uu, hb_ = (G[k_][i2] for k_ in gnames)
                            S.op("act", lambda e: e.activation(out=aa[:, 0:n], in_=er[:, 0:n], func=AF.Exp,
                                                               scale=cneg[:, 0, d, m:m + 1]), r=[BG["er"][i2], B_lw], w=[BG["aa"][i2]])
                            S.op("act", lambda e: e.activation(out=mmt[:, 0:n], in_=er[:, 0:n], func=AF.Exp,
                                                               scale=cneg[:, 1, d, m:m + 1]), r=[BG["er"][i2], B_lw], w=[BG["mm"][i2]])
                            S.op("act", lambda e: e.activation(out=mmt[:, 0:n], in_=mmt[:, 0:n], func=AF.Ln, scale=-1.0, bias=1.0),
                                 r=[BG["mm"][i2]], w=[BG["mm"][i2]])
                            S.op("act", lambda e: e.activation(out=mmt[:, 0:n], in_=mmt[:, 0:n], func=AF.Exp, scale=0.5),
                                 r=[BG["mm"][i2]], w=[BG["mm"][i2]])
                            S.op("pool", lambda e: e.tensor_tensor(out=mmt[:, 0:n], in0=mmt[:, 0:n], in1=ei[:, 0:n], op=ALU.mult),
                                 r=[BG["mm"][i2], BG["ei"][i2]], w=[BG["mm"][i2]])
                            S.op("pool", lambda e: e.tensor_tensor(out=uu[:, 0:n], in0=mmt[:, 0:n], in1=xc[:, t0:t0 + n], op=ALU.mult),
                                 r=[BG["mm"][i2], B_xc], w=[BG["uu"][i2]])
                            if d == 0:
                                init = 0.0 if g == 0 else hf[:, t0 - 1:t0]
                                rdeps = [BG["aa"][i2], BG["uu"][i2]] + ([B_hfg[g - 1]] if g > 0 else [])
                                S.op("dve", lambda e: e.tensor_tensor_scan(out=hf[:, t0:t0 + n], data0=aa[:, 0:n], data1=uu[:, 0:n],
                                                                           initial=init, op0=ALU.mult, op1=ALU.add),
                                     r=rdeps, w=[B_hfg[g]])
                            else:
                                init = 0.0 if prv is None else G["hb"][prv][:, 0:1]
                                rdeps = [BG["aa"][i2], BG["uu"][i2]] + ([BG["hb"][prv]] if prv is not None else [])
                                S.op("dve", lambda e: e.tensor_tensor_scan(out=hb_[:, 0:n][:, ::-1], data0=aa[:, 0:n][:, ::-1],
                                                                           data1=uu[:, 0:n][:, ::-1], initial=init,
                                                                           op0=ALU.mult, op1=ALU.add),
                                     r=rdeps, w=[BG["hb"][i2]])
                                S.op("pool", lambda e: e.tensor_tensor(out=uu[:, 0:n], in0=hb_[:, 0:n], in1=hf[:, t0:t0 + n], op=ALU.add),
                                     r=[BG["hb"][i2], B_hfg[g]], w=[BG["uu"][i2]])
                                S.op("pool", lambda e: e.tensor_tensor(out=featT[:, 4 + m, t0:t0 + n], in0=uu[:, 0:n],
                                                                       in1=gg[:, m, t0:t0 + n], op=ALU.mult),
                                     r=[BG["uu"][i2], B_gg[m]], w=[B_feat[4 + m][t0 // 128 + t] for t in range(n // 128)])

                        for si in range(len(steps) + 1):
                            if si < len(steps):
                                stage1(*steps[si])
                            if si >= 1:
                                stage2(*steps[si - 1])
                es_cur[0] = pes
                prev_bufs[0] = TL.bufs
            if stop == "lru":
                raise _Stop()

            with ExitStack() as tes:
                es_cur[0] = tes
                TT = Tracker()
                load_w_out(l)
                if b == NB - 1:
                    load_w1(l, 0)
                else:
                    load_w_in(l)
                arena_cur[0] = Arena(A_TW)
                sink = sb("sink", [128, 4], F32)
                esk = sb("esk", [128, 4], F32)
                B_sink = TT.new("sink")
                S.dma("sp", sink[:], W["sink_b"][l, :].partition_broadcast(128), w=[B_sink])
                S.op("act", lambda e: e.activation(out=esk[:], in_=sink[:], func=AF.Exp), r=[B_sink], w=[B_sink])
                NSA = 4
                pT = [sb("pT%d" % i, [128, 512], BF16) for i in range(NSA)]
                B_pT = [TT.new("pT%d" % i) for i in range(NSA)]
                rd = [sb("rd%d" % i, [64, 512], F32) for i in range(2)]
                B_rd = [TT.new("rd%d" % i) for i in range(2)]
                sps = [ps("sps%d" % i, [128, 512]) for i in range(NSA)]
                B_sps = [TT.new("sps%d" % i) for i in range(NSA)]
                ops_ = [ps("ops%d" % i, [128, 512]) for i in range(2)]
                B_ops = [TT.new("ops%d" % i) for i in range(2)]
                cs = [0]
                co = [0]
                asteps = []

                def attn_block(Q, Kt, V, B_Q, B_V, qtiles, pair_cols, rows_a, ktiles, masks, n_per, out_specs, sink_cols):
                    a = rows_a
                    rws = slice(a * 64, (a + 1) * 64)
                    io = co[0] % 2
                    co[0] += 1
                    O = ops_[io]
                    B_O = B_ops[io]
                    nq = len(qtiles) * len(pair_cols) * 128
                    kz = sink_cols
                    qrows = slice(0, 128) if kz is not None else rws
                    if len(pair_cols) == 1:
                        rhs = Q[qrows, qtiles[0]:qtiles[0] + len(qtiles), pair_cols[0], :]
                    else:
                        rhs = Q[qrows, qtiles[0], 0:2, :]
                    nk = len(ktiles)
                    for ki, kt in enumerate(ktiles):
                        isx = cs[0] % NSA
                        cs[0] += 1

                        def front(isx=isx, kt=kt, ki=ki):
                            S.op("pe", lambda e: e.matmul(sps[isx][:, 0:nq],
                                                          lhsT=(kz[:, a, kt, :] if kz is not None else Kt[rws, kt, 2, :]),
                                                          rhs=rhs, start=True, stop=True),
                                 r=[B_Q[kt]] + [B_Q[t] for t in qtiles], w=[B_sps[isx]])
                            S.op("act", lambda e: e.activation(out=pT[isx][:, 0:nq], in_=sps[isx][:, 0:nq], func=AF.Exp, scale=0.125),
                                 r=[B_sps[isx]], w=[B_pT[isx]])
                            if masks[ki] is not None:
                                mk = masks[ki]
                                S.op("pool", lambda e: e.tensor_tensor(out=pT[isx][:, 0:nq].rearrange("p (h q) -> p h q", q=128),
                                                                       in0=pT[isx][:, 0:nq].rearrange("p (h q) -> p h q", q=128),
                                                                       in1=mk[:].unsqueeze(1).to_broadcast([128, nq // 128, 128]),
                                                                       op=ALU.mult), r=[B_pT[isx], B_const], w=[B_pT[isx]])

                        def back(isx=isx, kt=kt, ki=ki):
                            S.op("pe", lambda e: e.matmul(O[:, 0:nq], lhsT=V[:, kt, a, :], rhs=pT[isx][:, 0:nq],
                                                          start=(ki == 0), stop=(ki == nk - 1)),
                                 r=[B_V[kt], B_pT[isx]], w=[B_O], signal=(ki == nk - 1))
                            if ki != nk - 1:
                                return
                            rdt = rd[io]
                            B_rdt = B_rd[io]
                            for (c0, ncol, chunk, poff, tt0, sk) in out_specs:
                                if sk is None:
                                    S.op("dve", lambda e: e.reciprocal(out=rdt[:, c0:c0 + ncol], in_=O[64:128, c0:c0 + ncol]),
                                         r=[B_O], w=[B_rdt])
                                else:
                                    S.op("dve", lambda e: e.tensor_scalar(out=rdt[:, c0:c0 + ncol], in0=O[64:128, c0:c0 + ncol],
                                                                          scalar1=esk[64:128, sk:sk + 1], scalar2=None, op0=ALU.add),
                                         r=[B_O, B_sink], w=[B_rdt])
                                    S.op("dve", lambda e: e.reciprocal(out=rdt[:, c0:c0 + ncol], in_=rdt[:, c0:c0 + ncol]),
                                         r=[B_rdt], w=[B_rdt])
                                ntile = ncol // 128
                                S.op("dve", lambda e: e.tensor_tensor(out=featT[poff:poff + 64, chunk, tt0 * 128:tt0 * 128 + ncol],
                                                                      in0=O[0:64, c0:c0 + ncol], in1=rdt[:, c0:c0 + ncol], op=ALU.mult),
                                     r=[B_O, B_rdt], w=[B_feat[chunk][tt0 + t] for t in range(ntile)])

                        asteps.append((front, back))

                for a in range(2):
                    for bq in range(2):
                        for qb in range(4):
                            qt = list(range(2 + 4 * qb, 6 + 4 * qb))
                            attn_block(QA, QA, VA, B_QA, B_VA, qt, [bq], a, list(range(NT)), [None] * NT, 512,
                                       [(0, 512, a, bq * 64, qt[0], None)], KzA)
                        if do_ctx:
                            attn_block(QA, QA, VA, B_QA, B_VA, [0, 1], [bq], a, [0, 1], [None, None], 256,
                                       [(0, 256, a, bq * 64, 0, None)], KzA)
                for a in range(2):
                    for i in range(16):
                        j = 2 + i
                        kts, mks = [0, 1], [None, None]
                        if i > 0:
                            kts.append(j - 1)
                            mks.append(mlo)
                        kts.append(j)
                        mks.append(None)
                        if i < 15:
                            kts.append(j + 1)
                            mks.append(mhi)
                        attn_block(QB, QB, VB, B_QB, B_VB, [j], [0, 1], a, kts, mks, 256,
                                   [(0, 128, 2 + a, 0, j, 2 * a), (128, 128, 2 + a, 64, j, 2 * a + 1)], None)
                    if do_ctx:
                        for j in (0, 1):
                            attn_block(QB, QB, VB, B_QB, B_VB, [j], [0, 1], a, [0, 1], [None, None], 256,
                                       [(0, 128, 2 + a, 0, j, 2 * a), (128, 128, 2 + a, 64, j, 2 * a + 1)], None)
                LOOK = NSA - 1
                for si in range(len(asteps) + LOOK):
                    if si < len(asteps):
                        asteps[si][0]()
                    if si >= LOOK:
                        asteps[si - LOOK][1]()
                es_cur[0] = pes
                prev_bufs[0] = TT.bufs
            if stop == "attn":
                raise _Stop()
            es_cur[0] = es
            prev_bufs[0] = prev_bufs[0] + T0.bufs

        if dbg and "feat" in dbg_out and l == dbg.get("_l", 0) and b == 0:
            with ExitStack() as des:
                es_cur[0] = des
                arena_cur[0] = Arena(A_DBG)
                f32t = sb("dbgf", [128, NTOK], F32)
                B_d = Buf("dbgf")
                S.barrier_bufs(prev_bufs[0], [B_d])
                for k in range(8):
                    S.op("dve", lambda e, k=k: e.tensor_copy(out=f32t[:], in_=featT[:, k, :]),
                         r=[bb for bb in B_feat[k]], w=[B_d])
                    S.dma("sp", dbg_out["feat"][k], f32t[:], r=[B_d])
                es_cur[0] = es
                prev_bufs[0] = prev_bufs[0] + [B_d]

        tiles_c = list(range(NT)) if do_ctx else list(range(2, NT))
        with ExitStack() as ces:
            es_cur[0] = ces
            TC = Tracker()
            if b == NB - 1:
                load_w1(l, 1)
                load_w2(l, 0)
            w_out, B_wout = WTS.w_out
            arena_cur[0] = Arena(A_CAW[1 if b == NB - 1 else 0])
            GT1 = sb("GT1", [128, D], F32)
            G2 = sb("G2", [128, D], F32)
            S2 = sb("S2", [128, D], F32)
            B_GC = TC.new("GC")
            xt = [sb("xtc%d" % i, [128, D], F32) for i in range(4)]
            B_xt = [TC.new("xtc%d" % i) for i in range(4)]
            xn = [sb("xn%d" % i, [128, D], F32) for i in range(3)]
            B_xn = [TC.new("xn%d" % i) for i in range(3)]
            junk = sb("junkc", [128, D], BF16)
            B_junk = TC.new("junkc")
            st = [sb("stc%d" % i, [128, 8], F32) for i in range(3)]
            B_st = [TC.new("stc%d" % i) for i in range(3)]
            tmp = sb("tmpC", [128, D], F32)
            B_tmp = TC.new("tmpC")
            hb = [sb("hbc%d" % i, [128, D], BF16) for i in range(2)]
            B_hb = [TC.new("hbc%d" % i) for i in range(2)]
            h2t = [sb("h2t%d" % i, [128, 8, 128], BF16) for i in range(2)]
            B_h2t = [TC.new("h2t%d" % i) for i in range(2)]
            yp = [ps("yp%d" % i, [128, D]) for i in range(3)]
            B_yp = [TC.new("yp%d" % i) for i in range(3)]
            tp = ps("tpC", [128, 8, 128], BF16)
            B_tp = TC.new("tpC")
            cur_who = [None]

            def c_load(j):
                ix = tiles_c.index(j) % 4
                src, stok = src_tile(l, b, j)
                S.dma("sp", xt[ix][:], src, r=([stok] if stok else []), w=[B_xt[ix]])

            def c_1(j):
                i2 = tiles_c.index(j) % 3
                y = yp[i2]
                for k in range(8):
                    for n in range(2):
                        S.op("pe", lambda e, k=k, n=n: e.matmul(y[:, n * 512:(n + 1) * 512],
                                                                lhsT=featT[:, k, j * 128:(j + 1) * 128],
                                                                rhs=w_out[:, k, n * 512:(n + 1) * 512],
                                                                start=(k == 0), stop=(k == 7)),
                             r=[B_feat[k][j], B_wout], w=[B_yp[i2]], signal=(k == 7 and n == 1))

            def c_2(j, part):
                i2 = tiles_c.index(j) % 3
                y = yp[i2]
                if part == "act":
                    S.op("act", lambda e: e.activation(out=junk[:], in_=y[:], func=AF.Square, accum_out=st[i2][:, 0:1]),
                         r=[B_yp[i2]], w=[B_junk, B_st[i2]])
                    S.op("act", lambda e: e.activation(out=st[i2][:, 1:2], in_=st[i2][:, 0:1], func=AF.Ln,
                                                       scale=1.0 / D, bias=epsc[:, 0:1]), r=[B_st[i2], B_const], w=[B_st[i2]])
                    S.op("act", lambda e: e.activation(out=st[i2][:, 2:3], in_=st[i2][:, 1:2], func=AF.Exp, scale=-0.5),
                         r=[B_st[i2]], w=[B_st[i2]])
                if part == "dve":
                    who = 2 if j < 2 else b
                    if cur_who[0] != who:
                        cur_who[0] = who
                        S.dma("sp", GT1[:], modrows[l, who, 2, :].partition_broadcast(128), r=[B_mod[l]], w=[B_GC])
                    S.op("dve", lambda e: e.scalar_tensor_tensor(out=tmp[:], in0=y[:], scalar=st[i2][:, 2:3], in1=GT1[:],
                                                                 op0=ALU.mult, op1=ALU.mult),
                         r=[B_yp[i2], B_st[i2], B_GC], w=[B_tmp])
                    ix = tiles_c.index(j) % 4
                    S.op("dve", lambda e: e.tensor_tensor(out=xn[i2][:], in0=tmp[:], in1=xt[ix][:], op=ALU.add),
                         r=[B_tmp, B_xt[ix]], w=[B_xn[i2]])
                    S.dma("sp", xmid[b, j * 128:(j + 1) * 128, :], xn[i2][:], r=[B_xn[i2]], w=[B_xmid[b][j]])

            def c_3(j, part):
                i2 = tiles_c.index(j) % 3
                ih = j % 2
                s3 = st3[i2]
                B_s3 = B_st3[i2]
                if part == "act":
                    S.op("act", lambda e: e.activation(out=junk[:], in_=xn[i2][:], func=AF.Square, accum_out=s3[:, 0:1]),
                         r=[B_xn[i2]], w=[B_junk, B_s3])
                    S.op("act", lambda e: e.activation(out=s3[:, 1:2], in_=s3[:, 0:1], func=AF.Ln,
                                                       scale=1.0 / D, bias=epsc[:, 0:1]), r=[B_s3, B_const], w=[B_s3])
                    S.op("act", lambda e: e.activation(out=s3[:, 2:3], in_=s3[:, 1:2], func=AF.Exp, scale=-0.5),
                         r=[B_s3], w=[B_s3])
                if part == "dve":
                    who = 2 if j < 2 else b
                    if cur_who3[0] != who:
                        cur_who3[0] = who
                        S.dma("sp", G2[:], modrows[l, who, 4, :].partition_broadcast(128), r=[B_mod[l]], w=[B_GC3])
                        S.dma("sp", S2[:], modrows[l, who, 3, :].partition_broadcast(128), r=[B_mod[l]], w=[B_GC3])
                    S.op("dve", lambda e: e.scalar_tensor_tensor(out=tmp[:], in0=xn[i2][:], scalar=s3[:, 2:3], in1=G2[:],
                                                                 op0=ALU.mult, op1=ALU.mult),
                         r=[B_xn[i2], B_s3, B_GC3], w=[B_tmp])
                    S.op("dve", lambda e: e.tensor_tensor(out=hb[ih][:], in0=tmp[:], in1=S2[:], op=ALU.add),
                         r=[B_tmp, B_GC3], w=[B_hb[ih]])

            def c_4(j, part):
                i2 = j % 2
                if part == "T":
                    for k in range(8):
                        S.op("pe", lambda e, k=k: e.transpose(out=tp[:, k, :], in_=hb[i2][:, k * 128:(k + 1) * 128],
                                                              identity=ident[:]),
                             r=[B_hb[i2], B_const], w=[B_tp], signal=(k == 7))
                if part == "copy":
                    S.op("act", lambda e: e.activation(out=h2t[i2][:], in_=tp[:], func=AF.Copy), r=[B_tp], w=[B_h2t[i2]])
                    S.dma("sp", h2s[b, j].rearrange("p (k t) -> p k t", k=8), h2t[i2][:], r=[B_h2t[i2]], w=[B_h2s[b][j]])

            cur_who3 = [None]
            B_GC3 = TC.new("GC3")
            st3 = [sb("st3c%d" % i, [128, 8], F32) for i in range(3)]
            B_st3 = [TC.new("st3c%d" % i) for i in range(3)]
            ntc = len(tiles_c)

            def tl(k_):
                return tiles_c[k_] if 0 <= k_ < ntc else None
            for ti in range(ntc + 5):
                j1, j2a, j2d, j3a, j3d, j4 = tl(ti), tl(ti - 1), tl(ti - 2), tl(ti - 3), tl(ti - 4), tl(ti - 5)
                if ti == 0:
                    c_load(tl(0))
                if tl(ti + 1) is not None:
                    c_load(tl(ti + 1))
                if j4 is not None:
                    c_4(j4, "T")
                if j1 is not None:
                    c_1(j1)
                if j2a is not None:
                    c_2(j2a, "act")
                if j3a is not None:
                    c_3(j3a, "act")
                if j4 is not None:
                    c_4(j4, "copy")
                if j2d is not None:
                    c_2(j2d, "dve")
                if j3d is not None:
                    c_3(j3d, "dve")
            es_cur[0] = es
            prev_bufs[0] = TC.bufs
        if stop == "Ca":
            raise _Stop()
        fes.close()
        prev_bufs[0] = prev_bufs[0] + TF.bufs

    def mlp_phase(l, last_layer):
        do_ctx = not last_layer
        tiles_c = list(range(NT)) if do_ctx else list(range(2, NT))
        with ExitStack() as mes:
            es_cur[0] = mes
            TM = Tracker()
            load_w2(l, 1)
            w1, B_w1 = WTS.w1
            w2, B_w2 = WTS.w2
            arena_cur[0] = Arena(A_MW)
            ck("M:w")
            GT2 = sb("GT2", [128, D], F32)
            B_GT2 = TM.new("GT2")
            h2g = [sb("h2g0", [128, 4, 8, 128], BF16)]
            B_h2g = [TM.new("h2g0")]
            aT = sb("aT", [128, 32, 512], BF16)
            B_aT = TM.new("aT")
            rl = [sb("rl%d" % i, [128, 512], F32) for i in range(3)]
            B_rl = [TM.new("rl%d" % i) for i in range(3)]
            xm = [sb("xm%d" % i, [128, D], F32) for i in range(2)]
            B_xm = [TM.new("xm%d" % i) for i in range(2)]
            xo = [sb("xo%d" % i, [128, D], F32) for i in range(2)]
            B_xo = [TM.new("xo%d" % i) for i in range(2)]
            junk = sb("junkm", [128, D], BF16)
            B_junk = TM.new("junkm")
            st = [sb("stm%d" % i, [128, 8], F32) for i in range(2)]
            B_st = [TM.new("stm%d" % i) for i in range(2)]
            tmp = sb("tmpM", [128, D], F32)
            B_tmp = TM.new("tmpM")
            pu = [ps("pu%d" % i, [128, 512]) for i in range(4)]
            B_pu = [TM.new("pu%d" % i) for i in range(4)]
            y2 = [ps("y2_%d" % i, [128, D]) for i in range(2)]
            B_y2 = [TM.new("y2_%d" % i) for i in range(2)]
            cur_who = [None]
            mgroups = ([(0, 2)] if do_ctx else []) + [(2 + 4 * g_, 4) for g_ in range(4)]
            for b in range(NB):
                for (j0, ng) in mgroups:
                    who = 2 if j0 < 2 else b
                    if cur_who[0] != who:
                        cur_who[0] = who
                        S.dma("sp", GT2[:], modrows[l, who, 5, :].partition_broadcast(128), r=[B_mod[l]], w=[B_GT2])
                    hg = h2g[0]
                    B_hg = B_h2g[0]
                    nn = ng * 128
                    for t in range(ng):
                        S.dma("sp", hg[:, t, :, :], h2s[b, j0 + t].rearrange("p (k t) -> p k t", k=8), r=[B_h2s[b][j0 + t]], w=[B_hg])
                    for f in range(32):
                        p_ = pu[f % 4]
                        for k in range(8):
                            S.op("pe", lambda e, k=k, f=f: e.matmul(p_[:, 0:nn].rearrange("p (t q) -> p t q", t=ng),
                                                                    lhsT=w1[:, k, f * 128:(f + 1) * 128], rhs=hg[:, 0:ng, k, :],
                                                                    start=(k == 0), stop=(k == 7)),
                                 r=[B_hg, B_w1[k]], w=[B_pu[f % 4]], signal=(k == 7))
                        r_ = rl[f % 3]
                        S.op("act", lambda e: e.activation(out=r_[:, 0:nn], in_=p_[:, 0:nn], func=AF.Relu),
                             r=[B_pu[f % 4]], w=[B_rl[f % 3]])
                        S.op("pool", lambda e, f=f: e.tensor_tensor(out=aT[:, f, 0:nn], in0=r_[:, 0:nn], in1=r_[:, 0:nn], op=ALU.mult),
                             r=[B_rl[f % 3]], w=[B_aT])
                    for t in range(ng):
                        j = j0 + t
                        i2 = j % 2
                        y = y2[i2]
                        S.dma("sp", xm[i2][:], xmid[b, j * 128:(j + 1) * 128, :], r=[B_xmid[b][j]], w=[B_xm[i2]])
                        for f in range(32):
                            for n in range(2):
                                S.op("pe", lambda e, f=f, n=n: e.matmul(y[:, n * 512:(n + 1) * 512], lhsT=aT[:, f, t * 128:(t + 1) * 128],
                                                                        rhs=w2[:, f, n * 512:(n + 1) * 512],
                                                                        start=(f == 0), stop=(f == 31)),
                                     r=[B_aT, B_w2[f // 4]], w=[B_y2[i2]], signal=(f == 31 and n == 1))
                        S.op("act", lambda e: e.activation(out=junk[:], in_=y[:], func=AF.Square, accum_out=st[i2][:, 0:1]),
                             r=[B_y2[i2]], w=[B_junk, B_st[i2]])
                        S.op("act", lambda e: e.activation(out=st[i2][:, 1:2], in_=st[i2][:, 0:1], func=AF.Ln,
                                                           scale=1.0 / D, bias=epsc[:, 0:1]), r=[B_st[i2], B_const], w=[B_st[i2]])
                        S.op("act", lambda e: e.activation(out=st[i2][:, 2:3], in_=st[i2][:, 1:2], func=AF.Exp, scale=-0.5),
                             r=[B_st[i2]], w=[B_st[i2]])
                        S.op("dve", lambda e: e.scalar_tensor_tensor(out=tmp[:], in0=y[:], scalar=st[i2][:, 2:3], in1=GT2[:],
                                                                     op0=ALU.mult, op1=ALU.mult),
                             r=[B_y2[i2], B_st[i2], B_GT2], w=[B_tmp])
                        S.op("dve", lambda e: e.tensor_tensor(out=xo[i2][:], in0=tmp[:], in1=xm[i2][:], op=ALU.add),
                             r=[B_tmp, B_xm[i2]], w=[B_xo[i2]])
                        if last_layer and final:
                            out_toks.append(S.dma("sp", out_d[b, (j - 2) * 128:(j - 1) * 128, :], xo[i2][:], r=[B_xo[i2]]))
                        else:
                            S.dma("sp", xs[b, j * 128:(j + 1) * 128, :], xo[i2][:], r=[B_xo[i2]], w=[B_xs[b][j]])

            es_cur[0] = es
            prev_bufs[0] = TM.bufs


    stopped = [False]
    try:
        for l in layers:
            mb = modulation(l, preload=lambda l=l: load_w_in(l))
            prev_bufs[0] = prev_bufs[0] + mb
            for b in range(NB):
                layer_batch(l, b, last_layer=(l == DEPTH - 1))
            mlp_phase(l, last_layer=(l == DEPTH - 1))
            if stop == "Cb":
                raise _Stop()
    except _Stop:
        es_cur[0] = es
        stopped[0] = True
    if not final:
        pass
    for t in out_toks:
        S._wait("sp", t)
    for e in ("pe", "act", "dve", "pool"):
        if S.cnt[e] > 0:
            S._wait("sp", (e, S.cnt[e]))
    if not stopped[0]:
        es.close()
    return nc, S


_PROG = {}


def kernel(**inputs):
    n_cores = 8
    if "p" not in _PROG:
        _PROG["p"] = build_program()
    nc, _ = _PROG["p"]
    consts = _const_inputs()
    in_maps = []
    f = lambda a: np.ascontiguousarray(np.asarray(a, dtype=np.float32))
    wts = {k: f(inputs[k]) for k in W_SHAPES}
    x = f(inputs["x"])
    ctx = f(inputs["ctx"])
    c = f(inputs["c"])
    c_ctx = f(inputs["c_ctx"])
    for i in range(n_cores):
        m = {"x": x[NB * i:NB * (i + 1)], "ctx": ctx[NB * i:NB * (i + 1)], "c": c[NB * i:NB * (i + 1)], "c_ctx": c_ctx}
        m.update(wts)
        m.update(consts)
        in_maps.append(m)
    res = run_bass_kernel_spmd(nc, in_maps, core_ids=list(range(n_cores)))
    return np.concatenate([r["out"] for r in res.results], axis=0)
```
